# Optimizing a Trainium2 kernel written in Bass

```python
import jax, jax.numpy as jnp
from jax import lax
import numpy as np

D_MODEL = 1024
BATCH = 2
SEQ = 8192
DEPTH = 1

N_HEADS = 16
HEAD_DIM = 64
N_KV = 4
GROUP = N_HEADS // N_KV
ROPE_DIM = HEAD_DIM // 4
ROPE_THETA = 500000.0
CMP_BLOCK = 32
CMP_STRIDE = 16
CMP_HIDDEN = 4 * HEAD_DIM
SEL_BLOCK = 64
SEL_TOPK = 16
WINDOW = 512
Q_BLOCK = 128
CONV_DIM = D_MODEL
CONV_WIDTH = 31
D_FF = ((8 * D_MODEL // 3 + 255) // 256) * 256
EPS = 1e-6

kernel_name = "hybrid_nsa_conformer_gated_block"

SPLIT_SIZES = [N_HEADS * HEAD_DIM] + [N_KV * HEAD_DIM] * 6 + [3 * N_HEADS, CONV_DIM, CONV_DIM, D_MODEL, D_MODEL]
IN_COLS = sum(SPLIT_SIZES)


def rms_norm(x, g):
    xf = x.astype(jnp.float32)
    y = xf * lax.rsqrt(jnp.mean(xf * xf, axis=-1, keepdims=True) + EPS)
    return (y * g.astype(jnp.float32)).astype(x.dtype)


def rope_tables(positions, dtype):
    inv = ROPE_THETA ** (-jnp.arange(0, ROPE_DIM, 2, dtype=jnp.float32) / ROPE_DIM)
    ang = positions.astype(jnp.float32)[..., None] * inv
    return jnp.cos(ang).astype(dtype), jnp.sin(ang).astype(dtype)


def apply_partial_rope(t, cos, sin):
    half = ROPE_DIM // 2
    c = cos[:, :, None, :]
    s = sin[:, :, None, :]
    t1 = t[..., :half]
    t2 = t[..., half:ROPE_DIM]
    return jnp.concatenate([t1 * c - t2 * s, t2 * c + t1 * s, t[..., ROPE_DIM:]], axis=-1)


def masked_softmax(s, mask):
    s = jnp.where(mask, s, -jnp.inf)
    m = jnp.max(s, axis=-1, keepdims=True)
    m = jnp.where(jnp.isfinite(m), m, 0.0)
    e = jnp.where(mask, jnp.exp(s - m), 0.0)
    d = jnp.sum(e, axis=-1, keepdims=True)
    return e / jnp.where(d > 0, d, 1.0)


def compress_blocks(t, pos_emb, w1, w2):
    B, G, S, Dh = t.shape
    chunks = t.reshape(B, G, S // CMP_STRIDE, CMP_STRIDE, Dh)
    blocks = jnp.concatenate([chunks[:, :, :-1], chunks[:, :, 1:]], axis=3)
    blocks = (blocks + pos_emb).reshape(B, G, -1, CMP_BLOCK * Dh)
    return jax.nn.silu(blocks @ w1) @ w2


def gather_sel_blocks(t, idx):
    B, G, S, Dh = t.shape
    Qn, K = idx.shape[2], idx.shape[3]
    tb = t.reshape(B, G, S // SEL_BLOCK, SEL_BLOCK * Dh)
    g = jax.vmap(jax.vmap(lambda a, i: a[i]))(tb, idx.reshape(B, G, Qn * K))
    return g.reshape(B, G, Qn, K * SEL_BLOCK, Dh)


def nsa_attention(q, k_c, v_c, k_s, v_s, k_w, v_w, gates):
    B, G, R, S, Dh = q.shape
    dt = q.dtype
    scale = HEAD_DIM ** -0.5
    n_cmp = k_c.shape[2]
    n_sel = S // SEL_BLOCK
    top_k = min(SEL_TOPK, n_sel)
    c_end = jnp.arange(n_cmp) * CMP_STRIDE + CMP_BLOCK - 1
    ks = jnp.arange(n_cmp)[:, None] * CMP_STRIDE
    bs = jnp.arange(n_sel)[None, :] * SEL_BLOCK
    overlap = (jnp.clip(jnp.minimum(ks + CMP_BLOCK, bs + SEL_BLOCK) - jnp.maximum(ks, bs), 0)
               .astype(jnp.float32) / CMP_STRIDE)
    k_w_pad = jnp.pad(k_w, ((0, 0), (0, 0), (WINDOW, 0), (0, 0)))
    v_w_pad = jnp.pad(v_w, ((0, 0), (0, 0), (WINDOW, 0), (0, 0)))
    sel_off = jnp.arange(SEL_BLOCK)
    j_all = jnp.arange(n_sel)

    def one_block(i):
        t0 = i * Q_BLOCK
        qb = lax.dynamic_slice_in_dim(q, t0, Q_BLOCK, axis=3)
        gb = lax.dynamic_slice_in_dim(gates, t0, Q_BLOCK, axis=3)
        tpos = t0 + jnp.arange(Q_BLOCK)
        s_c = jnp.einsum('bgrqd,bgnd->bgrqn', qb, k_c).astype(jnp.float32) * scale
        p_c = masked_softmax(s_c, c_end[None, :] <= tpos[:, None])
        o_c = jnp.einsum('bgrqn,bgnd->bgrqd', p_c.astype(dt), v_c)
        imp = jnp.einsum('bgrqn,nj->bgqj', p_c, overlap)
        qblk = tpos // SEL_BLOCK
        forced = (j_all[None, :] == 0) | (j_all[None, :] == qblk[:, None]) | (j_all[None, :] == qblk[:, None] - 1)
        causal = j_all[None, :] <= qblk[:, None]
        score = jnp.where(forced, jnp.inf, jnp.where(causal, imp, -jnp.inf))
        vals, idx = lax.top_k(score, top_k)
        ks_g = gather_sel_blocks(k_s, idx)
        vs_g = gather_sel_blocks(v_s, idx)
        tok = idx[..., None] * SEL_BLOCK + sel_off
        m_s = ((vals != -jnp.inf)[..., None] & (tok <= tpos[:, None, None])).reshape(B, G, Q_BLOCK, top_k * SEL_BLOCK)
        s_s = jnp.einsum('bgrqd,bgqkd->bgrqk', qb, ks_g).astype(jnp.float32) * scale
        p_s = masked_softmax(s_s, m_s[:, :, None])
        o_s = jnp.einsum('bgrqk,bgqkd->bgrqd', p_s.astype(dt), vs_g)
        kw = lax.dynamic_slice_in_dim(k_w_pad, t0, WINDOW + Q_BLOCK, axis=2)
        vw = lax.dynamic_slice_in_dim(v_w_pad, t0, WINDOW + Q_BLOCK, axis=2)
        kpos = t0 - WINDOW + jnp.arange(WINDOW + Q_BLOCK)
        m_w = (kpos[None, :] <= tpos[:, None]) & (kpos[None, :] > tpos[:, None] - WINDOW) & (kpos[None, :] >= 0)
        s_w = jnp.einsum('bgrqd,bgkd->bgrqk', qb, kw).astype(jnp.float32) * scale
        p_w = masked_softmax(s_w, m_w)
        o_w = jnp.einsum('bgrqk,bgkd->bgrqd', p_w.astype(dt), vw)
        return gb[..., 0:1] * o_c + gb[..., 1:2] * o_s + gb[..., 2:3] * o_w

    out = lax.map(one_block, jnp.arange(S // Q_BLOCK))
    return out.transpose(1, 0, 4, 2, 3, 5).reshape(B, S, G * R * Dh)


def conformer_conv(u_a, u_b, conv_w, conv_b, ln_g, ln_b, w_o, b_o):
    z = u_a * jax.nn.sigmoid(u_b)
    zp = jnp.pad(z, ((0, 0), (CONV_WIDTH - 1, 0), (0, 0)))
    z = lax.conv_general_dilated(zp, conv_w, window_strides=(1,), padding='VALID',
                                 dimension_numbers=('NWC', 'WIO', 'NWC'),
                                 feature_group_count=CONV_DIM) + conv_b
    zf = z.astype(jnp.float32)
    mu = jnp.mean(zf, axis=-1, keepdims=True)
    var = jnp.mean(jnp.square(zf - mu), axis=-1, keepdims=True)
    zn = ((zf - mu) * lax.rsqrt(var + EPS) * ln_g.astype(jnp.float32) + ln_b.astype(jnp.float32)).astype(z.dtype)
    return jax.nn.silu(zn) @ w_o + b_o


def setup_inputs(seed: int = 0) -> dict:
    key = jax.random.key(seed)
    ks = jax.random.split(key, 24)
    f32 = jnp.float32
    nrm = lambda k, shape, s: jax.random.normal(k, shape, f32) * s
    d = D_MODEL
    return {
        'x': jax.random.normal(ks[0], (BATCH, SEQ, d), f32),
        'positions': jnp.broadcast_to(jnp.arange(SEQ, dtype=jnp.int32), (BATCH, SEQ)),
        'norm_mix_g': 1.0 + nrm(ks[1], (d,), 0.01),
        'w_in': nrm(ks[2], (d, IN_COLS), d ** -0.5),
        'cmp_k_pos': nrm(ks[3], (CMP_BLOCK, HEAD_DIM), 0.02),
        'cmp_k_w1': nrm(ks[4], (CMP_BLOCK * HEAD_DIM, CMP_HIDDEN), (CMP_BLOCK * HEAD_DIM) ** -0.5),
        'cmp_k_w2': nrm(ks[5], (CMP_HIDDEN, HEAD_DIM), CMP_HIDDEN ** -0.5),
        'cmp_v_pos': nrm(ks[6], (CMP_BLOCK, HEAD_DIM), 0.02),
        'cmp_v_w1': nrm(ks[7], (CMP_BLOCK * HEAD_DIM, CMP_HIDDEN), (CMP_BLOCK * HEAD_DIM) ** -0.5),
        'cmp_v_w2': nrm(ks[8], (CMP_HIDDEN, HEAD_DIM), CMP_HIDDEN ** -0.5),
        'conv_w': nrm(ks[9], (CONV_WIDTH, 1, CONV_DIM), CONV_WIDTH ** -0.5),
        'conv_b': nrm(ks[10], (CONV_DIM,), 0.01),
        'conv_norm_g': 1.0 + nrm(ks[11], (CONV_DIM,), 0.01),
        'conv_norm_b': nrm(ks[12], (CONV_DIM,), 0.01),
        'w_conv_out': nrm(ks[13], (CONV_DIM, d), CONV_DIM ** -0.5),
        'b_conv_out': nrm(ks[14], (d,), 0.01),
        'w_out': nrm(ks[15], (d, d), d ** -0.5),
        'norm_ffn_g': 1.0 + nrm(ks[16], (d,), 0.01),
        'w_ffn_gate': nrm(ks[17], (d, D_FF), d ** -0.5),
        'w_ffn_up': nrm(ks[18], (d, D_FF), d ** -0.5),
        'w_ffn_down': nrm(ks[19], (D_FF, d), D_FF ** -0.5),
        'norm_final_g': 1.0 + nrm(ks[20], (d,), 0.01),
    }


def reference(x, positions, norm_mix_g, w_in, cmp_k_pos, cmp_k_w1, cmp_k_w2, cmp_v_pos, cmp_v_w1, cmp_v_w2,
              conv_w, conv_b, conv_norm_g, conv_norm_b, w_conv_out, b_conv_out, w_out,
              norm_ffn_g, w_ffn_gate, w_ffn_up, w_ffn_down, norm_final_g):
    B, S, _ = x.shape
    cos, sin = rope_tables(positions, x.dtype)
    split_idx = [int(v) for v in np.cumsum(SPLIT_SIZES)[:-1]]
    for _layer in range(DEPTH):
        h = rms_norm(x, norm_mix_g)
        parts = jnp.split(h @ w_in, split_idx, axis=-1)
        (q, kc, vc, ks_, vs_, kw, vw, g_nsa, u_a, u_b, g_attn, g_conv) = parts
        q = apply_partial_rope(q.reshape(B, S, N_HEADS, HEAD_DIM), cos, sin)
        q = q.reshape(B, S, N_KV, GROUP, HEAD_DIM).transpose(0, 2, 3, 1, 4)
        rk = lambda t: apply_partial_rope(t.reshape(B, S, N_KV, HEAD_DIM), cos, sin).transpose(0, 2, 1, 3)
        rv = lambda t: t.reshape(B, S, N_KV, HEAD_DIM).transpose(0, 2, 1, 3)
        k_c = compress_blocks(rk(kc), cmp_k_pos, cmp_k_w1, cmp_k_w2)
        v_c = compress_blocks(rv(vc), cmp_v_pos, cmp_v_w1, cmp_v_w2)
        gates = jax.nn.sigmoid(g_nsa).reshape(B, S, 3, N_KV, GROUP).transpose(0, 3, 4, 1, 2)
        attn = nsa_attention(q, k_c, v_c, rk(ks_), rv(vs_), rk(kw), rv(vw), gates)
        conv = conformer_conv(u_a, u_b, conv_w, conv_b, conv_norm_g, conv_norm_b, w_conv_out, b_conv_out)
        merged = jax.nn.sigmoid(g_attn) * attn + jax.nn.sigmoid(g_conv) * conv
        x = x + merged @ w_out
        h = rms_norm(x, norm_ffn_g)
        x = x + (jax.nn.silu(h @ w_ffn_gate) * (h @ w_ffn_up)) @ w_ffn_down
    return rms_norm(x, norm_final_g)
```

```python
import numpy as np
from contextlib import ExitStack
import concourse.bass as bass
import concourse.mybir as mybir

F32 = mybir.dt.float32
BF16 = mybir.dt.bfloat16
I32 = mybir.dt.int32
AF = mybir.ActivationFunctionType
ALU = mybir.AluOpType
AX = mybir.AxisListType

ENGS = ("pe", "act", "dve", "pool", "sp")
WRITE_KW = ("out", "accum_out", "out_max", "out_indices", "ap")
DT_SIZE = {F32: 4, BF16: 2, I32: 4}


def _dtsize(dt):
    try:
        return DT_SIZE[dt]
    except Exception:
        return mybir.dt.size(dt)


class Op:
    __slots__ = ("eng", "fn", "deps", "is_dma", "needed", "sig", "gidx", "pre_wait", "cc")

    def __init__(self, eng, fn, is_dma, gidx):
        self.eng = eng
        self.fn = fn
        self.deps = []
        self.is_dma = is_dma
        self.needed = False
        self.sig = None
        self.gidx = gidx
        self.pre_wait = None
        self.cc = False


class _EngProxy:
    def __init__(self, prog, eng):
        self._p = prog
        self._e = eng

    def __getattr__(self, name):
        p, e = self._p, self._e

        def call(*args, **kw):
            extra_r = kw.pop("_r", ())
            extra_w = kw.pop("_w", ())
            return p._record(e, name, args, kw, extra_r, extra_w)

        return call


class Prog:
    RING = 12

    def __init__(self, nc, prefix="", sem_stack=None):
        self.nc = nc
        self.prefix = prefix
        self.sem_stack = sem_stack
        self.ops = {e: [] for e in ENGS}
        self.n = 0
        self.blk = {}
        self.untracked = set()
        self.last_w = {}
        self.readers = {}
        self.pe = _EngProxy(self, "pe")
        self.act = _EngProxy(self, "act")
        self.dve = _EngProxy(self, "dve")
        self.pool = _EngProxy(self, "pool")
        self.sp = _EngProxy(self, "sp")
        self.stack = ExitStack()
        self.dma_count = {e: 0 for e in ENGS}

    def sbuf(self, name, shape, dtype, blk=None):
        name = self.prefix + name
        t = self.stack.enter_context(self.nc.sbuf_tensor(name, list(shape), dtype))
        self.blk[name] = None if blk is None else blk * _dtsize(dtype)
        return t

    def psum(self, name, shape, dtype, blk=None):
        name = self.prefix + name
        t = self.stack.enter_context(self.nc.psum_tensor(name, list(shape), dtype))
        self.blk[name] = None if blk is None else blk * _dtsize(dtype)
        return t

    def dram(self, name, shape, dtype, kind, track=False, blk=None):
        if kind == "Internal":
            t = self.nc.dram_tensor(name, list(shape), dtype)
        else:
            t = self.nc.dram_tensor(name, list(shape), dtype, kind=kind)
        if not track:
            self.untracked.add(name)
        else:
            self.blk[name] = None if blk is None else blk * _dtsize(dtype)
        return t

    def keys(self, ap):
        t = ap.tensor
        name = t.name
        if name in self.untracked:
            return []
        b = self.blk.get(name, None)
        if b is None:
            return [name]
        es = _dtsize(ap.dtype)
        space = str(type(t).__name__)
        pairs = list(ap.ap)
        if "DRam" in space:
            lo = ap.offset
            hi = lo + sum((c - 1) * abs(s) for s, c in pairs)
        else:
            shp = list(t.shape)
            F = 1
            for d in shp[1:]:
                F *= d
            F = F * _dtsize(t.dtype) // es
            lo = ap.offset % F
            hi = lo + sum((c - 1) * abs(s) for s, c in pairs[1:])
        lo_b, hi_b = lo * es, hi * es + es - 1
        return [(name, i) for i in range(lo_b // b, hi_b // b + 1)]

    def _record(self, eng, name, args, kw, extra_r, extra_w):
        is_dma = name in ("dma_start", "collective_compute", "dma_start_transpose")
        rk, wk = [], []
        if name == "collective_compute":
            for a in kw.get("ins", []):
                rk += self.keys(a)
            for a in kw.get("outs", []):
                wk += self.keys(a)
        else:
            for k, v in kw.items():
                if isinstance(v, bass.AP):
                    if "PSum" in type(v.tensor).__name__:
                        wk.extend(self.keys(v))
                    else:
                        (wk if k in WRITE_KW else rk).extend(self.keys(v))
            for a in args:
                if isinstance(a, bass.AP):
                    raise ValueError("pass APs as kwargs")
        for a in extra_r:
            rk += self.keys(a) if isinstance(a, bass.AP) else [a]
        for a in extra_w:
            wk += self.keys(a) if isinstance(a, bass.AP) else [a]

        def fn(engobj, name=name, args=args, kw=kw):
            return getattr(engobj, name)(*args, **kw)

        op = self.raw(eng, fn, rk, wk, is_dma)
        op.cc = (name == "collective_compute")
        return op

    def raw(self, eng, fn, rk, wk, is_dma=False):
        op = Op(eng, fn, is_dma, self.n)
        self.n += 1
        deps = {}
        for k in rk:
            lw = self.last_w.get(k)
            if lw is not None:
                deps[lw.gidx] = lw
        for k in wk:
            lw = self.last_w.get(k)
            if lw is not None:
                deps[lw.gidx] = lw
            rd = self.readers.get(k)
            if rd:
                for e, o in rd.items():
                    if e == "dma":
                        for oo in o:
                            deps[oo.gidx] = oo
                    else:
                        deps[o.gidx] = o
        for k in wk:
            self.last_w[k] = op
            self.readers[k] = {}
        for k in rk:
            if k in wk:
                continue
            rd = self.readers.setdefault(k, {})
            if is_dma:
                rd.setdefault("dma", []).append(op)
            else:
                rd[eng] = op
        for d in deps.values():
            if d.eng == eng and eng == "pe" and not d.is_dma and not is_dma:
                continue
            op.deps.append(d)
            d.needed = True
        if is_dma:
            op.needed = True
        self.ops[eng].append(op)
        return op

    def emit(self, final_wait_engs=("sp", "pool", "act")):
        nc = self.nc
        allsems = []

        def newsem(name):
            h = nc.alloc_semaphore(name=name)
            allsems.append(h)
            return h

        esem = {e: newsem(self.prefix + "es_" + e) for e in ENGS}
        ccsem = newsem(self.prefix + "ccsem")
        ncc = 0
        rings = {}
        for e in ENGS:
            if any(o.is_dma for o in self.ops[e]):
                rings[e] = [newsem(f"{self.prefix}dr_{e}_{i}") for i in range(self.RING)]
        for e in ENGS:
            c = 0
            nd = 0
            for o in self.ops[e]:
                if o.cc:
                    ncc += 1
                    o.sig = (ccsem, ncc)
                elif o.is_dma:
                    sem = rings[e][nd % self.RING]
                    o.sig = (sem, 16 * (nd // self.RING + 1))
                    if nd >= self.RING:
                        o.pre_wait = (sem, 16 * (nd // self.RING))
                    nd += 1
                elif o.needed:
                    c += 1
                    o.sig = (esem[e], c)
            self.dma_count[e] = nd
        prog = self

        def run_engine(e, engobj):
            waited = {}

            def wait(sem, val):
                key = id(sem)
                if waited.get(key, 0) >= val:
                    return
                engobj.wait_ge(sem, val)
                waited[key] = val

            semobj = {}
            for o in prog.ops[e]:
                for d in o.deps:
                    sem, val = d.sig
                    semobj[id(sem)] = sem
                    wait(sem, val)
                if o.pre_wait is not None:
                    wait(*o.pre_wait)
                ins = o.fn(engobj)
                if o.sig is not None:
                    sem, val = o.sig
                    if o.cc:
                        ins.then_inc(sem, 1)
                    elif o.is_dma:
                        ins.then_inc(sem, 16)
                    else:
                        ins.then_inc(sem, 1)
            if e in rings:
                nd = prog.dma_count[e]
                for i, sem in enumerate(rings[e]):
                    cnt = (nd - i + prog.RING - 1) // prog.RING if nd > i else 0
                    if cnt > 0:
                        wait(sem, 16 * cnt)

        with nc.Block() as block:
            if self.ops["sp"]:
                block.sync(lambda eng: run_engine("sp", eng))
            if self.ops["act"]:
                block.scalar(lambda eng: run_engine("act", eng))
            if self.ops["dve"]:
                block.vector(lambda eng: run_engine("dve", eng))
            if self.ops["pool"]:
                block.gpsimd(lambda eng: run_engine("pool", eng))
            if self.ops["pe"]:
                block.tensor(lambda eng: run_engine("pe", eng))
        nc.clear_and_free_semaphores(allsems)
        nc.all_engine_barrier()
        self.stack.close()
import math
import numpy as np

NEG = -240000.0
INV_FREQ = [float(np.float32(500000.0) ** np.float32(-(2 * i) / 16.0)) for i in range(8)]
MAGIC = 12582912.0
TWO_PI = 2.0 * math.pi


def AP(t, off, pairs):
    return bass.AP(t, off, [list(p) for p in pairs])


def build_A(S):
    nc = bass.Bass("TRN2", target_bir_lowering=False)
    P = Prog(nc)
    record_A(P, S, None)
    P.emit()
    return nc


def record_A(P, S, scrA):
    NT = S // 128
    NCMAX = S // 16 - 1
    NCT = (NCMAX + 127) // 128
    D = lambda n, sh, dt=F32, kind="ExternalInput": P.dram(n, sh, dt, kind)
    xa = D("xa", [S, 1024])
    pos = D("pos", [NT, 128], I32)
    wa = D("wa", [1024, 652])
    gmix = D("gmix", [128, 8])
    kw1 = D("kw1", [64, 32, 256]); kw2 = D("kw2", [128, 2, 64]); kpos = D("kpos", [64, 32])
    vw1 = D("vw1", [64, 32, 256]); vw2 = D("vw2", [128, 2, 64]); vpos = D("vpos", [64, 32])
    identd = D("identf", [128, 128])
    mwd = D("mw", [128, 1024]); cmwd = D("cmw", [128, 256]); fbwd = D("fbw", [128, 256])
    exnd = D("exn", [128, S]); diagd = D("diagc", [128, 512]); wlowd = D("wlow", [128, 512])
    attn = D("attn", [S, 256], F32, "ExternalOutput") if scrA is None else None

    identf = P.sbuf("identf_s", [128, 128], F32)
    identb = P.sbuf("identb", [128, 128], BF16)
    stg = P.sbuf("stg", [128, 8 * 652], F32)
    WA = P.sbuf("WA", [128, 8, 652], BF16)
    gm = P.sbuf("gm", [128, 8], F32)
    w1k = P.sbuf("w1k", [64, 32, 256], BF16); w1v = P.sbuf("w1v", [64, 32, 256], BF16)
    w2k = P.sbuf("w2k", [128, 2, 64], BF16); w2v = P.sbuf("w2v", [128, 2, 64], BF16)
    posk = P.sbuf("posk", [64, 32], BF16); posv = P.sbuf("posv", [64, 32], BF16)
    b1k = P.sbuf("b1k", [128, 2], F32); b1v = P.sbuf("b1v", [128, 2], F32)
    MW = P.sbuf("MW", [128, 1024], F32); CMW = P.sbuf("CMW", [128, 256], F32); FBW = P.sbuf("FBW", [128, 256], F32)
    EXN = P.sbuf("EXN", [128, S], BF16, blk=128)
    DIAG = P.sbuf("DIAG", [128, 512], BF16); WLOW = P.sbuf("WLOW", [128, 512], BF16)
    posi = P.sbuf("posi", [64, 128], I32); posf = P.sbuf("posf", [64, 128], F32)
    posT = P.sbuf("posT", [128, NT], F32)
    ANG = P.sbuf("ANG", [128, NT, 16], F32); RR = P.sbuf("RR", [128, NT, 16], F32); SC = P.sbuf("SC", [128, NT, 16], F32)
    KsT = P.sbuf("KsT", [64, S], BF16, blk=128); KwT = P.sbuf("KwT", [64, S], BF16, blk=128)
    Vs = P.sbuf("Vs", [128, NT, 65], BF16, blk=65); Vw = P.sbuf("Vw", [128, NT, 65], BF16, blk=65)
    KcR = P.sbuf("KcR", [64, 144], BF16); VcR = P.sbuf("VcR", [64, 144], BF16)
    KcT = P.sbuf("KcT", [64, NCT * 128], BF16); VcT = P.sbuf("VcT", [64, NCT * 128], BF16)
    Vc = P.sbuf("Vc", [128, NCT, 64], BF16)
    xs = [P.sbuf(f"xs{i}", [128, 1024], F32) for i in range(2)]
    junk = P.sbuf("junk", [128, 1024], BF16)
    ssq = P.sbuf("ssq", [128, 1], F32); rstd = P.sbuf("rstd", [128, 1], F32)
    hb = P.sbuf("hb", [128, 1024], BF16)
    hT = P.sbuf("hT", [128, 8, 128], BF16)
    pr = P.sbuf("pr", [128, 652], F32); prb = P.sbuf("prb", [128, 652], BF16)
    rt = [P.sbuf(f"rt{i}", [128, 7, 8], F32) for i in range(4)]
    gt = P.sbuf("gt", [128, 12], F32)
    QT = P.sbuf("QT", [64, 512], BF16)
    hid = P.sbuf("hid", [128, 2, 8], F32); hide = P.sbuf("hide", [128, 2, 8], F32); hsb = P.sbuf("hsb", [128, 2, 8], BF16)
    pc = P.sbuf("pc", [128, 4, 512], F32, blk=512); pcb = P.sbuf("pcb", [128, 4, 512], BF16, blk=512)
    den = P.sbuf("den", [128, 4], F32); rden = P.sbuf("rden", [128, 4], F32)
    PP = P.sbuf("PP", [128, 520], F32)
    imp = P.sbuf("imp", [128, 128], F32); score = P.sbuf("score", [128, 128], F32); wk = P.sbuf("wk", [128, 128], F32)
    m8 = P.sbuf("m8", [128, 16], F32)
    nsb = P.sbuf("nsb", [128, 128], BF16); nsT = P.sbuf("nsT", [128, 4, 128], BF16)
    pTc = P.sbuf("pTc", [128, NCT, 128], BF16)
    pT = [P.sbuf(f"pT{i}", [128, 512], BF16) for i in range(3)]
    oc = P.sbuf("oc", [128, 4, 64], F32)
    oTs = P.sbuf("oTs", [65, 512], F32); oTw = P.sbuf("oTw", [65, 512], F32)
    otok = [P.sbuf(f"otok{i}", [128, 4, 65], F32) for i in range(2)]
    fac = [P.sbuf(f"fac{i}", [128, 4], F32) for i in range(2)]
    at = P.sbuf("at", [128, 4, 64], F32); at2 = P.sbuf("at2", [128, 4, 64], F32)
    atT = P.sbuf("atT", [128, 2, 128], F32)
    pb = [P.psum(f"pb{i}", [128, 1024], BF16) for i in range(2)]
    pf = [P.psum(f"pf{i}", [128, 512], F32) for i in range(6)]

    P.sp.dma_start(out=identf[:], in_=identd.ap())
    P.dve.tensor_copy(out=identb[:], in_=identf[:])
    P.sp.dma_start(out=gm[:], in_=gmix.ap())
    stgw = AP(stg, 0, [[8 * 652, 128], [652, 8], [1, 652]])
    P.sp.dma_start(out=stgw, in_=wa.ap().rearrange("(c p) n -> p c n", p=128))
    for c in range(8):
        P.dve.tensor_scalar(out=WA[:, c, :], in0=AP(stg, c * 652, [[8 * 652, 128], [1, 652]]),
                            scalar1=gm[:, c:c + 1], scalar2=None, op0=ALU.mult)
    for (w1d, w1s, w2d, w2s, pd, ps_, b1) in ((kw1, w1k, kw2, w2k, kpos, posk, b1k), (vw1, w1v, vw2, w2v, vpos, posv, b1v)):
        for q4 in range(2):
            sv = AP(stg, 0, [[8 * 652, 64], [256, 16], [1, 256]])
            P.sp.dma_start(out=sv, in_=w1d[:, q4 * 16:(q4 + 1) * 16, :])
            P.act.activation(out=w1s[:, q4 * 16:(q4 + 1) * 16, :], in_=sv, func=AF.Copy)
        sv = AP(stg, 0, [[8 * 652, 128], [64, 2], [1, 64]])
        P.sp.dma_start(out=sv, in_=w2d.ap())
        P.act.activation(out=w2s[:], in_=sv, func=AF.Copy)
        sv = AP(stg, 0, [[8 * 652, 64], [1, 32]])
        P.sp.dma_start(out=sv, in_=pd.ap())
        P.act.activation(out=ps_[:], in_=sv, func=AF.Copy)
        for half in range(2):
            for tau in range(32):
                P.pe.matmul(out=pf[1][:, half:half + 1], lhsT=w1s[:, tau, half * 128:(half + 1) * 128],
                            rhs=ps_[:, tau:tau + 1], start=(tau == 0), stop=(tau == 31))
        P.dve.tensor_copy(out=b1[:], in_=pf[1][:, 0:2])
    P.sp.dma_start(out=MW[:], in_=mwd.ap())
    P.sp.dma_start(out=CMW[:], in_=cmwd.ap())
    P.sp.dma_start(out=FBW[:], in_=fbwd.ap())
    for c0 in range(0, S, 2048):
        w = min(2048, S - c0)
        sv = AP(stg, 0, [[8 * 652, 128], [1, w]])
        P.sp.dma_start(out=sv, in_=exnd[:, c0:c0 + w])
        P.act.activation(out=EXN[:, c0:c0 + w], in_=sv, func=AF.Copy)
    for (dd, ds) in ((diagd, DIAG), (wlowd, WLOW)):
        sv = AP(stg, 0, [[8 * 652, 128], [1, 512]])
        P.sp.dma_start(out=sv, in_=dd.ap())
        P.act.activation(out=ds[:], in_=sv, func=AF.Copy)
    for t_ in (KcT, VcT, KcR, VcR):
        P.pool.memset(ap=t_[:], constant=0.0)
    P.pool.memset(ap=Vc[:], constant=0.0)
    P.pool.memset(ap=pcb[:], constant=0.0)
    P.pool.memset(ap=PP[:], constant=0.0)
    P.pool.memset(ap=Vs[:, :, 64:65], constant=1.0)
    P.pool.memset(ap=Vw[:, :, 64:65], constant=1.0)
    P.sp.dma_start(out=posi[0:NT, :], in_=pos.ap())
    P.dve.tensor_copy(out=posf[0:NT, :], in_=posi[0:NT, :])
    P.pe.transpose(out=pf[1][:, 0:NT], in_=posf[0:NT, :], identity=identf[0:NT, 0:NT])
    P.dve.tensor_copy(out=posT[:], in_=pf[1][:, 0:NT])
    for i in range(8):
        P.dve.tensor_scalar(out=ANG[:, :, i], in0=posT[:], scalar1=INV_FREQ[i], scalar2=None, op0=ALU.mult)
        P.dve.tensor_scalar(out=ANG[:, :, 8 + i], in0=posT[:], scalar1=INV_FREQ[i], scalar2=math.pi / 2, op0=ALU.mult, op1=ALU.add)
    P.dve.tensor_scalar(out=RR[:], in0=ANG[:], scalar1=1.0 / TWO_PI, scalar2=MAGIC, op0=ALU.mult, op1=ALU.add)
    P.dve.tensor_scalar(out=RR[:], in0=RR[:], scalar1=MAGIC, scalar2=-TWO_PI, op0=ALU.subtract, op1=ALU.mult)
    P.dve.tensor_tensor(out=RR[:], in0=RR[:], in1=ANG[:], op=ALU.add)
    P.dve.tensor_scalar(out=RR[:], in0=RR[:], scalar1=3.1415925, scalar2=-3.1415925, op0=ALU.min, op1=ALU.max)
    P.act.activation(out=SC[:], in_=RR[:], func=AF.Sin)

    FSC = NT * 16
    nps = [0]

    def score_bank():
        nps[0] += 1
        return pf[2 + nps[0] % 2]

    npt = [0]

    def pt_buf():
        npt[0] += 1
        return pT[npt[0] % 3]

    for i in range(NT):
        x_ = xs[i % 2]
        P.sp.dma_start(out=x_[:], in_=xa[i * 128:(i + 1) * 128, :])
        P.dve.memset(ap=ssq[:], constant=0.0)
        P.act.activation(out=junk[:], in_=x_[:], func=AF.Square, accum_out=ssq[:])
        P.dve.tensor_scalar(out=rstd[:], in0=ssq[:], scalar1=1.0 / 1024.0, scalar2=1e-6, op0=ALU.mult, op1=ALU.add)
        P.act.activation(out=rstd[:], in_=rstd[:], func=AF.Sqrt)
        P.dve.reciprocal(out=rstd[:], in_=rstd[:])
        P.dve.tensor_scalar(out=hb[:], in0=x_[:], scalar1=rstd[:, 0:1], scalar2=None, op0=ALU.mult)
        for c in range(8):
            P.pe.transpose(out=pb[0][:, c * 128:(c + 1) * 128], in_=hb[:, c * 128:(c + 1) * 128], identity=identb[:])
        P.act.activation(out=hT[:], in_=pb[0][:, :], func=AF.Copy)
        for c in range(8):
            P.pe.matmul(out=pf[0][:, :], lhsT=hT[:, c, :], rhs=WA[:, c, 0:512], start=(c == 0), stop=(c == 7))
        for c in range(8):
            P.pe.matmul(out=pf[1][:, 0:140], lhsT=hT[:, c, :], rhs=WA[:, c, 512:652], start=(c == 0), stop=(c == 7))
        P.dve.tensor_copy(out=pr[:, 0:512], in_=pf[0][:, :])
        P.act.activation(out=pr[:, 512:652], in_=pf[1][:, 0:140], func=AF.Copy)
        P.dve.tensor_copy(out=prb[:], in_=pr[:])
        t1 = AP(pr, 0, [[652, 128], [64, 7], [1, 8]])
        t2 = AP(pr, 8, [[652, 128], [64, 7], [1, 8]])
        sin_b = AP(SC, i * 16, [[FSC, 128], [0, 7], [1, 8]])
        cos_b = AP(SC, i * 16 + 8, [[FSC, 128], [0, 7], [1, 8]])
        P.dve.tensor_tensor(out=rt[0][:], in0=t1, in1=cos_b, op=ALU.mult)
        P.dve.tensor_tensor(out=rt[1][:], in0=t2, in1=sin_b, op=ALU.mult)
        P.dve.tensor_tensor(out=rt[2][:], in0=t2, in1=cos_b, op=ALU.mult)
        P.dve.tensor_tensor(out=rt[3][:], in0=t1, in1=sin_b, op=ALU.mult)
        P.dve.tensor_tensor(out=AP(prb, 0, [[652, 128], [64, 7], [1, 8]]), in0=rt[0][:], in1=rt[1][:], op=ALU.subtract)
        P.dve.tensor_tensor(out=AP(prb, 8, [[652, 128], [64, 7], [1, 8]]), in0=rt[2][:], in1=rt[3][:], op=ALU.add)
        P.act.activation(out=gt[:], in_=pr[:, 640:652], func=AF.Exp, scale=-1.0)
        P.dve.tensor_scalar(out=gt[:], in0=gt[:], scalar1=1.0, scalar2=None, op0=ALU.add)
        P.dve.reciprocal(out=gt[:], in_=gt[:])
        for r in range(4):
            P.pe.transpose(out=pb[1][0:64, r * 128:(r + 1) * 128], in_=prb[:, r * 64:(r + 1) * 64], identity=identb[:])
        for j, c0 in enumerate((256, 320, 384, 448)):
            P.pe.transpose(out=pb[1][0:64, 512 + j * 128:512 + (j + 1) * 128], in_=prb[:, c0:c0 + 64], identity=identb[:])
        P.dve.tensor_copy(out=QT[:], in_=pb[1][0:64, 0:512])
        if i > 0:
            P.dve.tensor_copy(out=KcR[:, 0:16], in_=KcR[:, 128:144])
            P.dve.tensor_copy(out=VcR[:, 0:16], in_=VcR[:, 128:144])
        P.act.activation(out=KcR[:, 16:144], in_=pb[1][0:64, 512:640], func=AF.Copy)
        P.act.activation(out=KsT[:, i * 128:(i + 1) * 128], in_=pb[1][0:64, 640:768], func=AF.Copy)
        P.act.activation(out=KwT[:, i * 128:(i + 1) * 128], in_=pb[1][0:64, 768:896], func=AF.Copy)
        P.act.activation(out=VcR[:, 16:144], in_=pb[1][0:64, 896:1024], func=AF.Copy)
        P.dve.tensor_copy(out=Vs[:, i, 0:64], in_=prb[:, 512:576])
        P.dve.tensor_copy(out=Vw[:, i, 0:64], in_=prb[:, 576:640])
        nb = 8 if i > 0 else 7
        base = 0 if i > 0 else 16
        n0 = 8 * i - 1 if i > 0 else 0
        ncn = 8 * i + 7
        for (roll, w1s, w2s, b1, dstT) in ((KcR, w1k, w2k, b1k, KcT), (VcR, w1v, w2v, b1v, VcT)):
            for half in range(2):
                for tau in range(32):
                    P.pe.matmul(out=pf[1][:, half * 16:half * 16 + nb], lhsT=w1s[:, tau, half * 128:(half + 1) * 128],
                                rhs=AP(roll, base + tau, [[144, 64], [16, nb]]), start=(tau == 0), stop=(tau == 31))
            for half in range(2):
                P.dve.tensor_scalar(out=hid[:, half, 0:nb], in0=pf[1][:, half * 16:half * 16 + nb], scalar1=b1[:, half:half + 1],
                                    scalar2=None, op0=ALU.add)
            P.act.activation(out=hide[:, :, 0:nb], in_=hid[:, :, 0:nb], func=AF.Exp, scale=-1.0)
            P.dve.tensor_scalar(out=hide[:, :, 0:nb], in0=hide[:, :, 0:nb], scalar1=1.0, scalar2=None, op0=ALU.add)
            P.dve.reciprocal(out=hide[:, :, 0:nb], in_=hide[:, :, 0:nb])
            P.dve.tensor_tensor(out=hsb[:, :, 0:nb], in0=hid[:, :, 0:nb], in1=hide[:, :, 0:nb], op=ALU.mult)
            for half in range(2):
                P.pe.matmul(out=pf[1][0:64, 32:32 + nb], lhsT=w2s[:, half, :], rhs=hsb[:, half, 0:nb], start=(half == 0), stop=(half == 1))
            P.dve.tensor_copy(out=dstT[:, n0:n0 + nb], in_=pf[1][0:64, 32:32 + nb])
        nct = (ncn + 127) // 128
        for ct in range(n0 // 128, nct):
            P.pe.transpose(out=pb[1][:, 0:64], in_=VcT[:, ct * 128:(ct + 1) * 128], identity=identb[0:64, 0:64])
            P.dve.tensor_copy(out=Vc[:, ct, :], in_=pb[1][:, 0:64])
        for r in range(4):
            sb = score_bank()
            P.pe.matmul(out=sb[:, 0:ncn], lhsT=QT[:, r * 128:(r + 1) * 128], rhs=KcT[:, 0:ncn], start=True, stop=True)
            P.act.activation(out=pc[:, r, 0:ncn], in_=sb[:, 0:ncn], func=AF.Exp, scale=0.125)
            P.dve.tensor_tensor(out=pc[:, r, 0:ncn], in0=pc[:, r, 0:ncn], in1=MW[:, 512 - 8 * i:512 - 8 * i + ncn], op=ALU.mult)
            P.dve.reduce_sum(out=den[:, r:r + 1], in_=pc[:, r, 0:ncn], axis=AX.X)
        P.dve.tensor_scalar(out=rden[:], in0=den[:], scalar1=1e-30, scalar2=None, op0=ALU.max)
        P.dve.reciprocal(out=rden[:], in_=rden[:])
        for r in range(4):
            P.dve.tensor_scalar(out=pc[:, r, 0:ncn], in0=pc[:, r, 0:ncn], scalar1=rden[:, r:r + 1], scalar2=None, op0=ALU.mult)
            P.pool.tensor_copy(out=pcb[:, r, 0:ncn], in_=pc[:, r, 0:ncn])
        P.dve.tensor_tensor(out=PP[:, 1:1 + ncn], in0=pc[:, 0, 0:ncn], in1=pc[:, 1, 0:ncn], op=ALU.add)
        P.dve.tensor_tensor(out=PP[:, 1:1 + ncn], in0=PP[:, 1:1 + ncn], in1=pc[:, 2, 0:ncn], op=ALU.add)
        P.dve.tensor_tensor(out=PP[:, 1:1 + ncn], in0=PP[:, 1:1 + ncn], in1=pc[:, 3, 0:ncn], op=ALU.add)
        for r in range(4):
            for ct in range(nct):
                P.pe.transpose(out=pb[0][:, ct * 128:(ct + 1) * 128], in_=pcb[:, r, ct * 128:(ct + 1) * 128], identity=identb[:])
            P.act.activation(out=pTc[:, 0:nct, :], in_=pb[0][:, 0:nct * 128], func=AF.Copy)
            for ct in range(nct):
                P.pe.matmul(out=pf[1][:, 64 + r * 64:64 + (r + 1) * 64], lhsT=pTc[:, ct, :], rhs=Vc[:, ct, :], start=(ct == 0), stop=(ct == nct - 1))
        P.dve.tensor_copy(out=oc[:], in_=pf[1][:, 64:320])
        A = lambda k: AP(PP, k, [[520, 128], [4, 128]])
        P.dve.tensor_tensor(out=imp[:], in0=A(1), in1=A(2), op=ALU.add)
        P.dve.tensor_tensor(out=imp[:], in0=imp[:], in1=A(3), op=ALU.add)
        P.dve.scalar_tensor_tensor(out=imp[:], in0=imp[:], scalar=2.0, in1=A(0), op0=ALU.mult, op1=ALU.add)
        P.dve.tensor_tensor(out=imp[:], in0=imp[:], in1=A(4), op=ALU.add)
        P.dve.tensor_tensor(out=score[:], in0=imp[:], in1=CMW[:, 128 - 2 * i:256 - 2 * i], op=ALU.mult)
        P.dve.tensor_tensor(out=score[:], in0=score[:], in1=FBW[:, 128 - 2 * i:256 - 2 * i], op=ALU.add)
        P.dve.memset(ap=score[:, 0:1], constant=300.0)
        P.dve.max(out=m8[:, 0:8], in_=score[:])
        P.dve.match_replace(out=wk[:], in_to_replace=m8[:, 0:8], in_values=score[:], imm_value=-2.0)
        P.dve.max(out=m8[:, 8:16], in_=wk[:])
        P.dve.tensor_scalar(out=nsb[:], in0=score[:], scalar1=m8[:, 15:16], scalar2=None, op0=ALU.is_lt)
        P.pe.transpose(out=pb[1][:, 0:128], in_=nsb[:], identity=identb[:])
        P.dve.tensor_copy(out=nsT[:], in_=AP(pb[1], 0, [[1024, 128], [0, 4], [1, 128]]))
        nsTf = AP(nsT, 0, [[512, 128], [1, 512]])
        for kt in range(i + 1):
            sb = score_bank()
            P.pe.matmul(out=sb[:, :], lhsT=KsT[:, kt * 128:(kt + 1) * 128], rhs=QT[:], start=True, stop=False)
            P.pe.matmul(out=sb[:, :], lhsT=EXN[:, kt * 128:(kt + 1) * 128], rhs=nsTf, start=False, stop=(kt != i))
            if kt == i:
                P.pe.matmul(out=sb[:, :], lhsT=identb[:], rhs=DIAG[:], start=False, stop=True)
            p_ = pt_buf()
            P.act.activation(out=p_[:], in_=sb[:, :], func=AF.Exp, scale=0.125)
            P.pe.matmul(out=pf[4][0:65, :], lhsT=Vs[:, kt, :], rhs=p_[:], start=(kt == 0), stop=(kt == i))
        k0 = max(0, i - 4)
        for kt in range(k0, i + 1):
            sb = score_bank()
            lowm = (kt == i - 4)
            P.pe.matmul(out=sb[:, :], lhsT=KwT[:, kt * 128:(kt + 1) * 128], rhs=QT[:], start=True, stop=not (lowm or kt == i))
            if lowm:
                P.pe.matmul(out=sb[:, :], lhsT=identb[:], rhs=WLOW[:], start=False, stop=True)
            if kt == i:
                P.pe.matmul(out=sb[:, :], lhsT=identb[:], rhs=DIAG[:], start=False, stop=True)
            p_ = pt_buf()
            P.act.activation(out=p_[:], in_=sb[:, :], func=AF.Exp, scale=0.125)
            P.pe.matmul(out=pf[5][0:65, :], lhsT=Vw[:, kt, :], rhs=p_[:], start=(kt == k0), stop=(kt == i))
        for bi, (pbank, oT_) in enumerate(((pf[4], oTs), (pf[5], oTw))):
            P.act.activation(out=oT_[:], in_=pbank[0:65, :], func=AF.Copy)
            for r in range(4):
                P.pe.transpose(out=pf[1][:, r * 65:(r + 1) * 65], in_=oT_[:, r * 128:(r + 1) * 128], identity=identf[0:65, 0:65])
            P.dve.tensor_copy(out=otok[bi][:], in_=pf[1][:, 0:260])
            P.dve.tensor_scalar(out=fac[bi][:], in0=otok[bi][:, :, 64], scalar1=1e-30, scalar2=None, op0=ALU.max)
            P.dve.reciprocal(out=fac[bi][:], in_=fac[bi][:])
            P.dve.tensor_tensor(out=fac[bi][:], in0=fac[bi][:], in1=gt[:, 4 + 4 * bi:8 + 4 * bi], op=ALU.mult)
        bc = lambda t, off: AP(t, off, [[t.shape[1], 128], [1, 4], [0, 64]])
        P.dve.tensor_tensor(out=at[:], in0=oc[:], in1=bc(gt, 0), op=ALU.mult)
        P.dve.tensor_tensor(out=at2[:], in0=otok[0][:, :, 0:64], in1=bc(fac[0], 0), op=ALU.mult)
        P.dve.tensor_tensor(out=at[:], in0=at[:], in1=at2[:], op=ALU.add)
        P.dve.tensor_tensor(out=at2[:], in0=otok[1][:, :, 0:64], in1=bc(fac[1], 0), op=ALU.mult)
        P.dve.tensor_tensor(out=at[:], in0=at[:], in1=at2[:], op=ALU.add)
        if scrA is None:
            P.pool.dma_start(out=attn[i * 128:(i + 1) * 128, :], in_=at[:])
        else:
            for h in range(2):
                P.pe.transpose(out=pf[1][:, h * 128:(h + 1) * 128], in_=AP(at, h * 128, [[256, 128], [1, 128]]), identity=identf[:])
            P.dve.tensor_copy(out=atT[:], in_=pf[1][:, 0:256])
            for h in range(2):
                P.pool.dma_start(out=scrA[i // 8][h * 128:(h + 1) * 128, (i % 8) * 128:(i % 8 + 1) * 128], in_=atT[:, h, :])


def consts_A(S):
    q = np.arange(128)
    fl = np.floor((q - 31) / 16.0).astype(np.int64)
    m = np.arange(1024)
    mw = (m[None, :] <= 512 + fl[:, None]).astype(np.float32)
    m2 = np.arange(256)
    h = (q >= 64).astype(np.int64)
    cmw = (m2[None, :] < 128 + h[:, None] - 1).astype(np.float32)
    fbw = np.zeros((128, 256), np.float32)
    fbw[(m2[None, :] == 128 + h[:, None])] = 200.0
    fbw[(m2[None, :] == 128 + h[:, None] - 1)] = 100.0
    fbw[m2[None, :] > 128 + h[:, None]] = -1.0
    j = np.arange(128)
    k = np.arange(S)
    exn = np.where((k[None, :] // 64) == j[:, None], NEG, 0.0).astype(np.float32)
    kl = np.arange(128)
    diag = np.where(kl[:, None] > q[None, :], NEG, 0.0).astype(np.float32)
    wlow = np.where(kl[:, None] <= q[None, :], NEG, 0.0).astype(np.float32)
    return dict(identf=np.eye(128, dtype=np.float32), mw=mw, cmw=cmw, fbw=fbw, exn=exn,
                diagc=np.tile(diag, (1, 4)), wlow=np.tile(wlow, (1, 4)))


def inputs_A(inp, b, g, S):
    w_in = inp["w_in"]
    cols = list(range(g * 256, g * 256 + 256))
    base = 1024
    off = lambda j: base + j * 256 + g * 64
    for j in (0, 2, 4, 1, 3, 5):
        cols += list(range(off(j), off(j) + 64))
    gb = 1024 + 6 * 256
    cols += [gb + br * 16 + g * 4 + r for br in range(3) for r in range(4)]
    d = dict(
        xa=np.ascontiguousarray(inp["x"][b, :S]),
        pos=np.ascontiguousarray(inp["positions"][b, :S].reshape(S // 128, 128)).astype(np.int32),
        wa=np.ascontiguousarray(w_in[:, cols]),
        gmix=np.ascontiguousarray(inp["norm_mix_g"].reshape(8, 128).T),
        kw1=np.ascontiguousarray(inp["cmp_k_w1"].reshape(32, 64, 256).transpose(1, 0, 2)),
        kw2=np.ascontiguousarray(inp["cmp_k_w2"].reshape(2, 128, 64).transpose(1, 0, 2)),
        kpos=np.ascontiguousarray(inp["cmp_k_pos"].T),
        vw1=np.ascontiguousarray(inp["cmp_v_w1"].reshape(32, 64, 256).transpose(1, 0, 2)),
        vw2=np.ascontiguousarray(inp["cmp_v_w2"].reshape(2, 128, 64).transpose(1, 0, 2)),
        vpos=np.ascontiguousarray(inp["cmp_v_pos"].T),
    )
    d.update(consts_A(S))
    return d
import numpy as np

TB = 512
HALO = 32


def build_B(T):
    nc = bass.Bass("TRN2", target_bir_lowering=False)
    P = Prog(nc)
    record_B(P, T, None, None)
    P.emit()
    return nc


def record_B(P, T, scrA, S):
    NP = T // TB
    TT = TB + HALO
    D = lambda n, sh, dt=F32, kind="ExternalInput": P.dram(n, sh, dt, kind)
    xTd = D("xT", [1024, HALO + T])
    if scrA is None:
        aTd = D("aT", [1024, T])
    else:
        scrG = [P.dram(f"scrG{j}", [1024, 1024], F32, "Internal", track=True) for j in range(S // 1024)]
        qmd = D("qmask", [128, 4])
    wb = D("wb", [1024, 4096])
    wco = D("wco", [1024, 1024]); wo = D("wo", [1024, 1024])
    wg = D("wg", [1024, 2816]); wu = D("wu", [1024, 2816]); wd = D("wd", [2816, 1024])
    vecd = D("vecs", [128, 8, 8])
    cwd = D("cw", [128, 8, 31])
    yTd = D("yT", [1024, T], F32, "ExternalOutput")

    vec = P.sbuf("vec", [128, 8, 8], F32)
    cw = P.sbuf("cw_s", [128, 8, 31], F32)
    onesb = P.sbuf("onesb", [128, 128], BF16)
    xT = P.sbuf("xT_s", [128, 8, TT], F32, blk=TT)
    aT = P.sbuf("aT_s", [128, 8, TB], F32, blk=TB)
    sqb = P.sbuf("sqb", [128, 8, TT], BF16, blk=TT)
    rrep = P.sbuf("rrep", [128, TT], F32)
    hT = P.sbuf("hT_s", [128, 8, TT], BF16, blk=TT)
    zb = [P.sbuf(f"zb{i}", [128, TT], F32) for i in range(2)]
    eb = [P.sbuf(f"eb{i}", [128, TT], F32) for i in range(2)]
    zc = P.sbuf("zc", [128, 8, TB], F32, blk=TB)
    mu = P.sbuf("mu", [128, TB], F32)
    lrs = P.sbuf("lrs", [128, TB], F32)
    sT = P.sbuf("sT", [128, 8, TB], BF16, blk=TB)
    t1 = [P.sbuf(f"t1_{i}", [128, TB], F32) for i in range(2)]
    t2 = [P.sbuf(f"t2_{i}", [128, TB], F32) for i in range(2)]
    t3 = [P.sbuf(f"t3_{i}", [128, TB], F32) for i in range(2)]
    mg = P.sbuf("mg", [128, 8, TB], BF16, blk=TB)
    actT = P.sbuf("actT", [128, 22, TB], BF16, blk=TB)
    yb = [P.sbuf(f"yb{i}", [128, TB], F32) for i in range(2)]
    stgs = [P.sbuf(f"stw{i}", [128, 22, 128], F32) for i in range(2)]
    wbufs = [P.sbuf(f"wbf{i}", [128, 22, 128], BF16) for i in range(3)]
    pf = [P.psum(f"pf{i}", [128, 512], F32) for i in range(8)]
    if scrA is not None:
        cand = [P.sbuf("cand0", [128, 8, TB], F32)]
        qm = P.sbuf("qm", [128, 4], F32)
        P.sp.dma_start(out=qm[:], in_=qmd.ap())
        for j in range(S // 1024):
            P.pool.collective_compute("AllGather", ALU.bypass, replica_groups=[[0, 1, 2, 3], [4, 5, 6, 7]],
                                      ins=[scrA[j].ap()], outs=[scrG[j].ap()])

    P.sp.dma_start(out=vec[:], in_=vecd.ap())
    P.sp.dma_start(out=cw[:], in_=cwd.ap())
    P.pool.memset(ap=onesb[:], constant=1.0)
    GMIX, GFFN, GFIN, CONVB, LNG, LNB, BCO = range(7)
    V = lambda k, c: vec[:, k, c:c + 1]

    cnt = {"w": 0, "pb": 0}

    def load_w(Wd, col0, KC):
        n = cnt["w"]; cnt["w"] += 1
        st = stgs[n % 2]; wbf = wbufs[n % 3]
        P.sp.dma_start(out=st[:, 0:KC, :], in_=Wd[:, col0:col0 + 128].rearrange("(c p) n -> p c n", p=128))
        if n % 2 == 0:
            P.pool.tensor_copy(out=wbf[:, 0:KC, :], in_=st[:, 0:KC, :])
        else:
            P.act.activation(out=wbf[:, 0:KC, :], in_=st[:, 0:KC, :], func=AF.Copy)
        return wbf

    def bank():
        cnt["pb"] += 1
        return pf[2 + cnt["pb"] % 6]

    def stats(src_fn, N, rep, sq_scale=1.0 / 1024.0, eps=1e-6, bankidx=0):
        segs = [(0, min(N, 512))] + ([(512, N)] if N > 512 else [])
        for c in range(8):
            P.act.activation(out=sqb[:, c, 0:N], in_=src_fn(c), func=AF.Square)
        for si, (a, b_) in enumerate(segs):
            pb_ = pf[bankidx + si]
            for c in range(8):
                P.pe.matmul(out=pb_[:, 0:b_ - a], lhsT=onesb[:], rhs=sqb[:, c, a:b_], start=(c == 0), stop=(c == 7))
            P.dve.tensor_scalar(out=rep[:, a:b_], in0=pb_[:, 0:b_ - a], scalar1=sq_scale, scalar2=eps, op0=ALU.mult, op1=ALU.add)
        P.act.activation(out=rep[:, 0:N], in_=rep[:, 0:N], func=AF.Sqrt)
        P.dve.reciprocal(out=rep[:, 0:N], in_=rep[:, 0:N])

    def sigmoid_from(dst, src, n):
        P.act.activation(out=dst[:, 0:n], in_=src, func=AF.Exp, scale=-1.0)
        P.dve.tensor_scalar(out=dst[:, 0:n], in0=dst[:, 0:n], scalar1=1.0, scalar2=None, op0=ALU.add)
        P.dve.reciprocal(out=dst[:, 0:n], in_=dst[:, 0:n])

    for p in range(NP):
        t0 = p * TB
        P.sp.dma_start(out=xT[:], in_=xTd[:, t0:t0 + TT].rearrange("(c p) n -> p c n", p=128))
        if scrA is None:
            P.sp.dma_start(out=aT[:], in_=aTd[:, t0:t0 + TB].rearrange("(c p) n -> p c n", p=128))
        stats(lambda c: xT[:, c, :], TT, rrep)
        for c in range(8):
            P.dve.scalar_tensor_tensor(out=hT[:, c, :], in0=xT[:, c, :], scalar=V(GMIX, c), in1=rrep[:], op0=ALU.mult, op1=ALU.mult)
        for c in range(8):
            wa_ = load_w(wb, c * 128, 8)
            wb_ = load_w(wb, 1024 + c * 128, 8)
            z = zb[c % 2]; e = eb[c % 2]
            pa, pa2, pbk, pb2 = bank(), bank(), bank(), bank()
            for (w_, pm, ph) in ((wa_, pa, pa2), (wb_, pbk, pb2)):
                for k in range(8):
                    P.pe.matmul(out=pm[:, :], lhsT=w_[:, k, :], rhs=hT[:, k, HALO:TT], start=(k == 0), stop=(k == 7))
                for k in range(8):
                    P.pe.matmul(out=ph[:, 0:HALO], lhsT=w_[:, k, :], rhs=hT[:, k, 0:HALO], start=(k == 0), stop=(k == 7))
            P.act.activation(out=e[:, HALO:TT], in_=pbk[:, :], func=AF.Exp, scale=-1.0)
            P.act.activation(out=e[:, 0:HALO], in_=pb2[:, 0:HALO], func=AF.Exp, scale=-1.0)
            P.dve.tensor_scalar(out=e[:], in0=e[:], scalar1=1.0, scalar2=None, op0=ALU.add)
            P.dve.reciprocal(out=e[:], in_=e[:])
            P.dve.tensor_tensor(out=z[:, HALO:TT], in0=e[:, HALO:TT], in1=pa[:, :], op=ALU.mult)
            P.dve.tensor_tensor(out=z[:, 0:HALO], in0=e[:, 0:HALO], in1=pa2[:, 0:HALO], op=ALU.mult)
            eng = P.dve
            eng.tensor_scalar(out=zc[:, c, :], in0=z[:, 2:2 + TB], scalar1=cw[:, c, 0:1], scalar2=V(CONVB, c), op0=ALU.mult, op1=ALU.add)
            for w in range(1, 31):
                eng.scalar_tensor_tensor(out=zc[:, c, :], in0=z[:, 2 + w:2 + w + TB], scalar=cw[:, c, w:w + 1], in1=zc[:, c, :],
                                         op0=ALU.mult, op1=ALU.add)
        for c in range(8):
            P.act.activation(out=sqb[:, c, 0:TB], in_=zc[:, c, :], func=AF.Copy)
        for c in range(8):
            P.pe.matmul(out=pf[0][:, :], lhsT=onesb[:], rhs=sqb[:, c, 0:TB], start=(c == 0), stop=(c == 7))
        P.dve.tensor_scalar(out=mu[:], in0=pf[0][:, :], scalar1=1.0 / 1024.0, scalar2=None, op0=ALU.mult)
        for c in range(8):
            P.dve.tensor_tensor(out=zc[:, c, :], in0=zc[:, c, :], in1=mu[:], op=ALU.subtract)
        stats(lambda c: zc[:, c, :], TB, lrs)
        for c in range(8):
            a = t1[c % 2]; b_ = t2[c % 2]
            P.dve.tensor_tensor(out=a[:], in0=zc[:, c, :], in1=lrs[:], op=ALU.mult)
            P.dve.tensor_scalar(out=a[:], in0=a[:], scalar1=V(LNG, c), scalar2=V(LNB, c), op0=ALU.mult, op1=ALU.add)
            sigmoid_from(b_, a[:], TB)
            P.pool.tensor_tensor(out=sT[:, c, :], in0=a[:], in1=b_[:], op=ALU.mult)
        if scrA is not None:
            for q in range(4):
                cd = cand[0]
                tg = q * T + t0
                P.sp.dma_start(out=cd[:], in_=scrG[tg // 1024][:, tg % 1024:tg % 1024 + TB].rearrange("(c p) n -> p c n", p=128))
                if q == 0:
                    P.dve.tensor_scalar(out=aT[:], in0=cd[:], scalar1=qm[:, 0:1], scalar2=None, op0=ALU.mult)
                else:
                    P.dve.scalar_tensor_tensor(out=aT[:], in0=cd[:], scalar=qm[:, q:q + 1], in1=aT[:], op0=ALU.mult, op1=ALU.add)
        for oc in range(8):
            wga_ = load_w(wb, 2048 + oc * 128, 8)
            wgc_ = load_w(wb, 3072 + oc * 128, 8)
            wco_ = load_w(wco, oc * 128, 8)
            pga, pgc, pco = bank(), bank(), bank()
            for k in range(8):
                P.pe.matmul(out=pga[:, :], lhsT=wga_[:, k, :], rhs=hT[:, k, HALO:TT], start=(k == 0), stop=(k == 7))
            for k in range(8):
                P.pe.matmul(out=pgc[:, :], lhsT=wgc_[:, k, :], rhs=hT[:, k, HALO:TT], start=(k == 0), stop=(k == 7))
            for k in range(8):
                P.pe.matmul(out=pco[:, :], lhsT=wco_[:, k, :], rhs=sT[:, k, :], start=(k == 0), stop=(k == 7))
            a = t1[oc % 2]; b_ = t2[oc % 2]; c_ = t3[oc % 2]
            sigmoid_from(a, pga[:, :], TB)
            sigmoid_from(b_, pgc[:, :], TB)
            P.dve.tensor_tensor(out=a[:], in0=a[:], in1=aT[:, oc, :], op=ALU.mult)
            P.dve.scalar_tensor_tensor(out=c_[:], in0=pco[:, :], scalar=V(BCO, oc), in1=b_[:], op0=ALU.add, op1=ALU.mult)
            P.pool.tensor_tensor(out=mg[:, oc, :], in0=a[:], in1=c_[:], op=ALU.add)
        for oc in range(8):
            wo_ = load_w(wo, oc * 128, 8)
            po = bank()
            for k in range(8):
                P.pe.matmul(out=po[:, :], lhsT=wo_[:, k, :], rhs=mg[:, k, :], start=(k == 0), stop=(k == 7))
            P.dve.tensor_tensor(out=xT[:, oc, HALO:TT], in0=xT[:, oc, HALO:TT], in1=po[:, :], op=ALU.add)
        stats(lambda c: xT[:, c, HALO:TT], TB, rrep)
        for c in range(8):
            P.dve.scalar_tensor_tensor(out=hT[:, c, 0:TB], in0=xT[:, c, HALO:TT], scalar=V(GFFN, c), in1=rrep[:, 0:TB], op0=ALU.mult, op1=ALU.mult)
        for f in range(22):
            wg_ = load_w(wg, f * 128, 8)
            wu_ = load_w(wu, f * 128, 8)
            pg, pu = bank(), bank()
            for k in range(8):
                P.pe.matmul(out=pg[:, :], lhsT=wg_[:, k, :], rhs=hT[:, k, 0:TB], start=(k == 0), stop=(k == 7))
            for k in range(8):
                P.pe.matmul(out=pu[:, :], lhsT=wu_[:, k, :], rhs=hT[:, k, 0:TB], start=(k == 0), stop=(k == 7))
            a = t1[f % 2]; b_ = t2[f % 2]
            sigmoid_from(a, pg[:, :], TB)
            P.dve.tensor_tensor(out=b_[:], in0=a[:], in1=pg[:, :], op=ALU.mult)
            P.dve.tensor_tensor(out=actT[:, f, :], in0=b_[:], in1=pu[:, :], op=ALU.mult)
        for oc in range(8):
            wd_ = load_w(wd, oc * 128, 22)
            pd_ = bank()
            for k in range(22):
                P.pe.matmul(out=pd_[:, :], lhsT=wd_[:, k, :], rhs=actT[:, k, :], start=(k == 0), stop=(k == 21))
            P.dve.tensor_tensor(out=xT[:, oc, HALO:TT], in0=xT[:, oc, HALO:TT], in1=pd_[:, :], op=ALU.add)
        stats(lambda c: xT[:, c, HALO:TT], TB, rrep)
        for c in range(8):
            y = yb[c % 2]
            P.dve.scalar_tensor_tensor(out=y[:], in0=xT[:, c, HALO:TT], scalar=V(GFIN, c), in1=rrep[:, 0:TB], op0=ALU.mult, op1=ALU.mult)
            P.pool.dma_start(out=yTd[c * 128:(c + 1) * 128, t0:t0 + TB], in_=y[:])


def inputs_B(inp, attn_b, b, t0, T):
    x = inp["x"][b]
    xh = np.zeros((HALO + T, 1024), np.float32)
    lo = t0 - HALO
    if lo >= 0:
        xh[:] = x[lo:t0 + T]
    else:
        xh[-lo:] = x[0:t0 + T]
    w_in = inp["w_in"]
    lay = lambda v: np.ascontiguousarray(v.reshape(8, 128).T)
    vecs = np.zeros((128, 8, 8), np.float32)
    for k, name in enumerate(["norm_mix_g", "norm_ffn_g", "norm_final_g", "conv_b", "conv_norm_g", "conv_norm_b", "b_conv_out"]):
        vecs[:, k, :] = lay(inp[name])
    cw = np.ascontiguousarray(inp["conv_w"][:, 0, :].reshape(31, 8, 128).transpose(2, 1, 0))
    return dict(
        xT=np.ascontiguousarray(xh.T), **({} if attn_b is None else {"aT": np.ascontiguousarray(attn_b[t0:t0 + T].T)}),
        wb=np.ascontiguousarray(w_in[:, 2608:6704]), wco=inp["w_conv_out"], wo=inp["w_out"],
        wg=inp["w_ffn_gate"], wu=inp["w_ffn_up"], wd=inp["w_ffn_down"], vecs=vecs, cw=cw)
from concourse.bass_utils import run_bass_kernel_spmd

S_FULL = 8192
T_CORE = 2048


def build_fused(S, T):
    nc = bass.Bass("TRN2", target_bir_lowering=False)
    scrA = [nc.dram_tensor(f"scrA{j}", [256, 1024], F32) for j in range(S // 1024)]
    P1 = Prog(nc, "A_")
    record_A(P1, S, scrA)
    P1.emit()
    P2 = Prog(nc, "B_")
    record_B(P2, T, scrA, S)
    P2.emit()
    return nc


def fused_inputs(inp, c, S, T):
    b, r = c // 4, c % 4
    d = inputs_A(inp, b, r, S)
    db = inputs_B(inp, None, b, r * T, T)
    d.update(db)
    qm = np.zeros((128, 4), np.float32)
    qm[:, r] = 1.0
    d["qmask"] = qm
    return d


def kernel(**inputs):
    inp = {k: np.asarray(v) for k, v in inputs.items()}
    B = inp["x"].shape[0]
    nc = build_fused(S_FULL, T_CORE)
    maps = [fused_inputs(inp, c, S_FULL, T_CORE) for c in range(8)]
    res = run_bass_kernel_spmd(nc, maps, core_ids=list(range(8)))
    out = np.zeros((B, S_FULL, 1024), np.float32)
    for c in range(8):
        out[c // 4, (c % 4) * T_CORE:(c % 4 + 1) * T_CORE, :] = res.results[c]["yT"].T
    return out
```

```python
import numpy as np
from contextlib import ExitStack
import concourse.bass as bass
import concourse.mybir as mybir

F32 = mybir.dt.float32
BF16 = mybir.dt.bfloat16
I32 = mybir.dt.int32
AF = mybir.ActivationFunctionType
ALU = mybir.AluOpType
AX = mybir.AxisListType

ENGS = ("pe", "act", "dve", "pool", "sp")
WRITE_KW = ("out", "accum_out", "out_max", "out_indices", "ap")
DT_SIZE = {F32: 4, BF16: 2, I32: 4}


def _dtsize(dt):
    try:
        return DT_SIZE[dt]
    except Exception:
        return mybir.dt.size(dt)


class Op:
    __slots__ = ("eng", "fn", "deps", "is_dma", "needed", "sig", "gidx", "pre_wait", "cc", "tag")

    def __init__(self, eng, fn, is_dma, gidx):
        self.eng = eng
        self.fn = fn
        self.deps = []
        self.is_dma = is_dma
        self.needed = False
        self.sig = None
        self.gidx = gidx
        self.pre_wait = None
        self.cc = False
        self.tag = None


class _EngProxy:
    def __init__(self, prog, eng):
        self._p = prog
        self._e = eng

    def __getattr__(self, name):
        p, e = self._p, self._e

        def call(*args, **kw):
            extra_r = kw.pop("_r", ())
            extra_w = kw.pop("_w", ())
            return p._record(e, name, args, kw, extra_r, extra_w)

        return call


class Prog:
    RING = 12

    def __init__(self, nc, prefix="", sem_stack=None):
        self.nc = nc
        self.prefix = prefix
        self.tag = None
        self.sem_stack = sem_stack
        self.ops = {e: [] for e in ENGS}
        self.n = 0
        self.blk = {}
        self.untracked = set()
        self.last_w = {}
        self.readers = {}
        self.pe = _EngProxy(self, "pe")
        self.act = _EngProxy(self, "act")
        self.dve = _EngProxy(self, "dve")
        self.pool = _EngProxy(self, "pool")
        self.sp = _EngProxy(self, "sp")
        self.stack = ExitStack()
        self.dma_count = {e: 0 for e in ENGS}

    def sbuf(self, name, shape, dtype, blk=None):
        name = self.prefix + name
        t = self.stack.enter_context(self.nc.sbuf_tensor(name, list(shape), dtype))
        self.blk[name] = None if blk is None else blk * _dtsize(dtype)
        return t

    def psum(self, name, shape, dtype, blk=None):
        name = self.prefix + name
        t = self.stack.enter_context(self.nc.psum_tensor(name, list(shape), dtype))
        self.blk[name] = None if blk is None else blk * _dtsize(dtype)
        return t

    def dram(self, name, shape, dtype, kind, track=False, blk=None):
        if kind == "Internal":
            t = self.nc.dram_tensor(name, list(shape), dtype)
        else:
            t = self.nc.dram_tensor(name, list(shape), dtype, kind=kind)
        if not track:
            self.untracked.add(name)
        else:
            self.blk[name] = None if blk is None else blk * _dtsize(dtype)
        return t

    def keys(self, ap):
        t = ap.tensor
        name = t.name
        if name in self.untracked:
            return []
        b = self.blk.get(name, None)
        if b is None:
            return [name]
        es = _dtsize(ap.dtype)
        space = str(type(t).__name__)
        pairs = list(ap.ap)
        if "DRam" in space:
            lo = ap.offset
            hi = lo + sum((c - 1) * abs(s) for s, c in pairs)
        else:
            shp = list(t.shape)
            F = 1
            for d in shp[1:]:
                F *= d
            F = F * _dtsize(t.dtype) // es
            lo = ap.offset % F
            hi = lo + sum((c - 1) * abs(s) for s, c in pairs[1:])
        lo_b, hi_b = lo * es, hi * es + es - 1
        return [(name, i) for i in range(lo_b // b, hi_b // b + 1)]

    def _record(self, eng, name, args, kw, extra_r, extra_w):
        is_dma = name in ("dma_start", "collective_compute", "dma_start_transpose")
        rk, wk = [], []
        if name == "collective_compute":
            for a in kw.get("ins", []):
                rk += self.keys(a)
            for a in kw.get("outs", []):
                wk += self.keys(a)
        else:
            for k, v in kw.items():
                if isinstance(v, bass.AP):
                    if "PSum" in type(v.tensor).__name__:
                        wk.extend(self.keys(v))
                    else:
                        (wk if k in WRITE_KW else rk).extend(self.keys(v))
            for a in args:
                if isinstance(a, bass.AP):
                    raise ValueError("pass APs as kwargs")
        for a in extra_r:
            rk += self.keys(a) if isinstance(a, bass.AP) else [a]
        for a in extra_w:
            wk += self.keys(a) if isinstance(a, bass.AP) else [a]

        def fn(engobj, name=name, args=args, kw=kw):
            return getattr(engobj, name)(*args, **kw)

        op = self.raw(eng, fn, rk, wk, is_dma)
        op.cc = (name == "collective_compute")
        return op

    def raw(self, eng, fn, rk, wk, is_dma=False):
        op = Op(eng, fn, is_dma, self.n)
        op.tag = self.tag
        self.n += 1
        deps = {}
        for k in rk:
            lw = self.last_w.get(k)
            if lw is not None:
                deps[lw.gidx] = lw
        for k in wk:
            lw = self.last_w.get(k)
            if lw is not None:
                deps[lw.gidx] = lw
            rd = self.readers.get(k)
            if rd:
                for e, o in rd.items():
                    if e == "dma":
                        for oo in o:
                            deps[oo.gidx] = oo
                    else:
                        deps[o.gidx] = o
        for k in wk:
            self.last_w[k] = op
            self.readers[k] = {}
        for k in rk:
            if k in wk:
                continue
            rd = self.readers.setdefault(k, {})
            if is_dma:
                rd.setdefault("dma", []).append(op)
            else:
                rd[eng] = op
        for d in deps.values():
            if d.eng == eng and eng == "pe" and not d.is_dma and not is_dma:
                continue
            op.deps.append(d)
            d.needed = True
        if is_dma:
            op.needed = True
        self.ops[eng].append(op)
        return op

    def emit(self, final_wait_engs=("sp", "pool", "act")):
        nc = self.nc
        allsems = []

        def newsem(name):
            h = nc.alloc_semaphore(name=name)
            allsems.append(h)
            return h

        esem = {e: newsem(self.prefix + "es_" + e) for e in ENGS}
        ccsem = newsem(self.prefix + "ccsem")
        ncc = 0
        rings = {}
        for e in ENGS:
            if any(o.is_dma for o in self.ops[e]):
                rings[e] = [newsem(f"{self.prefix}dr_{e}_{i}") for i in range(self.RING)]
        for e in ENGS:
            c = 0
            nd = 0
            for o in self.ops[e]:
                if o.cc:
                    ncc += 1
                    o.sig = (ccsem, ncc)
                elif o.is_dma:
                    sem = rings[e][nd % self.RING]
                    o.sig = (sem, 16 * (nd // self.RING + 1))
                    if nd >= self.RING:
                        o.pre_wait = (sem, 16 * (nd // self.RING))
                    nd += 1
                elif o.needed:
                    c += 1
                    o.sig = (esem[e], c)
            self.dma_count[e] = nd
        prog = self

        def run_engine(e, engobj):
            waited = {}

            def wait(sem, val):
                key = id(sem)
                if waited.get(key, 0) >= val:
                    return
                engobj.wait_ge(sem, val)
                waited[key] = val

            semobj = {}
            for o in prog.ops[e]:
                for d in o.deps:
                    sem, val = d.sig
                    semobj[id(sem)] = sem
                    wait(sem, val)
                if o.pre_wait is not None:
                    wait(*o.pre_wait)
                ins = o.fn(engobj)
                if o.tag is not None:
                    ins.annotate(o.tag)
                if o.sig is not None:
                    sem, val = o.sig
                    if o.cc:
                        ins.then_inc(sem, 1)
                    elif o.is_dma:
                        ins.then_inc(sem, 16)
                    else:
                        ins.then_inc(sem, 1)
            if e in rings:
                nd = prog.dma_count[e]
                for i, sem in enumerate(rings[e]):
                    cnt = (nd - i + prog.RING - 1) // prog.RING if nd > i else 0
                    if cnt > 0:
                        wait(sem, 16 * cnt)

        with nc.Block() as block:
            if self.ops["sp"]:
                block.sync(lambda eng: run_engine("sp", eng))
            if self.ops["act"]:
                block.scalar(lambda eng: run_engine("act", eng))
            if self.ops["dve"]:
                block.vector(lambda eng: run_engine("dve", eng))
            if self.ops["pool"]:
                block.gpsimd(lambda eng: run_engine("pool", eng))
            if self.ops["pe"]:
                block.tensor(lambda eng: run_engine("pe", eng))
        nc.clear_and_free_semaphores(allsems)
        nc.all_engine_barrier()
        self.stack.close()
import math
import numpy as np

NEG = -240000.0
INV_FREQ = [float(np.float32(500000.0) ** np.float32(-(2 * i) / 16.0)) for i in range(8)]
MAGIC = 12582912.0
TWO_PI = 2.0 * math.pi


def AP(t, off, pairs):
    return bass.AP(t, off, [list(p) for p in pairs])


def build_A(S):
    nc = bass.Bass("TRN2", target_bir_lowering=False)
    P = Prog(nc)
    record_A(P, S, None)
    P.emit()
    return nc


def record_A(P, S, scrA):
    NT = S // 128
    NCMAX = S // 16 - 1
    NCT = (NCMAX + 127) // 128
    D = lambda n, sh, dt=F32, kind="ExternalInput": P.dram(n, sh, dt, kind)
    xa = D("xa", [S, 1024])
    pos = D("pos", [NT, 128], I32)
    wa = D("wa", [1024, 652])
    gmix = D("gmix", [128, 8])
    kw1 = D("kw1", [64, 32, 256]); kw2 = D("kw2", [128, 2, 64]); kpos = D("kpos", [64, 32])
    vw1 = D("vw1", [64, 32, 256]); vw2 = D("vw2", [128, 2, 64]); vpos = D("vpos", [64, 32])
    identd = D("identf", [128, 128])
    mwd = D("mw", [128, 1024]); cmwd = D("cmw", [128, 256]); fbwd = D("fbw", [128, 256])
    exnd = D("exn", [128, S]); diagd = D("diagc", [128, 512]); wlowd = D("wlow", [128, 512])
    attn = D("attn", [S, 256], F32, "ExternalOutput") if scrA is None else None

    identf = P.sbuf("identf_s", [128, 128], F32)
    identb = P.sbuf("identb", [128, 128], BF16)
    stg = P.sbuf("stg", [128, 8 * 652], F32)
    WA = P.sbuf("WA", [128, 8, 652], BF16)
    gm = P.sbuf("gm", [128, 8], F32)
    w1k = P.sbuf("w1k", [64, 32, 256], BF16); w1v = P.sbuf("w1v", [64, 32, 256], BF16)
    w2k = P.sbuf("w2k", [128, 2, 64], BF16); w2v = P.sbuf("w2v", [128, 2, 64], BF16)
    posk = P.sbuf("posk", [64, 32], BF16); posv = P.sbuf("posv", [64, 32], BF16)
    b1k = P.sbuf("b1k", [128, 2], F32); b1v = P.sbuf("b1v", [128, 2], F32)
    MW = P.sbuf("MW", [128, 1024], F32); CMW = P.sbuf("CMW", [128, 256], F32); FBW = P.sbuf("FBW", [128, 256], F32)
    EXN = P.sbuf("EXN", [128, S], BF16, blk=128)
    DIAG = P.sbuf("DIAG", [128, 512], BF16); WLOW = P.sbuf("WLOW", [128, 512], BF16)
    posi = P.sbuf("posi", [64, 128], I32); posf = P.sbuf("posf", [64, 128], F32)
    posT = P.sbuf("posT", [128, NT], F32)
    ANG = P.sbuf("ANG", [128, NT, 16], F32); RR = P.sbuf("RR", [128, NT, 16], F32); SC = P.sbuf("SC", [128, NT, 16], F32)
    KsT = P.sbuf("KsT", [64, S], BF16, blk=128); KwT = P.sbuf("KwT", [64, S], BF16, blk=128)
    Vs = P.sbuf("Vs", [128, NT, 65], BF16, blk=65); Vw = P.sbuf("Vw", [128, NT, 65], BF16, blk=65)
    KcR = P.sbuf("KcR", [64, 144], BF16); VcR = P.sbuf("VcR", [64, 144], BF16)
    KcT = P.sbuf("KcT", [64, NCT * 128], BF16); VcT = P.sbuf("VcT", [64, NCT * 128], BF16)
    Vc = P.sbuf("Vc", [128, NCT, 64], BF16)
    xs = [P.sbuf(f"xs{i}", [128, 1024], F32) for i in range(2)]
    junk = P.sbuf("junk", [128, 1024], BF16)
    ssq = P.sbuf("ssq", [128, 1], F32); rstd = P.sbuf("rstd", [128, 1], F32)
    hb = P.sbuf("hb", [128, 1024], BF16)
    hT = P.sbuf("hT", [128, 8, 128], BF16)
    pr = P.sbuf("pr", [128, 652], F32); prb = P.sbuf("prb", [128, 652], BF16)
    rt = [P.sbuf(f"rt{i}", [128, 7, 8], F32) for i in range(4)]
    gt = P.sbuf("gt", [128, 12], F32)
    QT = P.sbuf("QT", [64, 512], BF16)
    hid = P.sbuf("hid", [128, 2, 8], F32); hide = P.sbuf("hide", [128, 2, 8], F32); hsb = P.sbuf("hsb", [128, 2, 8], BF16)
    pc = P.sbuf("pc", [128, 4, 512], F32, blk=512); pcb = P.sbuf("pcb", [128, 4, 512], BF16, blk=512)
    den = P.sbuf("den", [128, 4], F32); rden = P.sbuf("rden", [128, 4], F32)
    PP = P.sbuf("PP", [128, 520], F32)
    imp = P.sbuf("imp", [128, 128], F32); score = P.sbuf("score", [128, 128], F32); wk = P.sbuf("wk", [128, 128], F32)
    m8 = P.sbuf("m8", [128, 16], F32)
    nsb = P.sbuf("nsb", [128, 128], BF16); nsT = P.sbuf("nsT", [128, 4, 128], BF16)
    pTc = P.sbuf("pTc", [128, NCT, 128], BF16)
    pT = [P.sbuf(f"pT{i}", [128, 512], BF16) for i in range(3)]
    oc = P.sbuf("oc", [128, 4, 64], F32)
    oTs = P.sbuf("oTs", [65, 512], F32); oTw = P.sbuf("oTw", [65, 512], F32)
    otok = [P.sbuf(f"otok{i}", [128, 4, 65], F32) for i in range(2)]
    fac = [P.sbuf(f"fac{i}", [128, 4], F32) for i in range(2)]
    at = P.sbuf("at", [128, 4, 64], F32); at2 = P.sbuf("at2", [128, 4, 64], F32)
    atT = P.sbuf("atT", [128, 2, 128], F32)
    pb = [P.psum(f"pb{i}", [128, 1024], BF16) for i in range(2)]
    pf = [P.psum(f"pf{i}", [128, 512], F32) for i in range(6)]

    P.sp.dma_start(out=identf[:], in_=identd.ap())
    P.dve.tensor_copy(out=identb[:], in_=identf[:])
    P.sp.dma_start(out=gm[:], in_=gmix.ap())
    stgw = AP(stg, 0, [[8 * 652, 128], [652, 8], [1, 652]])
    P.sp.dma_start(out=stgw, in_=wa.ap().rearrange("(c p) n -> p c n", p=128))
    for c in range(8):
        P.dve.tensor_scalar(out=WA[:, c, :], in0=AP(stg, c * 652, [[8 * 652, 128], [1, 652]]),
                            scalar1=gm[:, c:c + 1], scalar2=None, op0=ALU.mult)
    for (w1d, w1s, w2d, w2s, pd, ps_, b1) in ((kw1, w1k, kw2, w2k, kpos, posk, b1k), (vw1, w1v, vw2, w2v, vpos, posv, b1v)):
        for q4 in range(2):
            sv = AP(stg, 0, [[8 * 652, 64], [256, 16], [1, 256]])
            P.sp.dma_start(out=sv, in_=w1d[:, q4 * 16:(q4 + 1) * 16, :])
            P.act.activation(out=w1s[:, q4 * 16:(q4 + 1) * 16, :], in_=sv, func=AF.Copy)
        sv = AP(stg, 0, [[8 * 652, 128], [64, 2], [1, 64]])
        P.sp.dma_start(out=sv, in_=w2d.ap())
        P.act.activation(out=w2s[:], in_=sv, func=AF.Copy)
        sv = AP(stg, 0, [[8 * 652, 64], [1, 32]])
        P.sp.dma_start(out=sv, in_=pd.ap())
        P.act.activation(out=ps_[:], in_=sv, func=AF.Copy)
        for half in range(2):
            for tau in range(32):
                P.pe.matmul(out=pf[1][:, half:half + 1], lhsT=w1s[:, tau, half * 128:(half + 1) * 128],
                            rhs=ps_[:, tau:tau + 1], start=(tau == 0), stop=(tau == 31))
        P.dve.tensor_copy(out=b1[:], in_=pf[1][:, 0:2])
    P.sp.dma_start(out=MW[:], in_=mwd.ap())
    P.sp.dma_start(out=CMW[:], in_=cmwd.ap())
    P.sp.dma_start(out=FBW[:], in_=fbwd.ap())
    for c0 in range(0, S, 2048):
        w = min(2048, S - c0)
        sv = AP(stg, 0, [[8 * 652, 128], [1, w]])
        P.sp.dma_start(out=sv, in_=exnd[:, c0:c0 + w])
        P.act.activation(out=EXN[:, c0:c0 + w], in_=sv, func=AF.Copy)
    for (dd, ds) in ((diagd, DIAG), (wlowd, WLOW)):
        sv = AP(stg, 0, [[8 * 652, 128], [1, 512]])
        P.sp.dma_start(out=sv, in_=dd.ap())
        P.act.activation(out=ds[:], in_=sv, func=AF.Copy)
    for t_ in (KcT, VcT, KcR, VcR):
        P.pool.memset(ap=t_[:], constant=0.0)
    P.pool.memset(ap=Vc[:], constant=0.0)
    P.pool.memset(ap=pcb[:], constant=0.0)
    P.pool.memset(ap=PP[:], constant=0.0)
    P.pool.memset(ap=Vs[:, :, 64:65], constant=1.0)
    P.pool.memset(ap=Vw[:, :, 64:65], constant=1.0)
    P.sp.dma_start(out=posi[0:NT, :], in_=pos.ap())
    P.dve.tensor_copy(out=posf[0:NT, :], in_=posi[0:NT, :])
    P.pe.transpose(out=pf[1][:, 0:NT], in_=posf[0:NT, :], identity=identf[0:NT, 0:NT])
    P.dve.tensor_copy(out=posT[:], in_=pf[1][:, 0:NT])
    for i in range(8):
        P.dve.tensor_scalar(out=ANG[:, :, i], in0=posT[:], scalar1=INV_FREQ[i], scalar2=None, op0=ALU.mult)
        P.dve.tensor_scalar(out=ANG[:, :, 8 + i], in0=posT[:], scalar1=INV_FREQ[i], scalar2=math.pi / 2, op0=ALU.mult, op1=ALU.add)
    P.dve.tensor_scalar(out=RR[:], in0=ANG[:], scalar1=1.0 / TWO_PI, scalar2=MAGIC, op0=ALU.mult, op1=ALU.add)
    P.dve.tensor_scalar(out=RR[:], in0=RR[:], scalar1=MAGIC, scalar2=-TWO_PI, op0=ALU.subtract, op1=ALU.mult)
    P.dve.tensor_tensor(out=RR[:], in0=RR[:], in1=ANG[:], op=ALU.add)
    P.dve.tensor_scalar(out=RR[:], in0=RR[:], scalar1=3.1415925, scalar2=-3.1415925, op0=ALU.min, op1=ALU.max)
    P.act.activation(out=SC[:], in_=RR[:], func=AF.Sin)

    FSC = NT * 16
    nps = [0]

    def score_bank():
        nps[0] += 1
        return pf[2 + nps[0] % 2]

    npt = [0]

    def pt_buf():
        npt[0] += 1
        return pT[npt[0] % 3]

    for i in range(NT):
        x_ = xs[i % 2]
        P.sp.dma_start(out=x_[:], in_=xa[i * 128:(i + 1) * 128, :])
        P.tag = "rmsnorm_g_folded_in_weig"
        P.dve.memset(ap=ssq[:], constant=0.0)
        P.act.activation(out=junk[:], in_=x_[:], func=AF.Square, accum_out=ssq[:])
        P.dve.tensor_scalar(out=rstd[:], in0=ssq[:], scalar1=1.0 / 1024.0, scalar2=1e-6, op0=ALU.mult, op1=ALU.add)
        P.act.activation(out=rstd[:], in_=rstd[:], func=AF.Sqrt)
        P.dve.reciprocal(out=rstd[:], in_=rstd[:])
        P.dve.tensor_scalar(out=hb[:], in0=x_[:], scalar1=rstd[:, 0:1], scalar2=None, op0=ALU.mult)
        for c in range(8):
            P.pe.transpose(out=pb[0][:, c * 128:(c + 1) * 128], in_=hb[:, c * 128:(c + 1) * 128], identity=identb[:])
        P.act.activation(out=hT[:], in_=pb[0][:, :], func=AF.Copy)
        P.tag = "projection_tokenmajor"
        for c in range(8):
            P.pe.matmul(out=pf[0][:, :], lhsT=hT[:, c, :], rhs=WA[:, c, 0:512], start=(c == 0), stop=(c == 7))
        for c in range(8):
            P.pe.matmul(out=pf[1][:, 0:140], lhsT=hT[:, c, :], rhs=WA[:, c, 512:652], start=(c == 0), stop=(c == 7))
        P.dve.tensor_copy(out=pr[:, 0:512], in_=pf[0][:, :])
        P.act.activation(out=pr[:, 512:652], in_=pf[1][:, 0:140], func=AF.Copy)
        P.dve.tensor_copy(out=prb[:], in_=pr[:])
        P.tag = "rope_on_7_heads_q03_kc_k"
        t1 = AP(pr, 0, [[652, 128], [64, 7], [1, 8]])
        t2 = AP(pr, 8, [[652, 128], [64, 7], [1, 8]])
        sin_b = AP(SC, i * 16, [[FSC, 128], [0, 7], [1, 8]])
        cos_b = AP(SC, i * 16 + 8, [[FSC, 128], [0, 7], [1, 8]])
        P.dve.tensor_tensor(out=rt[0][:], in0=t1, in1=cos_b, op=ALU.mult)
        P.dve.tensor_tensor(out=rt[1][:], in0=t2, in1=sin_b, op=ALU.mult)
        P.dve.tensor_tensor(out=rt[2][:], in0=t2, in1=cos_b, op=ALU.mult)
        P.dve.tensor_tensor(out=rt[3][:], in0=t1, in1=sin_b, op=ALU.mult)
        P.dve.tensor_tensor(out=AP(prb, 0, [[652, 128], [64, 7], [1, 8]]), in0=rt[0][:], in1=rt[1][:], op=ALU.subtract)
        P.dve.tensor_tensor(out=AP(prb, 8, [[652, 128], [64, 7], [1, 8]]), in0=rt[2][:], in1=rt[3][:], op=ALU.add)
        P.tag = "gates_sigmoid_via_exp"
        P.act.activation(out=gt[:], in_=pr[:, 640:652], func=AF.Exp, scale=-1.0)
        P.dve.tensor_scalar(out=gt[:], in0=gt[:], scalar1=1.0, scalar2=None, op0=ALU.add)
        P.dve.reciprocal(out=gt[:], in_=gt[:])
        P.tag = "transposes_to_featuremaj"
        for r in range(4):
            P.pe.transpose(out=pb[1][0:64, r * 128:(r + 1) * 128], in_=prb[:, r * 64:(r + 1) * 64], identity=identb[:])
        for j, c0 in enumerate((256, 320, 384, 448)):
            P.pe.transpose(out=pb[1][0:64, 512 + j * 128:512 + (j + 1) * 128], in_=prb[:, c0:c0 + 64], identity=identb[:])
        P.dve.tensor_copy(out=QT[:], in_=pb[1][0:64, 0:512])
        if i > 0:
            P.dve.tensor_copy(out=KcR[:, 0:16], in_=KcR[:, 128:144])
            P.dve.tensor_copy(out=VcR[:, 0:16], in_=VcR[:, 128:144])
        P.act.activation(out=KcR[:, 16:144], in_=pb[1][0:64, 512:640], func=AF.Copy)
        P.act.activation(out=KsT[:, i * 128:(i + 1) * 128], in_=pb[1][0:64, 640:768], func=AF.Copy)
        P.act.activation(out=KwT[:, i * 128:(i + 1) * 128], in_=pb[1][0:64, 768:896], func=AF.Copy)
        P.act.activation(out=VcR[:, 16:144], in_=pb[1][0:64, 896:1024], func=AF.Copy)
        P.dve.tensor_copy(out=Vs[:, i, 0:64], in_=prb[:, 512:576])
        P.dve.tensor_copy(out=Vw[:, i, 0:64], in_=prb[:, 576:640])
        P.tag = "compression_MLP_for_new"
        nb = 8 if i > 0 else 7
        base = 0 if i > 0 else 16
        n0 = 8 * i - 1 if i > 0 else 0
        ncn = 8 * i + 7
        for (roll, w1s, w2s, b1, dstT) in ((KcR, w1k, w2k, b1k, KcT), (VcR, w1v, w2v, b1v, VcT)):
            for half in range(2):
                for tau in range(32):
                    P.pe.matmul(out=pf[1][:, half * 16:half * 16 + nb], lhsT=w1s[:, tau, half * 128:(half + 1) * 128],
                                rhs=AP(roll, base + tau, [[144, 64], [16, nb]]), start=(tau == 0), stop=(tau == 31))
            for half in range(2):
                P.dve.tensor_scalar(out=hid[:, half, 0:nb], in0=pf[1][:, half * 16:half * 16 + nb], scalar1=b1[:, half:half + 1],
                                    scalar2=None, op0=ALU.add)
            P.act.activation(out=hide[:, :, 0:nb], in_=hid[:, :, 0:nb], func=AF.Exp, scale=-1.0)
            P.dve.tensor_scalar(out=hide[:, :, 0:nb], in0=hide[:, :, 0:nb], scalar1=1.0, scalar2=None, op0=ALU.add)
            P.dve.reciprocal(out=hide[:, :, 0:nb], in_=hide[:, :, 0:nb])
            P.dve.tensor_tensor(out=hsb[:, :, 0:nb], in0=hid[:, :, 0:nb], in1=hide[:, :, 0:nb], op=ALU.mult)
            for half in range(2):
                P.pe.matmul(out=pf[1][0:64, 32:32 + nb], lhsT=w2s[:, half, :], rhs=hsb[:, half, 0:nb], start=(half == 0), stop=(half == 1))
            P.dve.tensor_copy(out=dstT[:, n0:n0 + nb], in_=pf[1][0:64, 32:32 + nb])
        nct = (ncn + 127) // 128
        for ct in range(n0 // 128, nct):
            P.pe.transpose(out=pb[1][:, 0:64], in_=VcT[:, ct * 128:(ct + 1) * 128], identity=identb[0:64, 0:64])
            P.dve.tensor_copy(out=Vc[:, ct, :], in_=pb[1][:, 0:64])
        P.tag = "compressed_branch_q_n_la"
        for r in range(4):
            sb = score_bank()
            P.pe.matmul(out=sb[:, 0:ncn], lhsT=QT[:, r * 128:(r + 1) * 128], rhs=KcT[:, 0:ncn], start=True, stop=True)
            P.act.activation(out=pc[:, r, 0:ncn], in_=sb[:, 0:ncn], func=AF.Exp, scale=0.125)
            P.dve.tensor_tensor(out=pc[:, r, 0:ncn], in0=pc[:, r, 0:ncn], in1=MW[:, 512 - 8 * i:512 - 8 * i + ncn], op=ALU.mult)
            P.dve.reduce_sum(out=den[:, r:r + 1], in_=pc[:, r, 0:ncn], axis=AX.X)
        P.dve.tensor_scalar(out=rden[:], in0=den[:], scalar1=1e-30, scalar2=None, op0=ALU.max)
        P.dve.reciprocal(out=rden[:], in_=rden[:])
        for r in range(4):
            P.dve.tensor_scalar(out=pc[:, r, 0:ncn], in0=pc[:, r, 0:ncn], scalar1=rden[:, r:r + 1], scalar2=None, op0=ALU.mult)
            P.pool.tensor_copy(out=pcb[:, r, 0:ncn], in_=pc[:, r, 0:ncn])
        P.dve.tensor_tensor(out=PP[:, 1:1 + ncn], in0=pc[:, 0, 0:ncn], in1=pc[:, 1, 0:ncn], op=ALU.add)
        P.dve.tensor_tensor(out=PP[:, 1:1 + ncn], in0=PP[:, 1:1 + ncn], in1=pc[:, 2, 0:ncn], op=ALU.add)
        P.dve.tensor_tensor(out=PP[:, 1:1 + ncn], in0=PP[:, 1:1 + ncn], in1=pc[:, 3, 0:ncn], op=ALU.add)
        for r in range(4):
            for ct in range(nct):
                P.pe.transpose(out=pb[0][:, ct * 128:(ct + 1) * 128], in_=pcb[:, r, ct * 128:(ct + 1) * 128], identity=identb[:])
            P.act.activation(out=pTc[:, 0:nct, :], in_=pb[0][:, 0:nct * 128], func=AF.Copy)
            for ct in range(nct):
                P.pe.matmul(out=pf[1][:, 64 + r * 64:64 + (r + 1) * 64], lhsT=pTc[:, ct, :], rhs=Vc[:, ct, :], start=(ct == 0), stop=(ct == nct - 1))
        P.dve.tensor_copy(out=oc[:], in_=pf[1][:, 64:320])
        P.tag = "importance__top16_select"
        A = lambda k: AP(PP, k, [[520, 128], [4, 128]])
        P.dve.tensor_tensor(out=imp[:], in0=A(1), in1=A(2), op=ALU.add)
        P.dve.tensor_tensor(out=imp[:], in0=imp[:], in1=A(3), op=ALU.add)
        P.dve.scalar_tensor_tensor(out=imp[:], in0=imp[:], scalar=2.0, in1=A(0), op0=ALU.mult, op1=ALU.add)
        P.dve.tensor_tensor(out=imp[:], in0=imp[:], in1=A(4), op=ALU.add)
        P.dve.tensor_tensor(out=score[:], in0=imp[:], in1=CMW[:, 128 - 2 * i:256 - 2 * i], op=ALU.mult)
        P.dve.tensor_tensor(out=score[:], in0=score[:], in1=FBW[:, 128 - 2 * i:256 - 2 * i], op=ALU.add)
        P.dve.memset(ap=score[:, 0:1], constant=300.0)
        P.dve.max(out=m8[:, 0:8], in_=score[:])
        P.dve.match_replace(out=wk[:], in_to_replace=m8[:, 0:8], in_values=score[:], imm_value=-2.0)
        P.dve.max(out=m8[:, 8:16], in_=wk[:])
        P.dve.tensor_scalar(out=nsb[:], in0=score[:], scalar1=m8[:, 15:16], scalar2=None, op0=ALU.is_lt)
        P.pe.transpose(out=pb[1][:, 0:128], in_=nsb[:], identity=identb[:])
        P.dve.tensor_copy(out=nsT[:], in_=AP(pb[1], 0, [[1024, 128], [0, 4], [1, 128]]))
        nsTf = AP(nsT, 0, [[512, 128], [1, 512]])
        P.tag = "selected_dense_masked__s"
        k0 = max(0, i - 4)
        items = [("s", kt) for kt in range(i + 1)] + [("w", kt) for kt in range(k0, i + 1)]

        def emit_score(kind, kt):
            sb = score_bank()
            if kind == "s":
                P.pe.matmul(out=sb[:, :], lhsT=KsT[:, kt * 128:(kt + 1) * 128], rhs=QT[:], start=True, stop=False)
                P.pe.matmul(out=sb[:, :], lhsT=EXN[:, kt * 128:(kt + 1) * 128], rhs=nsTf, start=False, stop=(kt != i))
                if kt == i:
                    P.pe.matmul(out=sb[:, :], lhsT=identb[:], rhs=DIAG[:], start=False, stop=True)
            else:
                lowm = (kt == i - 4)
                P.pe.matmul(out=sb[:, :], lhsT=KwT[:, kt * 128:(kt + 1) * 128], rhs=QT[:], start=True, stop=not (lowm or kt == i))
                if lowm:
                    P.pe.matmul(out=sb[:, :], lhsT=identb[:], rhs=WLOW[:], start=False, stop=True)
                if kt == i:
                    P.pe.matmul(out=sb[:, :], lhsT=identb[:], rhs=DIAG[:], start=False, stop=True)
            return sb

        def emit_pv(kind, kt, p_):
            if kind == "s":
                P.pe.matmul(out=pf[4][0:65, :], lhsT=Vs[:, kt, :], rhs=p_[:], start=(kt == 0), stop=(kt == i))
            else:
                P.pe.matmul(out=pf[5][0:65, :], lhsT=Vw[:, kt, :], rhs=p_[:], start=(kt == k0), stop=(kt == i))

        prev = None
        for (kind, kt) in items:
            sb = emit_score(kind, kt)
            if prev is not None:
                emit_pv(*prev)
            p_ = pt_buf()
            P.act.activation(out=p_[:], in_=sb[:, :], func=AF.Exp, scale=0.125)
            prev = (kind, kt, p_)
        emit_pv(*prev)
        P.tag = "finalize_transpose_back"
        for bi, (pbank, oT_) in enumerate(((pf[4], oTs), (pf[5], oTw))):
            P.act.activation(out=oT_[:], in_=pbank[0:65, :], func=AF.Copy)
            for r in range(4):
                P.pe.transpose(out=pf[1][:, r * 65:(r + 1) * 65], in_=oT_[:, r * 128:(r + 1) * 128], identity=identf[0:65, 0:65])
            P.dve.tensor_copy(out=otok[bi][:], in_=pf[1][:, 0:260])
            P.dve.tensor_scalar(out=fac[bi][:], in0=otok[bi][:, :, 64], scalar1=1e-30, scalar2=None, op0=ALU.max)
            P.dve.reciprocal(out=fac[bi][:], in_=fac[bi][:])
            P.dve.tensor_tensor(out=fac[bi][:], in0=fac[bi][:], in1=gt[:, 4 + 4 * bi:8 + 4 * bi], op=ALU.mult)
        bc = lambda t, off: AP(t, off, [[t.shape[1], 128], [1, 4], [0, 64]])
        P.dve.tensor_tensor(out=at[:], in0=oc[:], in1=bc(gt, 0), op=ALU.mult)
        P.dve.tensor_tensor(out=at2[:], in0=otok[0][:, :, 0:64], in1=bc(fac[0], 0), op=ALU.mult)
        P.dve.tensor_tensor(out=at[:], in0=at[:], in1=at2[:], op=ALU.add)
        P.dve.tensor_tensor(out=at2[:], in0=otok[1][:, :, 0:64], in1=bc(fac[1], 0), op=ALU.mult)
        P.dve.tensor_tensor(out=at[:], in0=at[:], in1=at2[:], op=ALU.add)
        if scrA is None:
            P.pool.dma_start(out=attn[i * 128:(i + 1) * 128, :], in_=at[:])
        else:
            for h in range(2):
                P.pe.transpose(out=pf[1][:, h * 128:(h + 1) * 128], in_=AP(at, h * 128, [[256, 128], [1, 128]]), identity=identf[:])
            P.dve.tensor_copy(out=atT[:], in_=pf[1][:, 0:256])
            for h in range(2):
                P.pool.dma_start(out=scrA[i // 8][h * 128:(h + 1) * 128, (i % 8) * 128:(i % 8 + 1) * 128], in_=atT[:, h, :])


def consts_A(S):
    q = np.arange(128)
    fl = np.floor((q - 31) / 16.0).astype(np.int64)
    m = np.arange(1024)
    mw = (m[None, :] <= 512 + fl[:, None]).astype(np.float32)
    m2 = np.arange(256)
    h = (q >= 64).astype(np.int64)
    cmw = (m2[None, :] < 128 + h[:, None] - 1).astype(np.float32)
    fbw = np.zeros((128, 256), np.float32)
    fbw[(m2[None, :] == 128 + h[:, None])] = 200.0
    fbw[(m2[None, :] == 128 + h[:, None] - 1)] = 100.0
    fbw[m2[None, :] > 128 + h[:, None]] = -1.0
    j = np.arange(128)
    k = np.arange(S)
    exn = np.where((k[None, :] // 64) == j[:, None], NEG, 0.0).astype(np.float32)
    kl = np.arange(128)
    diag = np.where(kl[:, None] > q[None, :], NEG, 0.0).astype(np.float32)
    wlow = np.where(kl[:, None] <= q[None, :], NEG, 0.0).astype(np.float32)
    return dict(identf=np.eye(128, dtype=np.float32), mw=mw, cmw=cmw, fbw=fbw, exn=exn,
                diagc=np.tile(diag, (1, 4)), wlow=np.tile(wlow, (1, 4)))


def inputs_A(inp, b, g, S):
    w_in = inp["w_in"]
    cols = list(range(g * 256, g * 256 + 256))
    base = 1024
    off = lambda j: base + j * 256 + g * 64
    for j in (0, 2, 4, 1, 3, 5):
        cols += list(range(off(j), off(j) + 64))
    gb = 1024 + 6 * 256
    cols += [gb + br * 16 + g * 4 + r for br in range(3) for r in range(4)]
    d = dict(
        xa=np.ascontiguousarray(inp["x"][b, :S]),
        pos=np.ascontiguousarray(inp["positions"][b, :S].reshape(S // 128, 128)).astype(np.int32),
        wa=np.ascontiguousarray(w_in[:, cols]),
        gmix=np.ascontiguousarray(inp["norm_mix_g"].reshape(8, 128).T),
        kw1=np.ascontiguousarray(inp["cmp_k_w1"].reshape(32, 64, 256).transpose(1, 0, 2)),
        kw2=np.ascontiguousarray(inp["cmp_k_w2"].reshape(2, 128, 64).transpose(1, 0, 2)),
        kpos=np.ascontiguousarray(inp["cmp_k_pos"].T),
        vw1=np.ascontiguousarray(inp["cmp_v_w1"].reshape(32, 64, 256).transpose(1, 0, 2)),
        vw2=np.ascontiguousarray(inp["cmp_v_w2"].reshape(2, 128, 64).transpose(1, 0, 2)),
        vpos=np.ascontiguousarray(inp["cmp_v_pos"].T),
    )
    d.update(consts_A(S))
    return d
import numpy as np

TB = 512
HALO = 32


def build_B(T):
    nc = bass.Bass("TRN2", target_bir_lowering=False)
    P = Prog(nc)
    record_B(P, T, None, None)
    P.emit()
    return nc


def record_B(P, T, scrA, S):
    NP = T // TB
    TT = TB + HALO
    D = lambda n, sh, dt=F32, kind="ExternalInput": P.dram(n, sh, dt, kind)
    xTd = D("xT", [1024, HALO + T])
    if scrA is None:
        aTd = D("aT", [1024, T])
    else:
        scrG = [P.dram(f"scrG{j}", [1024, 1024], F32, "Internal", track=True) for j in range(S // 1024)]
        qmd = D("qmask", [128, 4])
    wb = D("wb", [1024, 4096])
    wco = D("wco", [1024, 1024]); wo = D("wo", [1024, 1024])
    wg = D("wg", [1024, 2816]); wu = D("wu", [1024, 2816]); wd = D("wd", [2816, 1024])
    vecd = D("vecs", [128, 8, 8])
    cwd = D("cw", [128, 8, 31])
    idd = D("identB", [128, 128])
    yTd = D("yT", [1024, T], F32, "ExternalOutput")

    vec = P.sbuf("vec", [128, 8, 8], F32)
    cw = P.sbuf("cw_s", [128, 8, 31], F32)
    onesb = P.sbuf("onesb", [128, 128], BF16)
    xT = P.sbuf("xT_s", [128, 8, TT], F32, blk=TT)
    aT = P.sbuf("aT_s", [128, 8, TB], F32, blk=TB)
    sqb = P.sbuf("sqb", [128, 8, TT], BF16, blk=TT)
    rrep = P.sbuf("rrep", [128, TT], F32)
    hT = P.sbuf("hT_s", [128, 8, TT], BF16, blk=TT)
    zb = [P.sbuf(f"zb{i}", [128, TT], BF16) for i in range(2)]
    identf = P.sbuf("identf_b", [128, 128], F32)
    Dg = P.sbuf("Dg", [128, 31, 128], BF16)
    eb = [P.sbuf(f"eb{i}", [128, TT], F32) for i in range(2)]
    zc = P.sbuf("zc", [128, 8, TB], F32, blk=TB)
    mu = P.sbuf("mu", [128, TB], F32)
    lrs = P.sbuf("lrs", [128, TB], F32)
    sT = P.sbuf("sT", [128, 8, TB], BF16, blk=TB)
    t1 = [P.sbuf(f"t1_{i}", [128, TB], F32) for i in range(2)]
    t2 = [P.sbuf(f"t2_{i}", [128, TB], F32) for i in range(2)]
    t3 = [P.sbuf(f"t3_{i}", [128, TB], F32) for i in range(2)]
    mg = P.sbuf("mg", [128, 8, TB], BF16, blk=TB)
    actT = P.sbuf("actT", [128, 22, TB], BF16, blk=TB)
    yb = [P.sbuf(f"yb{i}", [128, TB], F32) for i in range(2)]
    stgs = [P.sbuf(f"stw{i}", [128, 22, 128], F32) for i in range(2)]
    wbufs = [P.sbuf(f"wbf{i}", [128, 22, 128], BF16) for i in range(3)]
    pf = [P.psum(f"pf{i}", [128, 512], F32) for i in range(8)]
    if scrA is not None:
        cand = [P.sbuf("cand0", [128, 8, TB], F32)]
        qm = P.sbuf("qm", [128, 4], F32)
        P.sp.dma_start(out=qm[:], in_=qmd.ap())
        for j in range(S // 1024):
            P.pool.collective_compute("AllGather", ALU.bypass, replica_groups=[[0, 1, 2, 3], [4, 5, 6, 7]],
                                      ins=[scrA[j].ap()], outs=[scrG[j].ap()])

    P.sp.dma_start(out=vec[:], in_=vecd.ap())
    P.sp.dma_start(out=cw[:], in_=cwd.ap())
    P.sp.dma_start(out=identf[:], in_=idd.ap())
    P.pool.memset(ap=onesb[:], constant=1.0)
    GMIX, GFFN, GFIN, CONVB, LNG, LNB, BCO = range(7)
    V = lambda k, c: vec[:, k, c:c + 1]

    cnt = {"w": 0, "pb": 0}

    def load_w(Wd, col0, KC):
        n = cnt["w"]; cnt["w"] += 1
        st = stgs[n % 2]; wbf = wbufs[n % 3]
        P.sp.dma_start(out=st[:, 0:KC, :], in_=Wd[:, col0:col0 + 128].rearrange("(c p) n -> p c n", p=128))
        if n % 4 == 0:
            P.pool.tensor_copy(out=wbf[:, 0:KC, :], in_=st[:, 0:KC, :])
        elif n % 4 == 3:
            P.dve.tensor_copy(out=wbf[:, 0:KC, :], in_=st[:, 0:KC, :])
        else:
            P.act.activation(out=wbf[:, 0:KC, :], in_=st[:, 0:KC, :], func=AF.Copy)
        return wbf

    def bank():
        cnt["pb"] += 1
        return pf[2 + cnt["pb"] % 6]

    def stats(src_fn, N, rep, sq_scale=1.0 / 1024.0, eps=1e-6, bankidx=0):
        segs = [(0, min(N, 512))] + ([(512, N)] if N > 512 else [])
        for c in range(8):
            P.act.activation(out=sqb[:, c, 0:N], in_=src_fn(c), func=AF.Square)
        for si, (a, b_) in enumerate(segs):
            pb_ = pf[bankidx + si]
            for c in range(8):
                P.pe.matmul(out=pb_[:, 0:b_ - a], lhsT=onesb[:], rhs=sqb[:, c, a:b_], start=(c == 0), stop=(c == 7))
            P.dve.tensor_scalar(out=rep[:, a:b_], in0=pb_[:, 0:b_ - a], scalar1=sq_scale, scalar2=eps, op0=ALU.mult, op1=ALU.add)
        P.act.activation(out=rep[:, 0:N], in_=rep[:, 0:N], func=AF.Sqrt)
        P.dve.reciprocal(out=rep[:, 0:N], in_=rep[:, 0:N])

    def sigmoid_from(dst, src, n):
        P.act.activation(out=dst[:, 0:n], in_=src, func=AF.Exp, scale=-1.0)
        P.dve.tensor_scalar(out=dst[:, 0:n], in0=dst[:, 0:n], scalar1=1.0, scalar2=None, op0=ALU.add)
        P.dve.reciprocal(out=dst[:, 0:n], in_=dst[:, 0:n])

    for p in range(NP):
        t0 = p * TB
        P.sp.dma_start(out=xT[:], in_=xTd[:, t0:t0 + TT].rearrange("(c p) n -> p c n", p=128))
        if scrA is None:
            P.sp.dma_start(out=aT[:], in_=aTd[:, t0:t0 + TB].rearrange("(c p) n -> p c n", p=128))
        P.tag = "h__rmsnormx__g_featurema"
        stats(lambda c: xT[:, c, :], TT, rrep)
        for c in range(8):
            P.dve.scalar_tensor_tensor(out=hT[:, c, :], in0=xT[:, c, :], scalar=V(GMIX, c), in1=rrep[:], op0=ALU.mult, op1=ALU.mult)
        P.tag = "conv_branch_GLU__depthwi"
        for c in range(8):
            wa_ = load_w(wb, c * 128, 8)
            wb_ = load_w(wb, 1024 + c * 128, 8)
            z = zb[c % 2]; e = eb[c % 2]
            pa, pa2, pbk, pb2 = bank(), bank(), bank(), bank()
            for (w_, pm, ph) in ((wa_, pa, pa2), (wb_, pbk, pb2)):
                for k in range(8):
                    P.pe.matmul(out=pm[:, :], lhsT=w_[:, k, :], rhs=hT[:, k, HALO:TT], start=(k == 0), stop=(k == 7))
                for k in range(8):
                    P.pe.matmul(out=ph[:, 0:HALO], lhsT=w_[:, k, :], rhs=hT[:, k, 0:HALO], start=(k == 0), stop=(k == 7))
            P.act.activation(out=e[:, HALO:TT], in_=pbk[:, :], func=AF.Exp, scale=-1.0)
            P.act.activation(out=e[:, 0:HALO], in_=pb2[:, 0:HALO], func=AF.Exp, scale=-1.0)
            P.dve.tensor_scalar(out=e[:], in0=e[:], scalar1=1.0, scalar2=None, op0=ALU.add)
            P.dve.reciprocal(out=e[:], in_=e[:])
            P.dve.tensor_tensor(out=z[:, HALO:TT], in0=e[:, HALO:TT], in1=pa[:, :], op=ALU.mult)
            P.dve.tensor_tensor(out=z[:, 0:HALO], in0=e[:, 0:HALO], in1=pa2[:, 0:HALO], op=ALU.mult)
            P.dve.tensor_tensor(out=Dg[:], in0=AP(identf, 0, [[128, 128], [0, 31], [1, 128]]),
                                in1=AP(cw, c * 31, [[8 * 31, 128], [1, 31], [0, 128]]), op=ALU.mult)
            pcv = bank()
            for w in range(31):
                P.pe.matmul(out=pcv[:, :], lhsT=Dg[:, w, :], rhs=z[:, 2 + w:2 + w + TB], start=(w == 0), stop=(w == 30))
            P.dve.tensor_scalar(out=zc[:, c, :], in0=pcv[:, :], scalar1=V(CONVB, c), scalar2=None, op0=ALU.add)
        P.tag = "LayerNorm_over_channels"
        for c in range(8):
            P.act.activation(out=sqb[:, c, 0:TB], in_=zc[:, c, :], func=AF.Copy)
        for c in range(8):
            P.pe.matmul(out=pf[0][:, :], lhsT=onesb[:], rhs=sqb[:, c, 0:TB], start=(c == 0), stop=(c == 7))
        P.dve.tensor_scalar(out=mu[:], in0=pf[0][:, :], scalar1=1.0 / 1024.0, scalar2=None, op0=ALU.mult)
        for c in range(8):
            P.dve.tensor_tensor(out=zc[:, c, :], in0=zc[:, c, :], in1=mu[:], op=ALU.subtract)
        stats(lambda c: zc[:, c, :], TB, lrs)
        for c in range(8):
            a = t1[c % 2]; b_ = t2[c % 2]
            P.dve.tensor_tensor(out=a[:], in0=zc[:, c, :], in1=lrs[:], op=ALU.mult)
            P.dve.tensor_scalar(out=a[:], in0=a[:], scalar1=V(LNG, c), scalar2=V(LNB, c), op0=ALU.mult, op1=ALU.add)
            sigmoid_from(b_, a[:], TB)
            P.pool.tensor_tensor(out=sT[:, c, :], in0=a[:], in1=b_[:], op=ALU.mult)
        if scrA is not None:
            for q in range(4):
                cd = cand[0]
                tg = q * T + t0
                P.sp.dma_start(out=cd[:], in_=scrG[tg // 1024][:, tg % 1024:tg % 1024 + TB].rearrange("(c p) n -> p c n", p=128))
                if q == 0:
                    P.dve.tensor_scalar(out=aT[:], in0=cd[:], scalar1=qm[:, 0:1], scalar2=None, op0=ALU.mult)
                else:
                    P.dve.scalar_tensor_tensor(out=aT[:], in0=cd[:], scalar=qm[:, q:q + 1], in1=aT[:], op0=ALU.mult, op1=ALU.add)
        P.tag = "conv_out_gates_merge"
        for oc in range(8):
            wga_ = load_w(wb, 2048 + oc * 128, 8)
            wgc_ = load_w(wb, 3072 + oc * 128, 8)
            wco_ = load_w(wco, oc * 128, 8)
            pga, pgc, pco = bank(), bank(), bank()
            for k in range(8):
                P.pe.matmul(out=pga[:, :], lhsT=wga_[:, k, :], rhs=hT[:, k, HALO:TT], start=(k == 0), stop=(k == 7))
            for k in range(8):
                P.pe.matmul(out=pgc[:, :], lhsT=wgc_[:, k, :], rhs=hT[:, k, HALO:TT], start=(k == 0), stop=(k == 7))
            for k in range(8):
                P.pe.matmul(out=pco[:, :], lhsT=wco_[:, k, :], rhs=sT[:, k, :], start=(k == 0), stop=(k == 7))
            a = t1[oc % 2]; b_ = t2[oc % 2]; c_ = t3[oc % 2]
            sigmoid_from(a, pga[:, :], TB)
            sigmoid_from(b_, pgc[:, :], TB)
            P.dve.tensor_tensor(out=a[:], in0=a[:], in1=aT[:, oc, :], op=ALU.mult)
            P.dve.scalar_tensor_tensor(out=c_[:], in0=pco[:, :], scalar=V(BCO, oc), in1=b_[:], op0=ALU.add, op1=ALU.mult)
            P.pool.tensor_tensor(out=mg[:, oc, :], in0=a[:], in1=c_[:], op=ALU.add)
        P.tag = "x1__x__merged__w_out_in"
        for oc in range(8):
            wo_ = load_w(wo, oc * 128, 8)
            po = bank()
            for k in range(8):
                P.pe.matmul(out=po[:, :], lhsT=wo_[:, k, :], rhs=mg[:, k, :], start=(k == 0), stop=(k == 7))
            P.dve.tensor_tensor(out=xT[:, oc, HALO:TT], in0=xT[:, oc, HALO:TT], in1=po[:, :], op=ALU.add)
        P.tag = "FFN"
        stats(lambda c: xT[:, c, HALO:TT], TB, rrep)
        for c in range(8):
            P.dve.scalar_tensor_tensor(out=hT[:, c, 0:TB], in0=xT[:, c, HALO:TT], scalar=V(GFFN, c), in1=rrep[:, 0:TB], op0=ALU.mult, op1=ALU.mult)
        for f in range(22):
            wg_ = load_w(wg, f * 128, 8)
            wu_ = load_w(wu, f * 128, 8)
            pg, pu = bank(), bank()
            for k in range(8):
                P.pe.matmul(out=pg[:, :], lhsT=wg_[:, k, :], rhs=hT[:, k, 0:TB], start=(k == 0), stop=(k == 7))
            for k in range(8):
                P.pe.matmul(out=pu[:, :], lhsT=wu_[:, k, :], rhs=hT[:, k, 0:TB], start=(k == 0), stop=(k == 7))
            a = t1[f % 2]; b_ = t2[f % 2]
            sigmoid_from(a, pg[:, :], TB)
            P.dve.tensor_tensor(out=b_[:], in0=a[:], in1=pg[:, :], op=ALU.mult)
            P.dve.tensor_tensor(out=actT[:, f, :], in0=b_[:], in1=pu[:, :], op=ALU.mult)
        for oc in range(8):
            wd_ = load_w(wd, oc * 128, 22)
            pd_ = bank()
            for k in range(22):
                P.pe.matmul(out=pd_[:, :], lhsT=wd_[:, k, :], rhs=actT[:, k, :], start=(k == 0), stop=(k == 21))
            P.dve.tensor_tensor(out=xT[:, oc, HALO:TT], in0=xT[:, oc, HALO:TT], in1=pd_[:, :], op=ALU.add)
        P.tag = "final_norm__store"
        stats(lambda c: xT[:, c, HALO:TT], TB, rrep)
        for c in range(8):
            y = yb[c % 2]
            P.dve.scalar_tensor_tensor(out=y[:], in0=xT[:, c, HALO:TT], scalar=V(GFIN, c), in1=rrep[:, 0:TB], op0=ALU.mult, op1=ALU.mult)
            P.pool.dma_start(out=yTd[c * 128:(c + 1) * 128, t0:t0 + TB], in_=y[:])


def inputs_B(inp, attn_b, b, t0, T):
    x = inp["x"][b]
    xh = np.zeros((HALO + T, 1024), np.float32)
    lo = t0 - HALO
    if lo >= 0:
        xh[:] = x[lo:t0 + T]
    else:
        xh[-lo:] = x[0:t0 + T]
    w_in = inp["w_in"]
    lay = lambda v: np.ascontiguousarray(v.reshape(8, 128).T)
    vecs = np.zeros((128, 8, 8), np.float32)
    for k, name in enumerate(["norm_mix_g", "norm_ffn_g", "norm_final_g", "conv_b", "conv_norm_g", "conv_norm_b", "b_conv_out"]):
        vecs[:, k, :] = lay(inp[name])
    cw = np.ascontiguousarray(inp["conv_w"][:, 0, :].reshape(31, 8, 128).transpose(2, 1, 0))
    return dict(
        xT=np.ascontiguousarray(xh.T), **({} if attn_b is None else {"aT": np.ascontiguousarray(attn_b[t0:t0 + T].T)}),
        wb=np.ascontiguousarray(w_in[:, 2608:6704]), wco=inp["w_conv_out"], wo=inp["w_out"],
        wg=inp["w_ffn_gate"], wu=inp["w_ffn_up"], wd=inp["w_ffn_down"], vecs=vecs, cw=cw, identB=np.eye(128, dtype=np.float32))
from concourse.bass_utils import run_bass_kernel_spmd

S_FULL = 8192
T_CORE = 2048


def build_fused(S, T):
    nc = bass.Bass("TRN2", target_bir_lowering=False)
    scrA = [nc.dram_tensor(f"scrA{j}", [256, 1024], F32) for j in range(S // 1024)]
    P1 = Prog(nc, "A_")
    record_A(P1, S, scrA)
    P1.emit()
    P2 = Prog(nc, "B_")
    record_B(P2, T, scrA, S)
    P2.emit()
    return nc


def fused_inputs(inp, c, S, T):
    b, r = c // 4, c % 4
    d = inputs_A(inp, b, r, S)
    db = inputs_B(inp, None, b, r * T, T)
    d.update(db)
    qm = np.zeros((128, 4), np.float32)
    qm[:, r] = 1.0
    d["qmask"] = qm
    return d


def kernel(**inputs):
    inp = {k: np.asarray(v) for k, v in inputs.items()}
    B = inp["x"].shape[0]
    nc = build_fused(S_FULL, T_CORE)
    maps = [fused_inputs(inp, c, S_FULL, T_CORE) for c in range(8)]
    res = run_bass_kernel_spmd(nc, maps, core_ids=list(range(8)))
    out = np.zeros((B, S_FULL, 1024), np.float32)
    for c in range(8):
        out[c // 4, (c % 4) * T_CORE:(c % 4 + 1) * T_CORE, :] = res.results[c]["yT"].T
    return out
```

```python
import numpy as np
from contextlib import ExitStack
import concourse.bass as bass
import concourse.mybir as mybir

F32 = mybir.dt.float32
BF16 = mybir.dt.bfloat16
I32 = mybir.dt.int32
AF = mybir.ActivationFunctionType
ALU = mybir.AluOpType
AX = mybir.AxisListType

ENGS = ("pe", "act", "dve", "pool", "sp")
WRITE_KW = ("out", "accum_out", "out_max", "out_indices", "ap")
DT_SIZE = {F32: 4, BF16: 2, I32: 4}


def _dtsize(dt):
    try:
        return DT_SIZE[dt]
    except Exception:
        return mybir.dt.size(dt)


class Op:
    __slots__ = ("eng", "fn", "deps", "is_dma", "needed", "sig", "gidx", "pre_wait", "cc", "tag")

    def __init__(self, eng, fn, is_dma, gidx):
        self.eng = eng
        self.fn = fn
        self.deps = []
        self.is_dma = is_dma
        self.needed = False
        self.sig = None
        self.gidx = gidx
        self.pre_wait = None
        self.cc = False
        self.tag = None


class _EngProxy:
    def __init__(self, prog, eng):
        self._p = prog
        self._e = eng

    def __getattr__(self, name):
        p, e = self._p, self._e

        def call(*args, **kw):
            extra_r = kw.pop("_r", ())
            extra_w = kw.pop("_w", ())
            return p._record(e, name, args, kw, extra_r, extra_w)

        return call


class Prog:
    RING = 12

    def __init__(self, nc, prefix="", sem_stack=None):
        self.nc = nc
        self.prefix = prefix
        self.tag = None
        self.sem_stack = sem_stack
        self.ops = {e: [] for e in ENGS}
        self.n = 0
        self.blk = {}
        self.untracked = set()
        self.last_w = {}
        self.readers = {}
        self.pe = _EngProxy(self, "pe")
        self.act = _EngProxy(self, "act")
        self.dve = _EngProxy(self, "dve")
        self.pool = _EngProxy(self, "pool")
        self.sp = _EngProxy(self, "sp")
        self.stack = ExitStack()
        self.dma_count = {e: 0 for e in ENGS}

    def sbuf(self, name, shape, dtype, blk=None):
        name = self.prefix + name
        t = self.stack.enter_context(self.nc.sbuf_tensor(name, list(shape), dtype))
        self.blk[name] = None if blk is None else blk * _dtsize(dtype)
        return t

    def psum(self, name, shape, dtype, blk=None):
        name = self.prefix + name
        t = self.stack.enter_context(self.nc.psum_tensor(name, list(shape), dtype))
        self.blk[name] = None if blk is None else blk * _dtsize(dtype)
        return t

    def dram(self, name, shape, dtype, kind, track=False, blk=None):
        if kind == "Internal":
            t = self.nc.dram_tensor(name, list(shape), dtype)
        else:
            t = self.nc.dram_tensor(name, list(shape), dtype, kind=kind)
        if not track:
            self.untracked.add(name)
        else:
            self.blk[name] = None if blk is None else blk * _dtsize(dtype)
        return t

    def keys(self, ap):
        t = ap.tensor
        name = t.name
        if name in self.untracked:
            return []
        b = self.blk.get(name, None)
        if b is None:
            return [name]
        es = _dtsize(ap.dtype)
        space = str(type(t).__name__)
        pairs = list(ap.ap)
        if "DRam" in space:
            lo = ap.offset
            hi = lo + sum((c - 1) * abs(s) for s, c in pairs)
        else:
            shp = list(t.shape)
            F = 1
            for d in shp[1:]:
                F *= d
            F = F * _dtsize(t.dtype) // es
            lo = ap.offset % F
            hi = lo + sum((c - 1) * abs(s) for s, c in pairs[1:])
        lo_b, hi_b = lo * es, hi * es + es - 1
        return [(name, i) for i in range(lo_b // b, hi_b // b + 1)]

    def _record(self, eng, name, args, kw, extra_r, extra_w):
        is_dma = name in ("dma_start", "collective_compute", "dma_start_transpose")
        rk, wk = [], []
        if name == "collective_compute":
            for a in kw.get("ins", []):
                rk += self.keys(a)
            for a in kw.get("outs", []):
                wk += self.keys(a)
        else:
            for k, v in kw.items():
                if isinstance(v, bass.AP):
                    if "PSum" in type(v.tensor).__name__:
                        wk.extend(self.keys(v))
                    else:
                        (wk if k in WRITE_KW else rk).extend(self.keys(v))
            for a in args:
                if isinstance(a, bass.AP):
                    raise ValueError("pass APs as kwargs")
        for a in extra_r:
            rk += self.keys(a) if isinstance(a, bass.AP) else [a]
        for a in extra_w:
            wk += self.keys(a) if isinstance(a, bass.AP) else [a]

        def fn(engobj, name=name, args=args, kw=kw):
            return getattr(engobj, name)(*args, **kw)

        op = self.raw(eng, fn, rk, wk, is_dma)
        op.cc = (name == "collective_compute")
        return op

    def raw(self, eng, fn, rk, wk, is_dma=False):
        op = Op(eng, fn, is_dma, self.n)
        op.tag = self.tag
        self.n += 1
        deps = {}
        for k in rk:
            lw = self.last_w.get(k)
            if lw is not None:
                deps[lw.gidx] = lw
        for k in wk:
            lw = self.last_w.get(k)
            if lw is not None:
                deps[lw.gidx] = lw
            rd = self.readers.get(k)
            if rd:
                for e, o in rd.items():
                    if e == "dma":
                        for oo in o:
                            deps[oo.gidx] = oo
                    else:
                        deps[o.gidx] = o
        for k in wk:
            self.last_w[k] = op
            self.readers[k] = {}
        for k in rk:
            if k in wk:
                continue
            rd = self.readers.setdefault(k, {})
            if is_dma:
                rd.setdefault("dma", []).append(op)
            else:
                rd[eng] = op
        for d in deps.values():
            if d.eng == eng and eng == "pe" and not d.is_dma and not is_dma:
                continue
            op.deps.append(d)
            d.needed = True
        if is_dma:
            op.needed = True
        self.ops[eng].append(op)
        return op

    def emit(self, final_wait_engs=("sp", "pool", "act")):
        nc = self.nc
        allsems = []

        def newsem(name):
            h = nc.alloc_semaphore(name=name)
            allsems.append(h)
            return h

        esem = {e: newsem(self.prefix + "es_" + e) for e in ENGS}
        ccsem = newsem(self.prefix + "ccsem")
        ncc = 0
        rings = {}
        for e in ENGS:
            if any(o.is_dma for o in self.ops[e]):
                rings[e] = [newsem(f"{self.prefix}dr_{e}_{i}") for i in range(self.RING)]
        for e in ENGS:
            c = 0
            nd = 0
            for o in self.ops[e]:
                if o.cc:
                    ncc += 1
                    o.sig = (ccsem, ncc)
                elif o.is_dma:
                    sem = rings[e][nd % self.RING]
                    o.sig = (sem, 16 * (nd // self.RING + 1))
                    if nd >= self.RING:
                        o.pre_wait = (sem, 16 * (nd // self.RING))
                    nd += 1
                elif o.needed:
                    c += 1
                    o.sig = (esem[e], c)
            self.dma_count[e] = nd
        prog = self

        def run_engine(e, engobj):
            waited = {}

            def wait(sem, val):
                key = id(sem)
                if waited.get(key, 0) >= val:
                    return
                engobj.wait_ge(sem, val)
                waited[key] = val

            semobj = {}
            for o in prog.ops[e]:
                for d in o.deps:
                    sem, val = d.sig
                    semobj[id(sem)] = sem
                    wait(sem, val)
                if o.pre_wait is not None:
                    wait(*o.pre_wait)
                ins = o.fn(engobj)
                if o.tag is not None:
                    ins.annotate(o.tag)
                if o.sig is not None:
                    sem, val = o.sig
                    if o.cc:
                        ins.then_inc(sem, 1)
                    elif o.is_dma:
                        ins.then_inc(sem, 16)
                    else:
                        ins.then_inc(sem, 1)
            if e in rings:
                nd = prog.dma_count[e]
                for i, sem in enumerate(rings[e]):
                    cnt = (nd - i + prog.RING - 1) // prog.RING if nd > i else 0
                    if cnt > 0:
                        wait(sem, 16 * cnt)

        with nc.Block() as block:
            if self.ops["sp"]:
                block.sync(lambda eng: run_engine("sp", eng))
            if self.ops["act"]:
                block.scalar(lambda eng: run_engine("act", eng))
            if self.ops["dve"]:
                block.vector(lambda eng: run_engine("dve", eng))
            if self.ops["pool"]:
                block.gpsimd(lambda eng: run_engine("pool", eng))
            if self.ops["pe"]:
                block.tensor(lambda eng: run_engine("pe", eng))
        nc.clear_and_free_semaphores(allsems)
        nc.all_engine_barrier()
        self.stack.close()
import math
import numpy as np

NEG = -240000.0
INV_FREQ = [float(np.float32(500000.0) ** np.float32(-(2 * i) / 16.0)) for i in range(8)]
MAGIC = 12582912.0
TWO_PI = 2.0 * math.pi


def AP(t, off, pairs):
    return bass.AP(t, off, [list(p) for p in pairs])


def build_A(S):
    nc = bass.Bass("TRN2", target_bir_lowering=False)
    P = Prog(nc)
    record_A(P, S, None)
    P.emit()
    return nc


def record_A(P, S, scrA):
    NT = S // 128
    NCMAX = S // 16 - 1
    NCT = (NCMAX + 127) // 128
    D = lambda n, sh, dt=F32, kind="ExternalInput": P.dram(n, sh, dt, kind)
    xa = D("xa", [S, 1024])
    pos = D("pos", [NT, 128], I32)
    wa = D("wa", [1024, 652])
    gmix = D("gmix", [128, 8])
    kw1 = D("kw1", [64, 32, 256]); kw2 = D("kw2", [128, 2, 64]); kpos = D("kpos", [64, 32])
    vw1 = D("vw1", [64, 32, 256]); vw2 = D("vw2", [128, 2, 64]); vpos = D("vpos", [64, 32])
    identd = D("identf", [128, 128])
    mwd = D("mw", [128, 1024]); cmwd = D("cmw", [128, 256]); fbwd = D("fbw", [128, 256])
    exnd = D("exn", [128, S]); diagd = D("diagc", [128, 512]); wlowd = D("wlow", [128, 512])
    attn = D("attn", [S, 256], F32, "ExternalOutput") if scrA is None else None

    identf = P.sbuf("identf_s", [128, 128], F32)
    identb = P.sbuf("identb", [128, 128], BF16)
    stg = P.sbuf("stg", [128, 8 * 652], F32)
    WA = P.sbuf("WA", [128, 8, 652], BF16)
    gm = P.sbuf("gm", [128, 8], F32)
    w1k = P.sbuf("w1k", [64, 32, 256], BF16); w1v = P.sbuf("w1v", [64, 32, 256], BF16)
    w2k = P.sbuf("w2k", [128, 2, 64], BF16); w2v = P.sbuf("w2v", [128, 2, 64], BF16)
    posk = P.sbuf("posk", [64, 32], BF16); posv = P.sbuf("posv", [64, 32], BF16)
    b1k = P.sbuf("b1k", [128, 2], F32); b1v = P.sbuf("b1v", [128, 2], F32)
    MW = P.sbuf("MW", [128, 1024], F32); CMW = P.sbuf("CMW", [128, 256], F32); FBW = P.sbuf("FBW", [128, 256], F32)
    EXN = P.sbuf("EXN", [128, S], BF16, blk=128)
    DIAG = P.sbuf("DIAG", [128, 512], BF16); WLOW = P.sbuf("WLOW", [128, 512], BF16)
    posi = P.sbuf("posi", [64, 128], I32); posf = P.sbuf("posf", [64, 128], F32)
    posT = P.sbuf("posT", [128, NT], F32)
    ANG = P.sbuf("ANG", [128, NT, 16], F32); RR = P.sbuf("RR", [128, NT, 16], F32); SC = P.sbuf("SC", [128, NT, 16], F32)
    KsT = P.sbuf("KsT", [64, S], BF16, blk=128); KwT = P.sbuf("KwT", [64, S], BF16, blk=128)
    Vs = P.sbuf("Vs", [128, NT, 65], BF16, blk=65); Vw = P.sbuf("Vw", [128, NT, 65], BF16, blk=65)
    KcR = [P.sbuf(f"KcR{k}", [64, 144], BF16) for k in range(2)]; VcR = [P.sbuf(f"VcR{k}", [64, 144], BF16) for k in range(2)]
    KcT = P.sbuf("KcT", [64, NCT * 128], BF16); VcT = P.sbuf("VcT", [64, NCT * 128], BF16)
    Vc = P.sbuf("Vc", [128, NCT, 64], BF16)
    xs = [P.sbuf(f"xs{i}", [128, 1024], F32) for i in range(2)]
    ssq = P.sbuf("ssq", [128, 1], F32); rstd = P.sbuf("rstd", [128, 1], F32)
    hb = P.sbuf("hb", [128, 1024], BF16)
    hT = P.sbuf("hT", [128, 8, 128], BF16)
    pr = P.sbuf("pr", [128, 652], F32); prb = P.sbuf("prb", [128, 652], BF16)
    rt = [P.sbuf(f"rt{i}", [128, 7, 8], F32) for i in range(4)]
    gt = [P.sbuf(f"gt{k}", [128, 12], F32) for k in range(3)]
    QT = [P.sbuf(f"QT{k}", [64, 512], BF16) for k in range(3)]
    hid = P.sbuf("hid", [128, 2, 8], F32); hide = P.sbuf("hide", [128, 2, 8], F32); hsb = P.sbuf("hsb", [128, 2, 8], BF16)
    pc = P.sbuf("pc", [128, 4, 512], F32, blk=512); pcb = P.sbuf("pcb", [128, 4, 512], BF16, blk=512)
    den = P.sbuf("den", [128, 4], F32); rden = P.sbuf("rden", [128, 4], F32)
    PP = P.sbuf("PP", [128, 520], F32)
    imp = P.sbuf("imp", [128, 128], F32); score = P.sbuf("score", [128, 128], F32); wk = P.sbuf("wk", [128, 128], F32)
    m8 = P.sbuf("m8", [128, 16], F32)
    nsb = P.sbuf("nsb", [128, 128], BF16); nsT = [P.sbuf(f"nsT{k}", [128, 4, 128], BF16) for k in range(2)]
    pTc = P.sbuf("pTc", [128, NCT, 128], BF16)
    pT = [P.sbuf(f"pT{i}", [128, 512], BF16) for i in range(3)]
    oc = [P.sbuf(f"oc{k}", [128, 4, 64], F32) for k in range(2)]
    oTs = P.sbuf("oTs", [65, 512], F32); oTw = P.sbuf("oTw", [65, 512], F32)
    otok = [P.sbuf(f"otok{i}", [128, 4, 65], F32) for i in range(2)]
    fac = [P.sbuf(f"fac{i}", [128, 4], F32) for i in range(2)]
    at = P.sbuf("at", [128, 4, 64], F32); at2 = P.sbuf("at2", [128, 4, 64], F32)
    atT = P.sbuf("atT", [128, 2, 128], F32)
    pb = [P.psum(f"pb{i}", [128, 1024], BF16) for i in range(2)]
    pf = [P.psum(f"pf{i}", [128, 512], F32) for i in range(6)]

    P.sp.dma_start(out=identf[:], in_=identd.ap())
    P.dve.tensor_copy(out=identb[:], in_=identf[:])
    P.sp.dma_start(out=gm[:], in_=gmix.ap())
    stgw = AP(stg, 0, [[8 * 652, 128], [652, 8], [1, 652]])
    P.sp.dma_start(out=stgw, in_=wa.ap().rearrange("(c p) n -> p c n", p=128))
    for c in range(8):
        P.dve.tensor_scalar(out=WA[:, c, :], in0=AP(stg, c * 652, [[8 * 652, 128], [1, 652]]),
                            scalar1=gm[:, c:c + 1], scalar2=None, op0=ALU.mult)
    for (w1d, w1s, w2d, w2s, pd, ps_, b1) in ((kw1, w1k, kw2, w2k, kpos, posk, b1k), (vw1, w1v, vw2, w2v, vpos, posv, b1v)):
        for q4 in range(2):
            sv = AP(stg, 0, [[8 * 652, 64], [256, 16], [1, 256]])
            P.sp.dma_start(out=sv, in_=w1d[:, q4 * 16:(q4 + 1) * 16, :])
            P.act.activation(out=w1s[:, q4 * 16:(q4 + 1) * 16, :], in_=sv, func=AF.Copy)
        sv = AP(stg, 0, [[8 * 652, 128], [64, 2], [1, 64]])
        P.sp.dma_start(out=sv, in_=w2d.ap())
        P.act.activation(out=w2s[:], in_=sv, func=AF.Copy)
        sv = AP(stg, 0, [[8 * 652, 64], [1, 32]])
        P.sp.dma_start(out=sv, in_=pd.ap())
        P.act.activation(out=ps_[:], in_=sv, func=AF.Copy)
        for half in range(2):
            for tau in range(32):
                P.pe.matmul(out=pf[1][:, half:half + 1], lhsT=w1s[:, tau, half * 128:(half + 1) * 128],
                            rhs=ps_[:, tau:tau + 1], start=(tau == 0), stop=(tau == 31))
        P.dve.tensor_copy(out=b1[:], in_=pf[1][:, 0:2])
    P.sp.dma_start(out=MW[:], in_=mwd.ap())
    P.sp.dma_start(out=CMW[:], in_=cmwd.ap())
    P.sp.dma_start(out=FBW[:], in_=fbwd.ap())
    for c0 in range(0, S, 2048):
        w = min(2048, S - c0)
        sv = AP(stg, 0, [[8 * 652, 128], [1, w]])
        P.sp.dma_start(out=sv, in_=exnd[:, c0:c0 + w])
        P.act.activation(out=EXN[:, c0:c0 + w], in_=sv, func=AF.Copy)
    for (dd, ds) in ((diagd, DIAG), (wlowd, WLOW)):
        sv = AP(stg, 0, [[8 * 652, 128], [1, 512]])
        P.sp.dma_start(out=sv, in_=dd.ap())
        P.act.activation(out=ds[:], in_=sv, func=AF.Copy)
    for t_ in (KcT, VcT, KcR[0], KcR[1], VcR[0], VcR[1]):
        P.pool.memset(ap=t_[:], constant=0.0)
    P.pool.memset(ap=Vc[:], constant=0.0)
    P.pool.memset(ap=pcb[:], constant=0.0)
    P.pool.memset(ap=PP[:], constant=0.0)
    P.pool.memset(ap=Vs[:, :, 64:65], constant=1.0)
    P.pool.memset(ap=Vw[:, :, 64:65], constant=1.0)
    P.sp.dma_start(out=posi[0:NT, :], in_=pos.ap())
    P.dve.tensor_copy(out=posf[0:NT, :], in_=posi[0:NT, :])
    P.pe.transpose(out=pf[1][:, 0:NT], in_=posf[0:NT, :], identity=identf[0:NT, 0:NT])
    P.dve.tensor_copy(out=posT[:], in_=pf[1][:, 0:NT])
    for i in range(8):
        P.dve.tensor_scalar(out=ANG[:, :, i], in0=posT[:], scalar1=INV_FREQ[i], scalar2=None, op0=ALU.mult)
        P.dve.tensor_scalar(out=ANG[:, :, 8 + i], in0=posT[:], scalar1=INV_FREQ[i], scalar2=math.pi / 2, op0=ALU.mult, op1=ALU.add)
    P.dve.tensor_scalar(out=RR[:], in0=ANG[:], scalar1=1.0 / TWO_PI, scalar2=MAGIC, op0=ALU.mult, op1=ALU.add)
    P.dve.tensor_scalar(out=RR[:], in0=RR[:], scalar1=MAGIC, scalar2=-TWO_PI, op0=ALU.subtract, op1=ALU.mult)
    P.dve.tensor_tensor(out=RR[:], in0=RR[:], in1=ANG[:], op=ALU.add)
    P.dve.tensor_scalar(out=RR[:], in0=RR[:], scalar1=3.1415925, scalar2=-3.1415925, op0=ALU.min, op1=ALU.max)
    P.act.activation(out=SC[:], in_=RR[:], func=AF.Sin)

    FSC = NT * 16
    nps = [0]

    def score_bank():
        nps[0] += 1
        return pf[2 + nps[0] % 2]

    npt = [0]

    def pt_buf():
        npt[0] += 1
        return pT[npt[0] % 3]

    def front1(i):
        par = i % 2
        x_ = xs[i % 2]
        P.sp.dma_start(out=x_[:], in_=xa[i * 128:(i + 1) * 128, :])
        P.tag = "rmsnorm_g_folded_in_weig"
        yield
        P.dve.memset(ap=ssq[:], constant=0.0)
        P.act.activation(out=hb[:], in_=x_[:], func=AF.Square, accum_out=ssq[:])
        P.dve.tensor_scalar(out=rstd[:], in0=ssq[:], scalar1=1.0 / 1024.0, scalar2=1e-6, op0=ALU.mult, op1=ALU.add)
        P.act.activation(out=rstd[:], in_=rstd[:], func=AF.Sqrt)
        P.dve.reciprocal(out=rstd[:], in_=rstd[:])
        P.dve.tensor_scalar(out=hb[:], in0=x_[:], scalar1=rstd[:, 0:1], scalar2=None, op0=ALU.mult)
        for c in range(8):
            P.pe.transpose(out=pb[0][:, c * 128:(c + 1) * 128], in_=hb[:, c * 128:(c + 1) * 128], identity=identb[:])
        P.act.activation(out=hT[:], in_=pb[0][:, :], func=AF.Copy)
        P.tag = "projection_tokenmajor"
        yield
        for c in range(8):
            P.pe.matmul(out=pf[0][:, :], lhsT=hT[:, c, :], rhs=WA[:, c, 0:512], start=(c == 0), stop=(c == 7))
        P.dve.tensor_copy(out=pr[:, 0:512], in_=pf[0][:, :])
        for c in range(8):
            P.pe.matmul(out=pf[0][:, 0:140], lhsT=hT[:, c, :], rhs=WA[:, c, 512:652], start=(c == 0), stop=(c == 7))
        P.act.activation(out=pr[:, 512:652], in_=pf[0][:, 0:140], func=AF.Copy)
        P.dve.tensor_copy(out=prb[:], in_=pr[:])
        P.tag = "rope_on_7_heads_q03_kc_k"
        yield
        t1 = AP(pr, 0, [[652, 128], [64, 7], [1, 8]])
        t2 = AP(pr, 8, [[652, 128], [64, 7], [1, 8]])
        sin_b = AP(SC, i * 16, [[FSC, 128], [0, 7], [1, 8]])
        cos_b = AP(SC, i * 16 + 8, [[FSC, 128], [0, 7], [1, 8]])
        P.dve.tensor_tensor(out=rt[0][:], in0=t1, in1=cos_b, op=ALU.mult)
        P.dve.tensor_tensor(out=rt[1][:], in0=t2, in1=sin_b, op=ALU.mult)
        P.dve.tensor_tensor(out=rt[2][:], in0=t2, in1=cos_b, op=ALU.mult)
        P.dve.tensor_tensor(out=rt[3][:], in0=t1, in1=sin_b, op=ALU.mult)
        P.dve.tensor_tensor(out=AP(prb, 0, [[652, 128], [64, 7], [1, 8]]), in0=rt[0][:], in1=rt[1][:], op=ALU.subtract)
        P.dve.tensor_tensor(out=AP(prb, 8, [[652, 128], [64, 7], [1, 8]]), in0=rt[2][:], in1=rt[3][:], op=ALU.add)
        P.tag = "gates_sigmoid_via_exp"
        yield
        P.act.activation(out=gt[i % 3][:], in_=pr[:, 640:652], func=AF.Exp, scale=-1.0)
        P.dve.tensor_scalar(out=gt[i % 3][:], in0=gt[i % 3][:], scalar1=1.0, scalar2=None, op0=ALU.add)
        P.dve.reciprocal(out=gt[i % 3][:], in_=gt[i % 3][:])
        P.tag = "transposes_to_featuremaj"
        yield
        for r in range(4):
            P.pe.transpose(out=pb[0][0:64, r * 128:(r + 1) * 128], in_=prb[:, r * 64:(r + 1) * 64], identity=identb[:])
        for j, c0 in enumerate((256, 320, 384, 448)):
            P.pe.transpose(out=pb[0][0:64, 512 + j * 128:512 + (j + 1) * 128], in_=prb[:, c0:c0 + 64], identity=identb[:])
        P.dve.tensor_copy(out=QT[i % 3][:], in_=pb[0][0:64, 0:512])
        if i > 0:
            P.dve.tensor_copy(out=KcR[par][:, 0:16], in_=KcR[1 - par][:, 128:144])
            P.dve.tensor_copy(out=VcR[par][:, 0:16], in_=VcR[1 - par][:, 128:144])
        P.act.activation(out=KcR[par][:, 16:144], in_=pb[0][0:64, 512:640], func=AF.Copy)
        P.act.activation(out=KsT[:, i * 128:(i + 1) * 128], in_=pb[0][0:64, 640:768], func=AF.Copy)
        P.act.activation(out=KwT[:, i * 128:(i + 1) * 128], in_=pb[0][0:64, 768:896], func=AF.Copy)
        P.act.activation(out=VcR[par][:, 16:144], in_=pb[0][0:64, 896:1024], func=AF.Copy)
        P.dve.tensor_copy(out=Vs[:, i, 0:64], in_=prb[:, 512:576])
        P.dve.tensor_copy(out=Vw[:, i, 0:64], in_=prb[:, 576:640])

    def front2(i):
        par = i % 2
        P.tag = "compression_MLP_for_new"
        yield
        nb = 8 if i > 0 else 7
        base = 0 if i > 0 else 16
        n0 = 8 * i - 1 if i > 0 else 0
        ncn = 8 * i + 7
        for (roll, w1s, w2s, b1, dstT) in ((KcR[par], w1k, w2k, b1k, KcT), (VcR[par], w1v, w2v, b1v, VcT)):
            for half in range(2):
                for tau in range(32):
                    P.pe.matmul(out=pf[1][:, half * 16:half * 16 + nb], lhsT=w1s[:, tau, half * 128:(half + 1) * 128],
                                rhs=AP(roll, base + tau, [[144, 64], [16, nb]]), start=(tau == 0), stop=(tau == 31))
            for half in range(2):
                P.dve.tensor_scalar(out=hid[:, half, 0:nb], in0=pf[1][:, half * 16:half * 16 + nb], scalar1=b1[:, half:half + 1],
                                    scalar2=None, op0=ALU.add)
            P.act.activation(out=hide[:, :, 0:nb], in_=hid[:, :, 0:nb], func=AF.Exp, scale=-1.0)
            P.dve.tensor_scalar(out=hide[:, :, 0:nb], in0=hide[:, :, 0:nb], scalar1=1.0, scalar2=None, op0=ALU.add)
            P.dve.reciprocal(out=hide[:, :, 0:nb], in_=hide[:, :, 0:nb])
            P.dve.tensor_tensor(out=hsb[:, :, 0:nb], in0=hid[:, :, 0:nb], in1=hide[:, :, 0:nb], op=ALU.mult)
            for half in range(2):
                P.pe.matmul(out=pf[1][0:64, 32:32 + nb], lhsT=w2s[:, half, :], rhs=hsb[:, half, 0:nb], start=(half == 0), stop=(half == 1))
            P.dve.tensor_copy(out=dstT[:, n0:n0 + nb], in_=pf[1][0:64, 32:32 + nb])
            yield
        nct = (ncn + 127) // 128
        for ct in range(n0 // 128, nct):
            P.pe.transpose(out=pb[1][:, 0:64], in_=VcT[:, ct * 128:(ct + 1) * 128], identity=identb[0:64, 0:64])
            P.dve.tensor_copy(out=Vc[:, ct, :], in_=pb[1][:, 0:64])
        P.tag = "compressed_branch_q_n_la"
        yield
        for r in range(4):
            sb = pf[1]
            P.pe.matmul(out=sb[:, 0:ncn], lhsT=QT[i % 3][:, r * 128:(r + 1) * 128], rhs=KcT[:, 0:ncn], start=True, stop=True)
            P.act.activation(out=pc[:, r, 0:ncn], in_=sb[:, 0:ncn], func=AF.Exp, scale=0.125)
            P.dve.tensor_tensor(out=pc[:, r, 0:ncn], in0=pc[:, r, 0:ncn], in1=MW[:, 512 - 8 * i:512 - 8 * i + ncn], op=ALU.mult)
            P.dve.reduce_sum(out=den[:, r:r + 1], in_=pc[:, r, 0:ncn], axis=AX.X)
            yield
        P.dve.tensor_scalar(out=rden[:], in0=den[:], scalar1=1e-30, scalar2=None, op0=ALU.max)
        P.dve.reciprocal(out=rden[:], in_=rden[:])
        for r in range(4):
            P.dve.tensor_scalar(out=pc[:, r, 0:ncn], in0=pc[:, r, 0:ncn], scalar1=rden[:, r:r + 1], scalar2=None, op0=ALU.mult)
            P.pool.tensor_copy(out=pcb[:, r, 0:ncn], in_=pc[:, r, 0:ncn])
        P.dve.tensor_tensor(out=PP[:, 1:1 + ncn], in0=pc[:, 0, 0:ncn], in1=pc[:, 1, 0:ncn], op=ALU.add)
        P.dve.tensor_tensor(out=PP[:, 1:1 + ncn], in0=PP[:, 1:1 + ncn], in1=pc[:, 2, 0:ncn], op=ALU.add)
        P.dve.tensor_tensor(out=PP[:, 1:1 + ncn], in0=PP[:, 1:1 + ncn], in1=pc[:, 3, 0:ncn], op=ALU.add)
        for r in range(4):
            for ct in range(nct):
                P.pe.transpose(out=pb[1][:, ct * 128:(ct + 1) * 128], in_=pcb[:, r, ct * 128:(ct + 1) * 128], identity=identb[:])
            P.act.activation(out=pTc[:, 0:nct, :], in_=pb[1][:, 0:nct * 128], func=AF.Copy)
            for ct in range(nct):
                P.pe.matmul(out=pf[1][:, 64 + r * 64:64 + (r + 1) * 64], lhsT=pTc[:, ct, :], rhs=Vc[:, ct, :], start=(ct == 0), stop=(ct == nct - 1))
            yield
        P.dve.tensor_copy(out=oc[par][:], in_=pf[1][:, 64:320])
        P.tag = "importance__top16_select"
        yield
        A = lambda k: AP(PP, k, [[520, 128], [4, 128]])
        P.dve.tensor_tensor(out=imp[:], in0=A(1), in1=A(2), op=ALU.add)
        P.dve.tensor_tensor(out=imp[:], in0=imp[:], in1=A(3), op=ALU.add)
        P.dve.scalar_tensor_tensor(out=imp[:], in0=imp[:], scalar=2.0, in1=A(0), op0=ALU.mult, op1=ALU.add)
        P.dve.tensor_tensor(out=imp[:], in0=imp[:], in1=A(4), op=ALU.add)
        P.dve.tensor_tensor(out=score[:], in0=imp[:], in1=CMW[:, 128 - 2 * i:256 - 2 * i], op=ALU.mult)
        P.dve.tensor_tensor(out=score[:], in0=score[:], in1=FBW[:, 128 - 2 * i:256 - 2 * i], op=ALU.add)
        P.dve.memset(ap=score[:, 0:1], constant=300.0)
        P.dve.max(out=m8[:, 0:8], in_=score[:])
        P.dve.match_replace(out=wk[:], in_to_replace=m8[:, 0:8], in_values=score[:], imm_value=-2.0)
        P.dve.max(out=m8[:, 8:16], in_=wk[:])
        P.dve.tensor_scalar(out=nsb[:], in0=score[:], scalar1=m8[:, 15:16], scalar2=None, op0=ALU.is_lt)
        P.pe.transpose(out=pb[1][:, 0:128], in_=nsb[:], identity=identb[:])
        P.dve.tensor_copy(out=nsT[par][:], in_=AP(pb[1], 0, [[1024, 128], [0, 4], [1, 128]]))

    def loops(i):
        par = i % 2
        nsTf = AP(nsT[par], 0, [[512, 128], [1, 512]])
        P.tag = "selected_dense_masked__s"
        k0 = max(0, i - 4)
        items = [("s", kt) for kt in range(i + 1)] + [("w", kt) for kt in range(k0, i + 1)]

        def emit_score(kind, kt):
            sb = score_bank()
            if kind == "s":
                P.pe.matmul(out=sb[:, :], lhsT=KsT[:, kt * 128:(kt + 1) * 128], rhs=QT[i % 3][:], start=True, stop=False)
                P.pe.matmul(out=sb[:, :], lhsT=EXN[:, kt * 128:(kt + 1) * 128], rhs=nsTf, start=False, stop=(kt != i))
                if kt == i:
                    P.pe.matmul(out=sb[:, :], lhsT=identb[:], rhs=DIAG[:], start=False, stop=True)
            else:
                lowm = (kt == i - 4)
                P.pe.matmul(out=sb[:, :], lhsT=KwT[:, kt * 128:(kt + 1) * 128], rhs=QT[i % 3][:], start=True, stop=not (lowm or kt == i))
                if lowm:
                    P.pe.matmul(out=sb[:, :], lhsT=identb[:], rhs=WLOW[:], start=False, stop=True)
                if kt == i:
                    P.pe.matmul(out=sb[:, :], lhsT=identb[:], rhs=DIAG[:], start=False, stop=True)
            return sb

        def emit_pv(kind, kt, p_):
            if kind == "s":
                P.pe.matmul(out=pf[4][0:65, :], lhsT=Vs[:, kt, :], rhs=p_[:], start=(kt == 0), stop=(kt == i))
            else:
                P.pe.matmul(out=pf[5][0:65, :], lhsT=Vw[:, kt, :], rhs=p_[:], start=(kt == k0), stop=(kt == i))

        prev = None
        for (kind, kt) in items:
            sb = emit_score(kind, kt)
            if prev is not None:
                emit_pv(*prev)
            p_ = pt_buf()
            P.act.activation(out=p_[:], in_=sb[:, :], func=AF.Exp, scale=0.125)
            prev = (kind, kt, p_)
            yield
        emit_pv(*prev)
        yield
        P.tag = "finalize_transpose_back"
        for bi, (pbank, oT_) in enumerate(((pf[4], oTs), (pf[5], oTw))):
            P.act.activation(out=oT_[:], in_=pbank[0:65, :], func=AF.Copy)
            for r in range(4):
                P.pe.transpose(out=pf[2][:, r * 65:(r + 1) * 65], in_=oT_[:, r * 128:(r + 1) * 128], identity=identf[0:65, 0:65])
            P.dve.tensor_copy(out=otok[bi][:], in_=pf[2][:, 0:260])
            P.dve.tensor_scalar(out=fac[bi][:], in0=otok[bi][:, :, 64], scalar1=1e-30, scalar2=None, op0=ALU.max)
            P.dve.reciprocal(out=fac[bi][:], in_=fac[bi][:])
            P.dve.tensor_tensor(out=fac[bi][:], in0=fac[bi][:], in1=gt[i % 3][:, 4 + 4 * bi:8 + 4 * bi], op=ALU.mult)
            yield
        bc = lambda t, off: AP(t, off, [[t.shape[1], 128], [1, 4], [0, 64]])
        P.dve.tensor_tensor(out=at[:], in0=oc[par][:], in1=bc(gt[i % 3], 0), op=ALU.mult)
        P.dve.tensor_tensor(out=at2[:], in0=otok[0][:, :, 0:64], in1=bc(fac[0], 0), op=ALU.mult)
        P.dve.tensor_tensor(out=at[:], in0=at[:], in1=at2[:], op=ALU.add)
        P.dve.tensor_tensor(out=at2[:], in0=otok[1][:, :, 0:64], in1=bc(fac[1], 0), op=ALU.mult)
        P.dve.tensor_tensor(out=at[:], in0=at[:], in1=at2[:], op=ALU.add)
        if scrA is None:
            P.pool.dma_start(out=attn[i * 128:(i + 1) * 128, :], in_=at[:])
        else:
            for h in range(2):
                P.pe.transpose(out=pf[2][:, h * 128:(h + 1) * 128], in_=AP(at, h * 128, [[256, 128], [1, 128]]), identity=identf[:])
            P.dve.tensor_copy(out=atT[:], in_=pf[2][:, 0:256])
            for h in range(2):
                P.pool.dma_start(out=scrA[i // 8][h * 128:(h + 1) * 128, (i % 8) * 128:(i % 8 + 1) * 128], in_=atT[:, h, :])

        yield

    def adv(g, n):
        for _k in range(n):
            try:
                next(g)
            except StopIteration:
                return False
        return True

    for _ in front1(0):
        pass
    for _ in front2(0):
        pass
    if NT > 1:
        for _ in front1(1):
            pass
    for i in range(NT):
        gA = front2(i + 1) if i + 1 < NT else iter(())
        gB = front1(i + 2) if i + 2 < NT else iter(())
        nL = (i + 1) + (i + 1 - max(0, i - 4)) + 4
        perA = max(1, -(-26 // nL))
        perB = max(1, -(-10 // nL))
        for _ in loops(i):
            adv(gA, perA)
            adv(gB, perB)
        for _ in gA:
            pass
        for _ in gB:
            pass


def consts_A(S):
    q = np.arange(128)
    fl = np.floor((q - 31) / 16.0).astype(np.int64)
    m = np.arange(1024)
    mw = (m[None, :] <= 512 + fl[:, None]).astype(np.float32)
    m2 = np.arange(256)
    h = (q >= 64).astype(np.int64)
    cmw = (m2[None, :] < 128 + h[:, None] - 1).astype(np.float32)
    fbw = np.zeros((128, 256), np.float32)
    fbw[(m2[None, :] == 128 + h[:, None])] = 200.0
    fbw[(m2[None, :] == 128 + h[:, None] - 1)] = 100.0
    fbw[m2[None, :] > 128 + h[:, None]] = -1.0
    j = np.arange(128)
    k = np.arange(S)
    exn = np.where((k[None, :] // 64) == j[:, None], NEG, 0.0).astype(np.float32)
    kl = np.arange(128)
    diag = np.where(kl[:, None] > q[None, :], NEG, 0.0).astype(np.float32)
    wlow = np.where(kl[:, None] <= q[None, :], NEG, 0.0).astype(np.float32)
    return dict(identf=np.eye(128, dtype=np.float32), mw=mw, cmw=cmw, fbw=fbw, exn=exn,
                diagc=np.tile(diag, (1, 4)), wlow=np.tile(wlow, (1, 4)))


def inputs_A(inp, b, g, S):
    w_in = inp["w_in"]
    cols = list(range(g * 256, g * 256 + 256))
    base = 1024
    off = lambda j: base + j * 256 + g * 64
    for j in (0, 2, 4, 1, 3, 5):
        cols += list(range(off(j), off(j) + 64))
    gb = 1024 + 6 * 256
    cols += [gb + br * 16 + g * 4 + r for br in range(3) for r in range(4)]
    d = dict(
        xa=np.ascontiguousarray(inp["x"][b, :S]),
        pos=np.ascontiguousarray(inp["positions"][b, :S].reshape(S // 128, 128)).astype(np.int32),
        wa=np.ascontiguousarray(w_in[:, cols]),
        gmix=np.ascontiguousarray(inp["norm_mix_g"].reshape(8, 128).T),
        kw1=np.ascontiguousarray(inp["cmp_k_w1"].reshape(32, 64, 256).transpose(1, 0, 2)),
        kw2=np.ascontiguousarray(inp["cmp_k_w2"].reshape(2, 128, 64).transpose(1, 0, 2)),
        kpos=np.ascontiguousarray(inp["cmp_k_pos"].T),
        vw1=np.ascontiguousarray(inp["cmp_v_w1"].reshape(32, 64, 256).transpose(1, 0, 2)),
        vw2=np.ascontiguousarray(inp["cmp_v_w2"].reshape(2, 128, 64).transpose(1, 0, 2)),
        vpos=np.ascontiguousarray(inp["cmp_v_pos"].T),
    )
    d.update(consts_A(S))
    return d
import numpy as np

TB = 512
HALO = 32


def build_B(T):
    nc = bass.Bass("TRN2", target_bir_lowering=False)
    P = Prog(nc)
    record_B(P, T, None, None)
    P.emit()
    return nc


def record_B(P, T, scrA, S):
    NP = T // TB
    TT = TB + HALO
    D = lambda n, sh, dt=F32, kind="ExternalInput": P.dram(n, sh, dt, kind)
    xTd = D("xT", [1024, HALO + T])
    if scrA is None:
        aTd = D("aT", [1024, T])
    else:
        scrG = [P.dram(f"scrG{j}", [1024, 1024], F32, "Internal", track=True) for j in range(S // 1024)]
        qmd = D("qmask", [128, 4])
    wb = D("wb", [1024, 4096])
    wco = D("wco", [1024, 1024]); wo = D("wo", [1024, 1024])
    wg = D("wg", [1024, 2816]); wu = D("wu", [1024, 2816]); wd = D("wd", [2816, 1024])
    vecd = D("vecs", [128, 8, 8])
    cwd = D("cw", [128, 8, 31])
    idd = D("identB", [128, 128])
    yTd = D("yT", [1024, T], F32, "ExternalOutput")

    vec = P.sbuf("vec", [128, 8, 8], F32)
    cw = P.sbuf("cw_s", [128, 8, 31], F32)
    onesb = P.sbuf("onesb", [128, 128], BF16)
    xT = P.sbuf("xT_s", [128, 8, TT], F32, blk=TT)
    aT = P.sbuf("aT_s", [128, 8, TB], F32, blk=TB)
    sqb = P.sbuf("sqb", [128, 8, TT], BF16, blk=TT)
    rrep = P.sbuf("rrep", [128, TT], F32)
    hT = P.sbuf("hT_s", [128, 8, TT], BF16, blk=TT)
    zb = [P.sbuf(f"zb{i}", [128, TT], BF16) for i in range(2)]
    identf = P.sbuf("identf_b", [128, 128], F32)
    Dg = P.sbuf("Dg", [128, 31, 128], BF16)
    eb = [P.sbuf(f"eb{i}", [128, TT], F32) for i in range(2)]
    zc = P.sbuf("zc", [128, 8, TB], F32, blk=TB)
    mu = P.sbuf("mu", [128, TB], F32)
    lrs = P.sbuf("lrs", [128, TB], F32)
    sT = P.sbuf("sT", [128, 8, TB], BF16, blk=TB)
    t1 = [P.sbuf(f"t1_{i}", [128, TB], F32) for i in range(2)]
    t2 = [P.sbuf(f"t2_{i}", [128, TB], F32) for i in range(2)]
    t3 = [P.sbuf(f"t3_{i}", [128, TB], F32) for i in range(2)]
    mg = P.sbuf("mg", [128, 8, TB], BF16, blk=TB)
    actT = P.sbuf("actT", [128, 22, TB], BF16, blk=TB)
    yb = [P.sbuf(f"yb{i}", [128, TB], F32) for i in range(2)]
    stgs = [P.sbuf(f"stw{i}", [128, 22, 128], F32) for i in range(2)]
    wbufs = [P.sbuf(f"wbf{i}", [128, 22, 128], BF16) for i in range(3)]
    pf = [P.psum(f"pf{i}", [128, 512], F32) for i in range(8)]
    if scrA is not None:
        cand = [P.sbuf("cand0", [128, 8, TB], F32)]
        qm = P.sbuf("qm", [128, 4], F32)
        P.sp.dma_start(out=qm[:], in_=qmd.ap())
        for j in range(S // 1024):
            P.pool.collective_compute("AllGather", ALU.bypass, replica_groups=[[0, 1, 2, 3], [4, 5, 6, 7]],
                                      ins=[scrA[j].ap()], outs=[scrG[j].ap()])

    P.sp.dma_start(out=vec[:], in_=vecd.ap())
    P.sp.dma_start(out=cw[:], in_=cwd.ap())
    P.sp.dma_start(out=identf[:], in_=idd.ap())
    P.pool.memset(ap=onesb[:], constant=1.0)
    GMIX, GFFN, GFIN, CONVB, LNG, LNB, BCO = range(7)
    V = lambda k, c: vec[:, k, c:c + 1]

    cnt = {"w": 0, "pb": 0}

    def load_w(Wd, col0, KC):
        n = cnt["w"]; cnt["w"] += 1
        st = stgs[n % 2]; wbf = wbufs[n % 3]
        P.sp.dma_start(out=st[:, 0:KC, :], in_=Wd[:, col0:col0 + 128].rearrange("(c p) n -> p c n", p=128))
        if n % 4 == 0:
            P.pool.tensor_copy(out=wbf[:, 0:KC, :], in_=st[:, 0:KC, :])
        elif n % 4 == 3:
            P.dve.tensor_copy(out=wbf[:, 0:KC, :], in_=st[:, 0:KC, :])
        else:
            P.act.activation(out=wbf[:, 0:KC, :], in_=st[:, 0:KC, :], func=AF.Copy)
        return wbf

    def bank():
        cnt["pb"] += 1
        return pf[2 + cnt["pb"] % 6]

    def stats(src_fn, N, rep, sq_scale=1.0 / 1024.0, eps=1e-6, bankidx=0):
        segs = [(0, min(N, 512))] + ([(512, N)] if N > 512 else [])
        for c in range(8):
            P.act.activation(out=sqb[:, c, 0:N], in_=src_fn(c), func=AF.Square)
        for si, (a, b_) in enumerate(segs):
            pb_ = pf[bankidx + si]
            for c in range(8):
                P.pe.matmul(out=pb_[:, 0:b_ - a], lhsT=onesb[:], rhs=sqb[:, c, a:b_], start=(c == 0), stop=(c == 7))
            P.dve.tensor_scalar(out=rep[:, a:b_], in0=pb_[:, 0:b_ - a], scalar1=sq_scale, scalar2=eps, op0=ALU.mult, op1=ALU.add)
        P.act.activation(out=rep[:, 0:N], in_=rep[:, 0:N], func=AF.Sqrt)
        P.dve.reciprocal(out=rep[:, 0:N], in_=rep[:, 0:N])

    def sigmoid_from(dst, src, n):
        P.act.activation(out=dst[:, 0:n], in_=src, func=AF.Exp, scale=-1.0)
        P.dve.tensor_scalar(out=dst[:, 0:n], in0=dst[:, 0:n], scalar1=1.0, scalar2=None, op0=ALU.add)
        P.dve.reciprocal(out=dst[:, 0:n], in_=dst[:, 0:n])

    for p in range(NP):
        t0 = p * TB
        P.sp.dma_start(out=xT[:], in_=xTd[:, t0:t0 + TT].rearrange("(c p) n -> p c n", p=128))
        if scrA is None:
            P.sp.dma_start(out=aT[:], in_=aTd[:, t0:t0 + TB].rearrange("(c p) n -> p c n", p=128))
        P.tag = "h__rmsnormx__g_featurema"
        stats(lambda c: xT[:, c, :], TT, rrep)
        for c in range(8):
            P.dve.scalar_tensor_tensor(out=hT[:, c, :], in0=xT[:, c, :], scalar=V(GMIX, c), in1=rrep[:], op0=ALU.mult, op1=ALU.mult)
        P.tag = "conv_branch_GLU__depthwi"
        for c in range(8):
            wa_ = load_w(wb, c * 128, 8)
            wb_ = load_w(wb, 1024 + c * 128, 8)
            z = zb[c % 2]; e = eb[c % 2]
            pa, pa2, pbk, pb2 = bank(), bank(), bank(), bank()
            for (w_, pm, ph) in ((wa_, pa, pa2), (wb_, pbk, pb2)):
                for k in range(8):
                    P.pe.matmul(out=pm[:, :], lhsT=w_[:, k, :], rhs=hT[:, k, HALO:TT], start=(k == 0), stop=(k == 7))
                for k in range(8):
                    P.pe.matmul(out=ph[:, 0:HALO], lhsT=w_[:, k, :], rhs=hT[:, k, 0:HALO], start=(k == 0), stop=(k == 7))
            P.act.activation(out=e[:, HALO:TT], in_=pbk[:, :], func=AF.Exp, scale=-1.0)
            P.act.activation(out=e[:, 0:HALO], in_=pb2[:, 0:HALO], func=AF.Exp, scale=-1.0)
            P.dve.tensor_scalar(out=e[:], in0=e[:], scalar1=1.0, scalar2=None, op0=ALU.add)
            P.dve.reciprocal(out=e[:], in_=e[:])
            P.dve.tensor_tensor(out=z[:, HALO:TT], in0=e[:, HALO:TT], in1=pa[:, :], op=ALU.mult)
            P.dve.tensor_tensor(out=z[:, 0:HALO], in0=e[:, 0:HALO], in1=pa2[:, 0:HALO], op=ALU.mult)
            P.dve.tensor_tensor(out=Dg[:], in0=AP(identf, 0, [[128, 128], [0, 31], [1, 128]]),
                                in1=AP(cw, c * 31, [[8 * 31, 128], [1, 31], [0, 128]]), op=ALU.mult)
            pcv = bank()
            for w in range(31):
                P.pe.matmul(out=pcv[:, :], lhsT=Dg[:, w, :], rhs=z[:, 2 + w:2 + w + TB], start=(w == 0), stop=(w == 30))
            P.dve.tensor_scalar(out=zc[:, c, :], in0=pcv[:, :], scalar1=V(CONVB, c), scalar2=None, op0=ALU.add)
        P.tag = "LayerNorm_over_channels"
        for c in range(8):
            P.act.activation(out=sqb[:, c, 0:TB], in_=zc[:, c, :], func=AF.Copy)
        for c in range(8):
            P.pe.matmul(out=pf[0][:, :], lhsT=onesb[:], rhs=sqb[:, c, 0:TB], start=(c == 0), stop=(c == 7))
        P.dve.tensor_scalar(out=mu[:], in0=pf[0][:, :], scalar1=1.0 / 1024.0, scalar2=None, op0=ALU.mult)
        for c in range(8):
            P.dve.tensor_tensor(out=zc[:, c, :], in0=zc[:, c, :], in1=mu[:], op=ALU.subtract)
        stats(lambda c: zc[:, c, :], TB, lrs)
        for c in range(8):
            a = t1[c % 2]; b_ = t2[c % 2]
            P.dve.tensor_tensor(out=a[:], in0=zc[:, c, :], in1=lrs[:], op=ALU.mult)
            P.dve.tensor_scalar(out=a[:], in0=a[:], scalar1=V(LNG, c), scalar2=V(LNB, c), op0=ALU.mult, op1=ALU.add)
            sigmoid_from(b_, a[:], TB)
            P.pool.tensor_tensor(out=sT[:, c, :], in0=a[:], in1=b_[:], op=ALU.mult)
        if scrA is not None:
            for q in range(4):
                cd = cand[0]
                tg = q * T + t0
                P.sp.dma_start(out=cd[:], in_=scrG[tg // 1024][:, tg % 1024:tg % 1024 + TB].rearrange("(c p) n -> p c n", p=128))
                if q == 0:
                    P.dve.tensor_scalar(out=aT[:], in0=cd[:], scalar1=qm[:, 0:1], scalar2=None, op0=ALU.mult)
                else:
                    P.dve.scalar_tensor_tensor(out=aT[:], in0=cd[:], scalar=qm[:, q:q + 1], in1=aT[:], op0=ALU.mult, op1=ALU.add)
        P.tag = "conv_out_gates_merge"
        for oc in range(8):
            wga_ = load_w(wb, 2048 + oc * 128, 8)
            wgc_ = load_w(wb, 3072 + oc * 128, 8)
            wco_ = load_w(wco, oc * 128, 8)
            pga, pgc, pco = bank(), bank(), bank()
            for k in range(8):
                P.pe.matmul(out=pga[:, :], lhsT=wga_[:, k, :], rhs=hT[:, k, HALO:TT], start=(k == 0), stop=(k == 7))
            for k in range(8):
                P.pe.matmul(out=pgc[:, :], lhsT=wgc_[:, k, :], rhs=hT[:, k, HALO:TT], start=(k == 0), stop=(k == 7))
            for k in range(8):
                P.pe.matmul(out=pco[:, :], lhsT=wco_[:, k, :], rhs=sT[:, k, :], start=(k == 0), stop=(k == 7))
            a = t1[oc % 2]; b_ = t2[oc % 2]; c_ = t3[oc % 2]
            sigmoid_from(a, pga[:, :], TB)
            sigmoid_from(b_, pgc[:, :], TB)
            P.dve.tensor_tensor(out=a[:], in0=a[:], in1=aT[:, oc, :], op=ALU.mult)
            P.dve.scalar_tensor_tensor(out=c_[:], in0=pco[:, :], scalar=V(BCO, oc), in1=b_[:], op0=ALU.add, op1=ALU.mult)
            P.pool.tensor_tensor(out=mg[:, oc, :], in0=a[:], in1=c_[:], op=ALU.add)
        P.tag = "x1__x__merged__w_out_in"
        for oc in range(8):
            wo_ = load_w(wo, oc * 128, 8)
            po = bank()
            for k in range(8):
                P.pe.matmul(out=po[:, :], lhsT=wo_[:, k, :], rhs=mg[:, k, :], start=(k == 0), stop=(k == 7))
            P.dve.tensor_tensor(out=xT[:, oc, HALO:TT], in0=xT[:, oc, HALO:TT], in1=po[:, :], op=ALU.add)
        P.tag = "FFN"
        stats(lambda c: xT[:, c, HALO:TT], TB, rrep)
        for c in range(8):
            P.dve.scalar_tensor_tensor(out=hT[:, c, 0:TB], in0=xT[:, c, HALO:TT], scalar=V(GFFN, c), in1=rrep[:, 0:TB], op0=ALU.mult, op1=ALU.mult)
        for f in range(22):
            wg_ = load_w(wg, f * 128, 8)
            wu_ = load_w(wu, f * 128, 8)
            pg, pu = bank(), bank()
            for k in range(8):
                P.pe.matmul(out=pg[:, :], lhsT=wg_[:, k, :], rhs=hT[:, k, 0:TB], start=(k == 0), stop=(k == 7))
            for k in range(8):
                P.pe.matmul(out=pu[:, :], lhsT=wu_[:, k, :], rhs=hT[:, k, 0:TB], start=(k == 0), stop=(k == 7))
            a = t1[f % 2]; b_ = t2[f % 2]
            sigmoid_from(a, pg[:, :], TB)
            P.dve.tensor_tensor(out=b_[:], in0=a[:], in1=pg[:, :], op=ALU.mult)
            P.dve.tensor_tensor(out=actT[:, f, :], in0=b_[:], in1=pu[:, :], op=ALU.mult)
        for oc in range(8):
            wd_ = load_w(wd, oc * 128, 22)
            pd_ = bank()
            for k in range(22):
                P.pe.matmul(out=pd_[:, :], lhsT=wd_[:, k, :], rhs=actT[:, k, :], start=(k == 0), stop=(k == 21))
            P.dve.tensor_tensor(out=xT[:, oc, HALO:TT], in0=xT[:, oc, HALO:TT], in1=pd_[:, :], op=ALU.add)
        P.tag = "final_norm__store"
        stats(lambda c: xT[:, c, HALO:TT], TB, rrep)
        for c in range(8):
            y = yb[c % 2]
            P.dve.scalar_tensor_tensor(out=y[:], in0=xT[:, c, HALO:TT], scalar=V(GFIN, c), in1=rrep[:, 0:TB], op0=ALU.mult, op1=ALU.mult)
            P.pool.dma_start(out=yTd[c * 128:(c + 1) * 128, t0:t0 + TB], in_=y[:])


def inputs_B(inp, attn_b, b, t0, T):
    x = inp["x"][b]
    xh = np.zeros((HALO + T, 1024), np.float32)
    lo = t0 - HALO
    if lo >= 0:
        xh[:] = x[lo:t0 + T]
    else:
        xh[-lo:] = x[0:t0 + T]
    w_in = inp["w_in"]
    lay = lambda v: np.ascontiguousarray(v.reshape(8, 128).T)
    vecs = np.zeros((128, 8, 8), np.float32)
    for k, name in enumerate(["norm_mix_g", "norm_ffn_g", "norm_final_g", "conv_b", "conv_norm_g", "conv_norm_b", "b_conv_out"]):
        vecs[:, k, :] = lay(inp[name])
    cw = np.ascontiguousarray(inp["conv_w"][:, 0, :].reshape(31, 8, 128).transpose(2, 1, 0))
    return dict(
        xT=np.ascontiguousarray(xh.T), **({} if attn_b is None else {"aT": np.ascontiguousarray(attn_b[t0:t0 + T].T)}),
        wb=np.ascontiguousarray(w_in[:, 2608:6704]), wco=inp["w_conv_out"], wo=inp["w_out"],
        wg=inp["w_ffn_gate"], wu=inp["w_ffn_up"], wd=inp["w_ffn_down"], vecs=vecs, cw=cw, identB=np.eye(128, dtype=np.float32))
from concourse.bass_utils import run_bass_kernel_spmd

S_FULL = 8192
T_CORE = 2048


def build_fused(S, T):
    nc = bass.Bass("TRN2", target_bir_lowering=False)
    scrA = [nc.dram_tensor(f"scrA{j}", [256, 1024], F32) for j in range(S // 1024)]
    P1 = Prog(nc, "A_")
    record_A(P1, S, scrA)
    P1.emit()
    P2 = Prog(nc, "B_")
    record_B(P2, T, scrA, S)
    P2.emit()
    return nc


def fused_inputs(inp, c, S, T):
    b, r = c // 4, c % 4
    d = inputs_A(inp, b, r, S)
    db = inputs_B(inp, None, b, r * T, T)
    d.update(db)
    qm = np.zeros((128, 4), np.float32)
    qm[:, r] = 1.0
    d["qmask"] = qm
    return d


def kernel(**inputs):
    inp = {k: np.asarray(v) for k, v in inputs.items()}
    B = inp["x"].shape[0]
    nc = build_fused(S_FULL, T_CORE)
    maps = [fused_inputs(inp, c, S_FULL, T_CORE) for c in range(8)]
    res = run_bass_kernel_spmd(nc, maps, core_ids=list(range(8)))
    out = np.zeros((B, S_FULL, 1024), np.float32)
    for c in range(8):
        out[c // 4, (c % 4) * T_CORE:(c % 4 + 1) * T_CORE, :] = res.results[c]["yT"].T
    return out
```

```python
import numpy as np
from contextlib import ExitStack
import concourse.bass as bass
import concourse.mybir as mybir

F32 = mybir.dt.float32
BF16 = mybir.dt.bfloat16
I32 = mybir.dt.int32
AF = mybir.ActivationFunctionType
ALU = mybir.AluOpType
AX = mybir.AxisListType

ENGS = ("pe", "act", "dve", "pool", "sp")
WRITE_KW = ("out", "accum_out", "out_max", "out_indices", "ap")
DT_SIZE = {F32: 4, BF16: 2, I32: 4}


def _dtsize(dt):
    try:
        return DT_SIZE[dt]
    except Exception:
        return mybir.dt.size(dt)


class Op:
    __slots__ = ("eng", "fn", "deps", "is_dma", "needed", "sig", "gidx", "pre_wait", "cc", "tag")

    def __init__(self, eng, fn, is_dma, gidx):
        self.eng = eng
        self.fn = fn
        self.deps = []
        self.is_dma = is_dma
        self.needed = False
        self.sig = None
        self.gidx = gidx
        self.pre_wait = None
        self.cc = False
        self.tag = None


class _EngProxy:
    def __init__(self, prog, eng):
        self._p = prog
        self._e = eng

    def __getattr__(self, name):
        p, e = self._p, self._e

        def call(*args, **kw):
            extra_r = kw.pop("_r", ())
            extra_w = kw.pop("_w", ())
            return p._record(e, name, args, kw, extra_r, extra_w)

        return call


class Prog:
    RING = 12

    def __init__(self, nc, prefix="", sem_stack=None):
        self.nc = nc
        self.prefix = prefix
        self.tag = None
        self.sem_stack = sem_stack
        self.nosync_same = ("pe",)
        self.ops = {e: [] for e in ENGS}
        self.n = 0
        self.blk = {}
        self.untracked = set()
        self.last_w = {}
        self.readers = {}
        self.pe = _EngProxy(self, "pe")
        self.act = _EngProxy(self, "act")
        self.dve = _EngProxy(self, "dve")
        self.pool = _EngProxy(self, "pool")
        self.sp = _EngProxy(self, "sp")
        self.stack = ExitStack()
        self.dma_count = {e: 0 for e in ENGS}

    def sbuf(self, name, shape, dtype, blk=None):
        name = self.prefix + name
        t = self.stack.enter_context(self.nc.sbuf_tensor(name, list(shape), dtype))
        self.blk[name] = None if blk is None else blk * _dtsize(dtype)
        return t

    def psum(self, name, shape, dtype, blk=None):
        name = self.prefix + name
        t = self.stack.enter_context(self.nc.psum_tensor(name, list(shape), dtype))
        self.blk[name] = None if blk is None else blk * _dtsize(dtype)
        return t

    def dram(self, name, shape, dtype, kind, track=False, blk=None):
        if kind == "Internal":
            t = self.nc.dram_tensor(name, list(shape), dtype)
        else:
            t = self.nc.dram_tensor(name, list(shape), dtype, kind=kind)
        if not track:
            self.untracked.add(name)
        else:
            self.blk[name] = None if blk is None else blk * _dtsize(dtype)
        return t

    def keys(self, ap):
        t = ap.tensor
        name = t.name
        if name in self.untracked:
            return []
        b = self.blk.get(name, None)
        if b is None:
            return [name]
        es = _dtsize(ap.dtype)
        space = str(type(t).__name__)
        pairs = list(ap.ap)
        if "DRam" in space:
            lo = ap.offset
            hi = lo + sum((c - 1) * abs(s) for s, c in pairs)
        else:
            shp = list(t.shape)
            F = 1
            for d in shp[1:]:
                F *= d
            F = F * _dtsize(t.dtype) // es
            lo = ap.offset % F
            hi = lo + sum((c - 1) * abs(s) for s, c in pairs[1:])
        lo_b, hi_b = lo * es, hi * es + es - 1
        return [(name, i) for i in range(lo_b // b, hi_b // b + 1)]

    def _record(self, eng, name, args, kw, extra_r, extra_w):
        is_dma = name in ("dma_start", "collective_compute", "dma_start_transpose")
        rk, wk = [], []
        if name == "collective_compute":
            for a in kw.get("ins", []):
                rk += self.keys(a)
            for a in kw.get("outs", []):
                wk += self.keys(a)
        else:
            for k, v in kw.items():
                if isinstance(v, bass.AP):
                    if "PSum" in type(v.tensor).__name__:
                        wk.extend(self.keys(v))
                    else:
                        (wk if k in WRITE_KW else rk).extend(self.keys(v))
            for a in args:
                if isinstance(a, bass.AP):
                    raise ValueError("pass APs as kwargs")
        for a in extra_r:
            rk += self.keys(a) if isinstance(a, bass.AP) else [a]
        for a in extra_w:
            wk += self.keys(a) if isinstance(a, bass.AP) else [a]

        def fn(engobj, name=name, args=args, kw=kw):
            return getattr(engobj, name)(*args, **kw)

        op = self.raw(eng, fn, rk, wk, is_dma)
        op.cc = (name == "collective_compute")
        return op

    def raw(self, eng, fn, rk, wk, is_dma=False):
        op = Op(eng, fn, is_dma, self.n)
        op.tag = self.tag
        self.n += 1
        deps = {}
        for k in rk:
            lw = self.last_w.get(k)
            if lw is not None:
                deps[lw.gidx] = lw
        for k in wk:
            lw = self.last_w.get(k)
            if lw is not None:
                deps[lw.gidx] = lw
            rd = self.readers.get(k)
            if rd:
                for e, o in rd.items():
                    if e == "dma":
                        for oo in o:
                            deps[oo.gidx] = oo
                    else:
                        deps[o.gidx] = o
        for k in wk:
            self.last_w[k] = op
            self.readers[k] = {}
        for k in rk:
            if k in wk:
                continue
            rd = self.readers.setdefault(k, {})
            if is_dma:
                rd.setdefault("dma", []).append(op)
            else:
                rd[eng] = op
        for d in deps.values():
            if d.eng == eng and eng in self.nosync_same and not d.is_dma and not is_dma:
                continue
            op.deps.append(d)
            d.needed = True
        if is_dma:
            op.needed = True
        self.ops[eng].append(op)
        return op

    def emit(self, final_wait_engs=("sp", "pool", "act")):
        nc = self.nc
        allsems = []

        def newsem(name):
            h = nc.alloc_semaphore(name=name)
            allsems.append(h)
            return h

        esem = {e: newsem(self.prefix + "es_" + e) for e in ENGS}
        ccsem = newsem(self.prefix + "ccsem")
        ncc = 0
        rings = {}
        for e in ENGS:
            if any(o.is_dma for o in self.ops[e]):
                rings[e] = [newsem(f"{self.prefix}dr_{e}_{i}") for i in range(self.RING)]
        for e in ENGS:
            c = 0
            nd = 0
            for o in self.ops[e]:
                if o.cc:
                    ncc += 1
                    o.sig = (ccsem, ncc)
                elif o.is_dma:
                    sem = rings[e][nd % self.RING]
                    o.sig = (sem, 16 * (nd // self.RING + 1))
                    if nd >= self.RING:
                        o.pre_wait = (sem, 16 * (nd // self.RING))
                    nd += 1
                elif o.needed:
                    c += 1
                    o.sig = (esem[e], c)
            self.dma_count[e] = nd
        prog = self

        def run_engine(e, engobj):
            waited = {}

            def wait(sem, val):
                key = id(sem)
                if waited.get(key, 0) >= val:
                    return
                engobj.wait_ge(sem, val)
                waited[key] = val

            semobj = {}
            for o in prog.ops[e]:
                for d in o.deps:
                    sem, val = d.sig
                    semobj[id(sem)] = sem
                    wait(sem, val)
                if o.pre_wait is not None:
                    wait(*o.pre_wait)
                ins = o.fn(engobj)
                if o.tag is not None:
                    ins.annotate(o.tag)
                if o.sig is not None:
                    sem, val = o.sig
                    if o.cc:
                        ins.then_inc(sem, 1)
                    elif o.is_dma:
                        ins.then_inc(sem, 16)
                    else:
                        ins.then_inc(sem, 1)
            if e in rings:
                nd = prog.dma_count[e]
                for i, sem in enumerate(rings[e]):
                    cnt = (nd - i + prog.RING - 1) // prog.RING if nd > i else 0
                    if cnt > 0:
                        wait(sem, 16 * cnt)

        with nc.Block() as block:
            if self.ops["sp"]:
                block.sync(lambda eng: run_engine("sp", eng))
            if self.ops["act"]:
                block.scalar(lambda eng: run_engine("act", eng))
            if self.ops["dve"]:
                block.vector(lambda eng: run_engine("dve", eng))
            if self.ops["pool"]:
                block.gpsimd(lambda eng: run_engine("pool", eng))
            if self.ops["pe"]:
                block.tensor(lambda eng: run_engine("pe", eng))
        nc.clear_and_free_semaphores(allsems)
        nc.all_engine_barrier()
        self.stack.close()
import math
import numpy as np

NEG = -240000.0
INV_FREQ = [float(np.float32(500000.0) ** np.float32(-(2 * i) / 16.0)) for i in range(8)]
MAGIC = 12582912.0
TWO_PI = 2.0 * math.pi


def AP(t, off, pairs):
    return bass.AP(t, off, [list(p) for p in pairs])


def build_A(S):
    nc = bass.Bass("TRN2", target_bir_lowering=False)
    P = Prog(nc)
    record_A(P, S, None)
    P.emit()
    return nc


def record_A(P, S, scrA):
    NT = S // 128
    NCMAX = S // 16 - 1
    NCT = (NCMAX + 127) // 128
    D = lambda n, sh, dt=F32, kind="ExternalInput": P.dram(n, sh, dt, kind)
    xa = D("xa", [S, 1024])
    pos = D("pos", [NT, 128], I32)
    wa = D("wa", [1024, 652])
    gmix = D("gmix", [128, 8])
    kw1 = D("kw1", [64, 32, 256]); kw2 = D("kw2", [128, 2, 64]); kpos = D("kpos", [64, 32])
    vw1 = D("vw1", [64, 32, 256]); vw2 = D("vw2", [128, 2, 64]); vpos = D("vpos", [64, 32])
    identd = D("identf", [128, 128])
    mwd = D("mw", [128, 1024]); cmwd = D("cmw", [128, 256]); fbwd = D("fbw", [128, 256])
    exnd = D("exn", [128, S]); diagd = D("diagc", [128, 512]); wlowd = D("wlow", [128, 512])
    attn = D("attn", [S, 256], F32, "ExternalOutput") if scrA is None else None

    identf = P.sbuf("identf_s", [128, 128], F32)
    identb = P.sbuf("identb", [128, 128], BF16)
    stg = P.sbuf("stg", [128, 8 * 652], F32)
    WA = P.sbuf("WA", [128, 8, 652], BF16)
    gm = P.sbuf("gm", [128, 8], F32)
    w1k = P.sbuf("w1k", [128, 32, 256], BF16); w1v = P.sbuf("w1v", [128, 32, 256], BF16)
    w2k = P.sbuf("w2k", [128, 2, 64], BF16); w2v = P.sbuf("w2v", [128, 2, 64], BF16)
    posk = P.sbuf("posk", [128, 32], BF16); posv = P.sbuf("posv", [128, 32], BF16)
    b1k = P.sbuf("b1k", [128, 2], F32); b1v = P.sbuf("b1v", [128, 2], F32)
    MW = P.sbuf("MW", [128, 1024], F32); CMW = P.sbuf("CMW", [128, 256], F32); FBW = P.sbuf("FBW", [128, 256], F32)
    DIAG = P.sbuf("DIAG", [128, 512], BF16); WLOW = P.sbuf("WLOW", [128, 512], BF16)
    posi = P.sbuf("posi", [64, 128], I32); posf = P.sbuf("posf", [64, 128], F32)
    posT = P.sbuf("posT", [128, NT], F32)
    ANG = P.sbuf("ANG", [128, NT, 16], F32); RR = P.sbuf("RR", [128, NT, 16], F32); SC = P.sbuf("SC", [128, NT, 16], F32)
    KE = P.sbuf("KE", [128, S], BF16, blk=128); KwT = P.sbuf("KwT", [128, S], BF16, blk=128)
    Vs = P.sbuf("Vs", [128, NT, 65], BF16, blk=65); Vw = P.sbuf("Vw", [128, NT, 65], BF16, blk=65)
    KcR = [P.sbuf(f"KcR{k}", [128, 144], BF16) for k in range(2)]; VcR = [P.sbuf(f"VcR{k}", [128, 144], BF16) for k in range(2)]
    KcT = P.sbuf("KcT", [128, NCT * 128], BF16); VcT = P.sbuf("VcT", [128, NCT * 128], BF16)
    Vc = P.sbuf("Vc", [128, NCT, 64], BF16)
    xs = [P.sbuf(f"xs{i}", [128, 1024], F32) for i in range(2)]
    ssq = P.sbuf("ssq", [128, 1], F32); rstd = P.sbuf("rstd", [128, 1], F32)
    hb = P.sbuf("hb", [128, 1024], BF16)
    hT = P.sbuf("hT", [128, 8, 128], BF16)
    pr = P.sbuf("pr", [128, 652], F32); prb = P.sbuf("prb", [128, 652], BF16)
    rt = [P.sbuf(f"rt{i}", [128, 7, 8], F32) for i in range(4)]
    gt = [P.sbuf(f"gt{k}", [128, 12], F32) for k in range(3)]
    QQ = [P.sbuf(f"QQ{k}", [128, 512], BF16) for k in range(3)]
    hid = P.sbuf("hid", [128, 2, 8], F32); hide = P.sbuf("hide", [128, 2, 8], F32); hsb = P.sbuf("hsb", [128, 2, 8], BF16)
    pc = P.sbuf("pc", [128, 4, 512], F32, blk=512); pcb = P.sbuf("pcb", [128, 4, 512], BF16, blk=512)
    den = P.sbuf("den", [128, 4], F32); rden = P.sbuf("rden", [128, 4], F32)
    PP = P.sbuf("PP", [128, 520], F32)
    imp = P.sbuf("imp", [128, 128], F32); score = P.sbuf("score", [128, 128], F32); wk = P.sbuf("wk", [128, 128], F32)
    m8 = P.sbuf("m8", [128, 16], F32)
    nsb = P.sbuf("nsb", [128, 128], BF16); RA = [P.sbuf(f"RA{k}", [128, 512], BF16) for k in range(2)]; RB = [P.sbuf(f"RB{k}", [128, 512], BF16) for k in range(2)]
    pTc = P.sbuf("pTc", [128, NCT, 128], BF16)
    pT = [P.sbuf(f"pT{i}", [128, 512], BF16) for i in range(3)]
    oc = [P.sbuf(f"oc{k}", [128, 4, 64], F32) for k in range(2)]
    oTs = P.sbuf("oTs", [65, 512], F32); oTw = P.sbuf("oTw", [65, 512], F32)
    otok = [P.sbuf(f"otok{i}", [128, 4, 65], F32) for i in range(2)]
    fac = [P.sbuf(f"fac{i}", [128, 4], F32) for i in range(2)]
    at = P.sbuf("at", [128, 4, 64], F32); at2 = P.sbuf("at2", [128, 4, 64], F32)
    atT = P.sbuf("atT", [128, 2, 128], F32)
    pb = [P.psum(f"pb{i}", [128, 1024], BF16) for i in range(2)]
    pf = [P.psum(f"pf{i}", [128, 512], F32) for i in range(6)]

    P.sp.dma_start(out=identf[:], in_=identd.ap())
    P.dve.tensor_copy(out=identb[:], in_=identf[:])
    P.sp.dma_start(out=gm[:], in_=gmix.ap())
    stgw = AP(stg, 0, [[8 * 652, 128], [652, 8], [1, 652]])
    P.sp.dma_start(out=stgw, in_=wa.ap().rearrange("(c p) n -> p c n", p=128))
    for c in range(8):
        P.dve.tensor_scalar(out=WA[:, c, :], in0=AP(stg, c * 652, [[8 * 652, 128], [1, 652]]),
                            scalar1=gm[:, c:c + 1], scalar2=None, op0=ALU.mult)
    for t_ in (w1k, w1v, posk, posv, KwT):
        P.pool.memset(ap=t_[:], constant=0.0)
    for (w1d, w1s, w2d, w2s, pd, ps_, b1) in ((kw1, w1k, kw2, w2k, kpos, posk, b1k), (vw1, w1v, vw2, w2v, vpos, posv, b1v)):
        for q4 in range(2):
            sv = AP(stg, 0, [[8 * 652, 64], [256, 16], [1, 256]])
            P.sp.dma_start(out=sv, in_=w1d[:, q4 * 16:(q4 + 1) * 16, :])
            P.act.activation(out=w1s[0:64, q4 * 16:(q4 + 1) * 16, :], in_=sv, func=AF.Copy)
        sv = AP(stg, 0, [[8 * 652, 128], [64, 2], [1, 64]])
        P.sp.dma_start(out=sv, in_=w2d.ap())
        P.act.activation(out=w2s[:], in_=sv, func=AF.Copy)
        sv = AP(stg, 0, [[8 * 652, 64], [1, 32]])
        P.sp.dma_start(out=sv, in_=pd.ap())
        P.act.activation(out=ps_[0:64, :], in_=sv, func=AF.Copy)
        for half in range(2):
            for tau in range(32):
                P.pe.matmul(out=pf[1][:, half:half + 1], lhsT=w1s[:, tau, half * 128:(half + 1) * 128],
                            rhs=ps_[:, tau:tau + 1], start=(tau == 0), stop=(tau == 31))
        P.dve.tensor_copy(out=b1[:], in_=pf[1][:, 0:2])
    P.sp.dma_start(out=MW[:], in_=mwd.ap())
    P.sp.dma_start(out=CMW[:], in_=cmwd.ap())
    P.sp.dma_start(out=FBW[:], in_=fbwd.ap())
    for c0 in range(0, S, 2048):
        w = min(2048, S - c0)
        sv = AP(stg, 0, [[8 * 652, 128], [1, w]])
        P.sp.dma_start(out=sv, in_=exnd[:, c0:c0 + w])
        if c0 < 4096:
            P.act.activation(out=KE[0:64, c0:c0 + w], in_=AP(stg, 0, [[8 * 652, 64], [1, w]]), func=AF.Copy)
        else:
            P.act.activation(out=KE[64:128, c0:c0 + w], in_=AP(stg, 64 * 8 * 652, [[8 * 652, 64], [1, w]]), func=AF.Copy)
    for (dd, ds) in ((diagd, DIAG), (wlowd, WLOW)):
        sv = AP(stg, 0, [[8 * 652, 128], [1, 512]])
        P.sp.dma_start(out=sv, in_=dd.ap())
        P.act.activation(out=ds[:], in_=sv, func=AF.Copy)
    for t_ in (KcT, VcT, KcR[0], KcR[1], VcR[0], VcR[1]):
        P.pool.memset(ap=t_[:], constant=0.0)
    P.pool.memset(ap=Vc[:], constant=0.0)
    P.pool.memset(ap=pcb[:], constant=0.0)
    P.pool.memset(ap=PP[:], constant=0.0)
    P.pool.memset(ap=Vs[:, :, 64:65], constant=1.0)
    P.pool.memset(ap=Vw[:, :, 64:65], constant=1.0)
    P.sp.dma_start(out=posi[0:NT, :], in_=pos.ap())
    P.dve.tensor_copy(out=posf[0:NT, :], in_=posi[0:NT, :])
    P.pe.transpose(out=pf[1][:, 0:NT], in_=posf[0:NT, :], identity=identf[0:NT, 0:NT])
    P.dve.tensor_copy(out=posT[:], in_=pf[1][:, 0:NT])
    for i in range(8):
        P.dve.tensor_scalar(out=ANG[:, :, i], in0=posT[:], scalar1=INV_FREQ[i], scalar2=None, op0=ALU.mult)
        P.dve.tensor_scalar(out=ANG[:, :, 8 + i], in0=posT[:], scalar1=INV_FREQ[i], scalar2=math.pi / 2, op0=ALU.mult, op1=ALU.add)
    P.dve.tensor_scalar(out=RR[:], in0=ANG[:], scalar1=1.0 / TWO_PI, scalar2=MAGIC, op0=ALU.mult, op1=ALU.add)
    P.dve.tensor_scalar(out=RR[:], in0=RR[:], scalar1=MAGIC, scalar2=-TWO_PI, op0=ALU.subtract, op1=ALU.mult)
    P.dve.tensor_tensor(out=RR[:], in0=RR[:], in1=ANG[:], op=ALU.add)
    P.dve.tensor_scalar(out=RR[:], in0=RR[:], scalar1=3.1415925, scalar2=-3.1415925, op0=ALU.min, op1=ALU.max)
    P.act.activation(out=SC[:], in_=RR[:], func=AF.Sin)

    FSC = NT * 16
    nps = [0]

    def score_bank():
        nps[0] += 1
        return pf[2 + nps[0] % 2]

    npt = [0]

    def pt_buf():
        npt[0] += 1
        return pT[npt[0] % 3]

    def front1(i):
        par = i % 2
        x_ = xs[i % 2]
        P.sp.dma_start(out=x_[:], in_=xa[i * 128:(i + 1) * 128, :])
        P.tag = "rmsnorm_g_folded_in_weig"
        yield
        P.dve.memset(ap=ssq[:], constant=0.0)
        P.act.activation(out=hb[:], in_=x_[:], func=AF.Square, accum_out=ssq[:])
        P.dve.tensor_scalar(out=rstd[:], in0=ssq[:], scalar1=1.0 / 1024.0, scalar2=1e-6, op0=ALU.mult, op1=ALU.add)
        P.act.activation(out=rstd[:], in_=rstd[:], func=AF.Sqrt)
        P.dve.reciprocal(out=rstd[:], in_=rstd[:])
        P.dve.tensor_scalar(out=hb[:], in0=x_[:], scalar1=rstd[:, 0:1], scalar2=None, op0=ALU.mult)
        for c in range(8):
            P.pe.transpose(out=pb[0][:, c * 128:(c + 1) * 128], in_=hb[:, c * 128:(c + 1) * 128], identity=identb[:])
        P.act.activation(out=hT[:], in_=pb[0][:, :], func=AF.Copy)
        P.tag = "projection_tokenmajor"
        yield
        for c in range(8):
            P.pe.matmul(out=pf[0][:, :], lhsT=hT[:, c, :], rhs=WA[:, c, 0:512], start=(c == 0), stop=(c == 7))
        P.dve.tensor_copy(out=pr[:, 0:512], in_=pf[0][:, :])
        for c in range(8):
            P.pe.matmul(out=pf[0][:, 0:140], lhsT=hT[:, c, :], rhs=WA[:, c, 512:652], start=(c == 0), stop=(c == 7))
        P.act.activation(out=pr[:, 512:652], in_=pf[0][:, 0:140], func=AF.Copy)
        P.dve.tensor_copy(out=prb[:], in_=pr[:])
        P.tag = "rope_on_7_heads_q03_kc_k"
        yield
        t1 = AP(pr, 0, [[652, 128], [64, 7], [1, 8]])
        t2 = AP(pr, 8, [[652, 128], [64, 7], [1, 8]])
        sin_b = AP(SC, i * 16, [[FSC, 128], [0, 7], [1, 8]])
        cos_b = AP(SC, i * 16 + 8, [[FSC, 128], [0, 7], [1, 8]])
        P.dve.tensor_tensor(out=rt[0][:], in0=t1, in1=cos_b, op=ALU.mult)
        P.dve.tensor_tensor(out=rt[1][:], in0=t2, in1=sin_b, op=ALU.mult)
        P.dve.tensor_tensor(out=rt[2][:], in0=t2, in1=cos_b, op=ALU.mult)
        P.dve.tensor_tensor(out=rt[3][:], in0=t1, in1=sin_b, op=ALU.mult)
        P.dve.tensor_tensor(out=AP(prb, 0, [[652, 128], [64, 7], [1, 8]]), in0=rt[0][:], in1=rt[1][:], op=ALU.subtract)
        P.dve.tensor_tensor(out=AP(prb, 8, [[652, 128], [64, 7], [1, 8]]), in0=rt[2][:], in1=rt[3][:], op=ALU.add)
        P.tag = "gates_sigmoid_via_exp"
        yield
        P.act.activation(out=gt[i % 3][:], in_=pr[:, 640:652], func=AF.Exp, scale=-1.0)
        P.dve.tensor_scalar(out=gt[i % 3][:], in0=gt[i % 3][:], scalar1=1.0, scalar2=None, op0=ALU.add)
        P.dve.reciprocal(out=gt[i % 3][:], in_=gt[i % 3][:])
        P.tag = "transposes_to_featuremaj"
        yield
        for r in range(4):
            P.pe.transpose(out=pb[0][0:64, r * 128:(r + 1) * 128], in_=prb[:, r * 64:(r + 1) * 64], identity=identb[:])
            P.pe.transpose(out=pb[0][64:128, r * 128:(r + 1) * 128], in_=prb[:, r * 64:(r + 1) * 64], identity=identb[:])
        for j, c0 in enumerate((256, 320, 384, 448)):
            ks_hi = (j == 1 and i < 32)
            P.pe.transpose(out=pb[0][64:128, 512 + j * 128:512 + (j + 1) * 128] if ks_hi else pb[0][0:64, 512 + j * 128:512 + (j + 1) * 128],
                           in_=prb[:, c0:c0 + 64], identity=identb[:])
        P.dve.tensor_copy(out=QQ[i % 3][:], in_=pb[0][:, 0:512])
        if i > 0:
            P.dve.tensor_copy(out=KcR[par][0:64, 0:16], in_=KcR[1 - par][0:64, 128:144])
            P.dve.tensor_copy(out=VcR[par][0:64, 0:16], in_=VcR[1 - par][0:64, 128:144])
        P.act.activation(out=KcR[par][0:64, 16:144], in_=pb[0][0:64, 512:640], func=AF.Copy)
        if i < 32:
            P.act.activation(out=KE[64:128, i * 128:(i + 1) * 128], in_=pb[0][64:128, 640:768], func=AF.Copy)
        else:
            P.act.activation(out=KE[0:64, i * 128:(i + 1) * 128], in_=pb[0][0:64, 640:768], func=AF.Copy)
        P.act.activation(out=KwT[0:64, i * 128:(i + 1) * 128], in_=pb[0][0:64, 768:896], func=AF.Copy)
        P.act.activation(out=VcR[par][0:64, 16:144], in_=pb[0][0:64, 896:1024], func=AF.Copy)
        P.dve.tensor_copy(out=Vs[:, i, 0:64], in_=prb[:, 512:576])
        P.dve.tensor_copy(out=Vw[:, i, 0:64], in_=prb[:, 576:640])

    def front2(i):
        par = i % 2
        P.tag = "compression_MLP_for_new"
        yield
        nb = 8 if i > 0 else 7
        base = 0 if i > 0 else 16
        n0 = 8 * i - 1 if i > 0 else 0
        ncn = 8 * i + 7
        for (roll, w1s, w2s, b1, dstT) in ((KcR[par], w1k, w2k, b1k, KcT), (VcR[par], w1v, w2v, b1v, VcT)):
            for half in range(2):
                for tau in range(32):
                    P.pe.matmul(out=pf[1][:, half * 16:half * 16 + nb], lhsT=w1s[:, tau, half * 128:(half + 1) * 128],
                                rhs=AP(roll, base + tau, [[144, 128], [16, nb]]), start=(tau == 0), stop=(tau == 31))
            for half in range(2):
                P.dve.tensor_scalar(out=hid[:, half, 0:nb], in0=pf[1][:, half * 16:half * 16 + nb], scalar1=b1[:, half:half + 1],
                                    scalar2=None, op0=ALU.add)
            P.act.activation(out=hide[:, :, 0:nb], in_=hid[:, :, 0:nb], func=AF.Exp, scale=-1.0)
            P.dve.tensor_scalar(out=hide[:, :, 0:nb], in0=hide[:, :, 0:nb], scalar1=1.0, scalar2=None, op0=ALU.add)
            P.dve.reciprocal(out=hide[:, :, 0:nb], in_=hide[:, :, 0:nb])
            P.dve.tensor_tensor(out=hsb[:, :, 0:nb], in0=hid[:, :, 0:nb], in1=hide[:, :, 0:nb], op=ALU.mult)
            for half in range(2):
                P.pe.matmul(out=pf[1][0:64, 32:32 + nb], lhsT=w2s[:, half, :], rhs=hsb[:, half, 0:nb], start=(half == 0), stop=(half == 1))
            P.dve.tensor_copy(out=dstT[0:64, n0:n0 + nb], in_=pf[1][0:64, 32:32 + nb])
            yield
        nct = (ncn + 127) // 128
        for ct in range(n0 // 128, nct):
            P.pe.transpose(out=pb[1][:, 0:128], in_=VcT[:, ct * 128:(ct + 1) * 128], identity=identb[:])
            P.dve.tensor_copy(out=Vc[:, ct, :], in_=pb[1][:, 0:64])
        P.tag = "compressed_branch_q_n_la"
        yield
        for r in range(4):
            sb = pf[1]
            P.pe.matmul(out=sb[:, 0:ncn], lhsT=QQ[i % 3][:, r * 128:(r + 1) * 128], rhs=KcT[:, 0:ncn], start=True, stop=True)
            P.act.activation(out=pc[:, r, 0:ncn], in_=sb[:, 0:ncn], func=AF.Exp, scale=0.125)
            P.dve.tensor_tensor(out=pc[:, r, 0:ncn], in0=pc[:, r, 0:ncn], in1=MW[:, 512 - 8 * i:512 - 8 * i + ncn], op=ALU.mult)
            P.dve.reduce_sum(out=den[:, r:r + 1], in_=pc[:, r, 0:ncn], axis=AX.X)
            yield
        P.dve.tensor_scalar(out=rden[:], in0=den[:], scalar1=1e-30, scalar2=None, op0=ALU.max)
        P.dve.reciprocal(out=rden[:], in_=rden[:])
        for r in range(4):
            P.dve.tensor_scalar(out=pc[:, r, 0:ncn], in0=pc[:, r, 0:ncn], scalar1=rden[:, r:r + 1], scalar2=None, op0=ALU.mult)
            P.pool.tensor_copy(out=pcb[:, r, 0:ncn], in_=pc[:, r, 0:ncn])
        P.dve.tensor_tensor(out=PP[:, 1:1 + ncn], in0=pc[:, 0, 0:ncn], in1=pc[:, 1, 0:ncn], op=ALU.add)
        P.dve.tensor_tensor(out=PP[:, 1:1 + ncn], in0=PP[:, 1:1 + ncn], in1=pc[:, 2, 0:ncn], op=ALU.add)
        P.dve.tensor_tensor(out=PP[:, 1:1 + ncn], in0=PP[:, 1:1 + ncn], in1=pc[:, 3, 0:ncn], op=ALU.add)
        for r in range(4):
            for ct in range(nct):
                P.pe.transpose(out=pb[1][:, ct * 128:(ct + 1) * 128], in_=pcb[:, r, ct * 128:(ct + 1) * 128], identity=identb[:])
            P.act.activation(out=pTc[:, 0:nct, :], in_=pb[1][:, 0:nct * 128], func=AF.Copy)
            for ct in range(nct):
                P.pe.matmul(out=pf[1][:, 64 + r * 64:64 + (r + 1) * 64], lhsT=pTc[:, ct, :], rhs=Vc[:, ct, :], start=(ct == 0), stop=(ct == nct - 1))
            yield
        P.dve.tensor_copy(out=oc[par][:], in_=pf[1][:, 64:320])
        P.tag = "importance__top16_select"
        yield
        A = lambda k: AP(PP, k, [[520, 128], [4, 128]])
        P.dve.tensor_tensor(out=imp[:], in0=A(1), in1=A(2), op=ALU.add)
        P.dve.tensor_tensor(out=imp[:], in0=imp[:], in1=A(3), op=ALU.add)
        P.dve.scalar_tensor_tensor(out=imp[:], in0=imp[:], scalar=2.0, in1=A(0), op0=ALU.mult, op1=ALU.add)
        P.dve.tensor_tensor(out=imp[:], in0=imp[:], in1=A(4), op=ALU.add)
        P.dve.tensor_tensor(out=score[:], in0=imp[:], in1=CMW[:, 128 - 2 * i:256 - 2 * i], op=ALU.mult)
        P.dve.tensor_tensor(out=score[:], in0=score[:], in1=FBW[:, 128 - 2 * i:256 - 2 * i], op=ALU.add)
        P.dve.memset(ap=score[:, 0:1], constant=300.0)
        P.dve.max(out=m8[:, 0:8], in_=score[:])
        P.dve.match_replace(out=wk[:], in_to_replace=m8[:, 0:8], in_values=score[:], imm_value=-2.0)
        P.dve.max(out=m8[:, 8:16], in_=wk[:])
        P.dve.tensor_scalar(out=nsb[:], in0=score[:], scalar1=m8[:, 15:16], scalar2=None, op0=ALU.is_lt)
        P.pe.transpose(out=pb[1][:, 0:128], in_=nsb[:], identity=identb[:])
        P.dve.tensor_copy(out=AP(RA[par], 0, [[512, 64], [128, 4], [1, 128]]), in_=AP(pb[1], 0, [[1024, 64], [0, 4], [1, 128]]))
        P.dve.tensor_copy(out=AP(RB[par], 64 * 512, [[512, 64], [128, 4], [1, 128]]), in_=AP(pb[1], 64 * 1024, [[1024, 64], [0, 4], [1, 128]]))
        P.pool.tensor_copy(out=RA[par][64:128, :], in_=QQ[i % 3][64:128, :])
        P.pool.tensor_copy(out=RB[par][0:64, :], in_=QQ[i % 3][0:64, :])

    def loops(i):
        par = i % 2
        P.tag = "selected_dense_masked__s"
        k0 = max(0, i - 4)
        items = [("s", kt) for kt in range(i + 1)] + [("w", kt) for kt in range(k0, i + 1)]

        def emit_score(kind, kt):
            sb = score_bank()
            if kind == "s":
                P.pe.matmul(out=sb[:, :], lhsT=KE[:, kt * 128:(kt + 1) * 128], rhs=(RA[par][:] if kt < 32 else RB[par][:]), start=True, stop=(kt != i))
                if kt == i:
                    P.pe.matmul(out=sb[:, :], lhsT=identb[:], rhs=DIAG[:], start=False, stop=True)
            else:
                lowm = (kt == i - 4)
                P.pe.matmul(out=sb[:, :], lhsT=KwT[:, kt * 128:(kt + 1) * 128], rhs=RB[par][:], start=True, stop=not (lowm or kt == i))
                if lowm:
                    P.pe.matmul(out=sb[:, :], lhsT=identb[:], rhs=WLOW[:], start=False, stop=True)
                if kt == i:
                    P.pe.matmul(out=sb[:, :], lhsT=identb[:], rhs=DIAG[:], start=False, stop=True)
            return sb

        def emit_pv(kind, kt, p_):
            if kind == "s":
                P.pe.matmul(out=pf[4][0:65, :], lhsT=Vs[:, kt, :], rhs=p_[:], start=(kt == 0), stop=(kt == i))
            else:
                P.pe.matmul(out=pf[5][0:65, :], lhsT=Vw[:, kt, :], rhs=p_[:], start=(kt == k0), stop=(kt == i))

        prev = None
        for (kind, kt) in items:
            sb = emit_score(kind, kt)
            if prev is not None:
                emit_pv(*prev)
            p_ = pt_buf()
            P.act.activation(out=p_[:], in_=sb[:, :], func=AF.Exp, scale=0.125)
            prev = (kind, kt, p_)
            yield
        emit_pv(*prev)
        yield
        P.tag = "finalize_transpose_back"
        for bi, (pbank, oT_) in enumerate(((pf[4], oTs), (pf[5], oTw))):
            P.act.activation(out=oT_[:], in_=pbank[0:65, :], func=AF.Copy)
            for r in range(4):
                P.pe.transpose(out=pf[2][:, r * 65:(r + 1) * 65], in_=oT_[:, r * 128:(r + 1) * 128], identity=identf[0:65, 0:65])
            P.dve.tensor_copy(out=otok[bi][:], in_=pf[2][:, 0:260])
            P.dve.tensor_scalar(out=fac[bi][:], in0=otok[bi][:, :, 64], scalar1=1e-30, scalar2=None, op0=ALU.max)
            P.dve.reciprocal(out=fac[bi][:], in_=fac[bi][:])
            P.dve.tensor_tensor(out=fac[bi][:], in0=fac[bi][:], in1=gt[i % 3][:, 4 + 4 * bi:8 + 4 * bi], op=ALU.mult)
            yield
        bc = lambda t, off: AP(t, off, [[t.shape[1], 128], [1, 4], [0, 64]])
        P.dve.tensor_tensor(out=at[:], in0=oc[par][:], in1=bc(gt[i % 3], 0), op=ALU.mult)
        P.dve.tensor_tensor(out=at2[:], in0=otok[0][:, :, 0:64], in1=bc(fac[0], 0), op=ALU.mult)
        P.dve.tensor_tensor(out=at[:], in0=at[:], in1=at2[:], op=ALU.add)
        P.dve.tensor_tensor(out=at2[:], in0=otok[1][:, :, 0:64], in1=bc(fac[1], 0), op=ALU.mult)
        P.dve.tensor_tensor(out=at[:], in0=at[:], in1=at2[:], op=ALU.add)
        if scrA is None:
            P.pool.dma_start(out=attn[i * 128:(i + 1) * 128, :], in_=at[:])
        else:
            for h in range(2):
                P.pe.transpose(out=pf[2][:, h * 128:(h + 1) * 128], in_=AP(at, h * 128, [[256, 128], [1, 128]]), identity=identf[:])
            P.dve.tensor_copy(out=atT[:], in_=pf[2][:, 0:256])
            for h in range(2):
                P.pool.dma_start(out=scrA[i // 8][h * 128:(h + 1) * 128, (i % 8) * 128:(i % 8 + 1) * 128], in_=atT[:, h, :])

        yield

    def adv(g, n):
        for _k in range(n):
            try:
                next(g)
            except StopIteration:
                return False
        return True

    for _ in front1(0):
        pass
    for _ in front2(0):
        pass
    if NT > 1:
        for _ in front1(1):
            pass
    for i in range(NT):
        gA = front2(i + 1) if i + 1 < NT else iter(())
        gB = front1(i + 2) if i + 2 < NT else iter(())
        nL = (i + 1) + (i + 1 - max(0, i - 4)) + 4
        perA = max(1, -(-26 // nL))
        perB = max(1, -(-10 // nL))
        for _ in loops(i):
            adv(gA, perA)
            adv(gB, perB)
        for _ in gA:
            pass
        for _ in gB:
            pass


def consts_A(S):
    q = np.arange(128)
    fl = np.floor((q - 31) / 16.0).astype(np.int64)
    m = np.arange(1024)
    mw = (m[None, :] <= 512 + fl[:, None]).astype(np.float32)
    m2 = np.arange(256)
    h = (q >= 64).astype(np.int64)
    cmw = (m2[None, :] < 128 + h[:, None] - 1).astype(np.float32)
    fbw = np.zeros((128, 256), np.float32)
    fbw[(m2[None, :] == 128 + h[:, None])] = 200.0
    fbw[(m2[None, :] == 128 + h[:, None] - 1)] = 100.0
    fbw[m2[None, :] > 128 + h[:, None]] = -1.0
    j = np.arange(128)
    k = np.arange(S)
    exn = np.where((k[None, :] // 64) == j[:, None], NEG, 0.0).astype(np.float32)
    kl = np.arange(128)
    diag = np.where(kl[:, None] > q[None, :], NEG, 0.0).astype(np.float32)
    wlow = np.where(kl[:, None] <= q[None, :], NEG, 0.0).astype(np.float32)
    return dict(identf=np.eye(128, dtype=np.float32), mw=mw, cmw=cmw, fbw=fbw, exn=exn,
                diagc=np.tile(diag, (1, 4)), wlow=np.tile(wlow, (1, 4)))


def inputs_A(inp, b, g, S):
    w_in = inp["w_in"]
    cols = list(range(g * 256, g * 256 + 256))
    base = 1024
    off = lambda j: base + j * 256 + g * 64
    for j in (0, 2, 4, 1, 3, 5):
        cols += list(range(off(j), off(j) + 64))
    gb = 1024 + 6 * 256
    cols += [gb + br * 16 + g * 4 + r for br in range(3) for r in range(4)]
    d = dict(
        xa=np.ascontiguousarray(inp["x"][b, :S]),
        pos=np.ascontiguousarray(inp["positions"][b, :S].reshape(S // 128, 128)).astype(np.int32),
        wa=np.ascontiguousarray(w_in[:, cols]),
        gmix=np.ascontiguousarray(inp["norm_mix_g"].reshape(8, 128).T),
        kw1=np.ascontiguousarray(inp["cmp_k_w1"].reshape(32, 64, 256).transpose(1, 0, 2)),
        kw2=np.ascontiguousarray(inp["cmp_k_w2"].reshape(2, 128, 64).transpose(1, 0, 2)),
        kpos=np.ascontiguousarray(inp["cmp_k_pos"].T),
        vw1=np.ascontiguousarray(inp["cmp_v_w1"].reshape(32, 64, 256).transpose(1, 0, 2)),
        vw2=np.ascontiguousarray(inp["cmp_v_w2"].reshape(2, 128, 64).transpose(1, 0, 2)),
        vpos=np.ascontiguousarray(inp["cmp_v_pos"].T),
    )
    d.update(consts_A(S))
    return d
import numpy as np

TB = 512
HALO = 32


def build_B(T):
    nc = bass.Bass("TRN2", target_bir_lowering=False)
    P = Prog(nc)
    record_B(P, T, None, None)
    P.emit()
    return nc


def record_B(P, T, scrA, S):
    NP = T // TB
    TT = TB + HALO
    D = lambda n, sh, dt=F32, kind="ExternalInput": P.dram(n, sh, dt, kind)
    xTd = D("xT", [1024, HALO + T])
    if scrA is None:
        aTd = D("aT", [1024, T])
    else:
        scrG = [P.dram(f"scrG{j}", [1024, 1024], F32, "Internal", track=True) for j in range(S // 1024)]
        qmd = D("qmask", [128, 4])
    wb = D("wb", [1024, 4096])
    wco = D("wco", [1024, 1024]); wo = D("wo", [1024, 1024])
    wg = D("wg", [1024, 2816]); wu = D("wu", [1024, 2816]); wd = D("wd", [2816, 1024])
    vecd = D("vecs", [128, 8, 8])
    cwd = D("cw", [128, 8, 31])
    idd = D("identB", [128, 128])
    yTd = D("yT", [1024, T], F32, "ExternalOutput")

    vec = P.sbuf("vec", [128, 8, 8], F32)
    cw = P.sbuf("cw_s", [128, 8, 31], F32)
    onesb = P.sbuf("onesb", [128, 128], BF16)
    xT = P.sbuf("xT_s", [128, 8, TT], F32, blk=TT)
    aT = P.sbuf("aT_s", [128, 8, TB], F32, blk=TB)
    sqb = P.sbuf("sqb", [128, 8, TT], BF16, blk=TT)
    rrep = P.sbuf("rrep", [128, TT], F32)
    hT = P.sbuf("hT_s", [128, 8, TT], BF16, blk=TT)
    zb = [P.sbuf(f"zb{i}", [128, TT], BF16) for i in range(2)]
    identf = P.sbuf("identf_b", [128, 128], F32)
    Dg = P.sbuf("Dg", [128, 31, 128], BF16)
    eb = [P.sbuf(f"eb{i}", [128, TT], F32) for i in range(2)]
    zc = P.sbuf("zc", [128, 8, TB], F32, blk=TB)
    mu = P.sbuf("mu", [128, TB], F32)
    lrs = P.sbuf("lrs", [128, TB], F32)
    sT = P.sbuf("sT", [128, 8, TB], BF16, blk=TB)
    t1 = [P.sbuf(f"t1_{i}", [128, TB], F32) for i in range(2)]
    t2 = [P.sbuf(f"t2_{i}", [128, TB], F32) for i in range(2)]
    t3 = [P.sbuf(f"t3_{i}", [128, TB], F32) for i in range(2)]
    mg = P.sbuf("mg", [128, 8, TB], BF16, blk=TB)
    actT = P.sbuf("actT", [128, 22, TB], BF16, blk=TB)
    yb = [P.sbuf(f"yb{i}", [128, TB], F32) for i in range(2)]
    stgs = [P.sbuf(f"stw{i}", [128, 22, 128], F32) for i in range(2)]
    wbufs = [P.sbuf(f"wbf{i}", [128, 22, 128], BF16) for i in range(3)]
    pf = [P.psum(f"pf{i}", [128, 512], F32) for i in range(8)]
    if scrA is not None:
        cand = [P.sbuf("cand0", [128, 8, TB], F32)]
        qm = P.sbuf("qm", [128, 4], F32)
        P.sp.dma_start(out=qm[:], in_=qmd.ap())
        for j in range(S // 1024):
            P.pool.collective_compute("AllGather", ALU.bypass, replica_groups=[[0, 1, 2, 3], [4, 5, 6, 7]],
                                      ins=[scrA[j].ap()], outs=[scrG[j].ap()])

    P.sp.dma_start(out=vec[:], in_=vecd.ap())
    P.sp.dma_start(out=cw[:], in_=cwd.ap())
    P.sp.dma_start(out=identf[:], in_=idd.ap())
    P.pool.memset(ap=onesb[:], constant=1.0)
    GMIX, GFFN, GFIN, CONVB, LNG, LNB, BCO = range(7)
    V = lambda k, c: vec[:, k, c:c + 1]

    cnt = {"w": 0, "pb": 0}

    def load_w(Wd, col0, KC):
        n = cnt["w"]; cnt["w"] += 1
        st = stgs[n % 2]; wbf = wbufs[n % 3]
        P.sp.dma_start(out=st[:, 0:KC, :], in_=Wd[:, col0:col0 + 128].rearrange("(c p) n -> p c n", p=128))
        if n % 4 == 0:
            P.pool.tensor_copy(out=wbf[:, 0:KC, :], in_=st[:, 0:KC, :])
        elif n % 4 == 3:
            P.dve.tensor_copy(out=wbf[:, 0:KC, :], in_=st[:, 0:KC, :])
        else:
            P.act.activation(out=wbf[:, 0:KC, :], in_=st[:, 0:KC, :], func=AF.Copy)
        return wbf

    def bank():
        cnt["pb"] += 1
        return pf[2 + cnt["pb"] % 6]

    def stats(src_fn, N, rep, sq_scale=1.0 / 1024.0, eps=1e-6, bankidx=0):
        segs = [(0, min(N, 512))] + ([(512, N)] if N > 512 else [])
        for c in range(8):
            P.act.activation(out=sqb[:, c, 0:N], in_=src_fn(c), func=AF.Square)
        for si, (a, b_) in enumerate(segs):
            pb_ = pf[bankidx + si]
            for c in range(8):
                P.pe.matmul(out=pb_[:, 0:b_ - a], lhsT=onesb[:], rhs=sqb[:, c, a:b_], start=(c == 0), stop=(c == 7))
            P.dve.tensor_scalar(out=rep[:, a:b_], in0=pb_[:, 0:b_ - a], scalar1=sq_scale, scalar2=eps, op0=ALU.mult, op1=ALU.add)
        P.act.activation(out=rep[:, 0:N], in_=rep[:, 0:N], func=AF.Sqrt)
        P.dve.reciprocal(out=rep[:, 0:N], in_=rep[:, 0:N])

    def sigmoid_from(dst, src, n):
        P.act.activation(out=dst[:, 0:n], in_=src, func=AF.Exp, scale=-1.0)
        P.dve.tensor_scalar(out=dst[:, 0:n], in0=dst[:, 0:n], scalar1=1.0, scalar2=None, op0=ALU.add)
        P.dve.reciprocal(out=dst[:, 0:n], in_=dst[:, 0:n])

    for p in range(NP):
        t0 = p * TB
        P.sp.dma_start(out=xT[:], in_=xTd[:, t0:t0 + TT].rearrange("(c p) n -> p c n", p=128))
        if scrA is None:
            P.sp.dma_start(out=aT[:], in_=aTd[:, t0:t0 + TB].rearrange("(c p) n -> p c n", p=128))
        P.tag = "h__rmsnormx__g_featurema"
        stats(lambda c: xT[:, c, :], TT, rrep)
        for c in range(8):
            P.dve.scalar_tensor_tensor(out=hT[:, c, :], in0=xT[:, c, :], scalar=V(GMIX, c), in1=rrep[:], op0=ALU.mult, op1=ALU.mult)
        P.tag = "conv_branch_GLU__depthwi"
        for c in range(8):
            wa_ = load_w(wb, c * 128, 8)
            wb_ = load_w(wb, 1024 + c * 128, 8)
            z = zb[c % 2]; e = eb[c % 2]
            pa, pa2, pbk, pb2 = bank(), bank(), bank(), bank()
            for (w_, pm, ph) in ((wa_, pa, pa2), (wb_, pbk, pb2)):
                for k in range(8):
                    P.pe.matmul(out=pm[:, :], lhsT=w_[:, k, :], rhs=hT[:, k, HALO:TT], start=(k == 0), stop=(k == 7))
                for k in range(8):
                    P.pe.matmul(out=ph[:, 0:HALO], lhsT=w_[:, k, :], rhs=hT[:, k, 0:HALO], start=(k == 0), stop=(k == 7))
            P.act.activation(out=e[:, HALO:TT], in_=pbk[:, :], func=AF.Sigmoid)
            P.act.activation(out=e[:, 0:HALO], in_=pb2[:, 0:HALO], func=AF.Sigmoid)
            P.dve.tensor_tensor(out=z[:, HALO:TT], in0=e[:, HALO:TT], in1=pa[:, :], op=ALU.mult)
            P.dve.tensor_tensor(out=z[:, 0:HALO], in0=e[:, 0:HALO], in1=pa2[:, 0:HALO], op=ALU.mult)
            P.dve.tensor_tensor(out=Dg[:], in0=AP(identf, 0, [[128, 128], [0, 31], [1, 128]]),
                                in1=AP(cw, c * 31, [[8 * 31, 128], [1, 31], [0, 128]]), op=ALU.mult)
            pcv = bank()
            for w in range(31):
                P.pe.matmul(out=pcv[:, :], lhsT=Dg[:, w, :], rhs=z[:, 2 + w:2 + w + TB], start=(w == 0), stop=(w == 30))
            P.dve.tensor_scalar(out=zc[:, c, :], in0=pcv[:, :], scalar1=V(CONVB, c), scalar2=None, op0=ALU.add)
        P.tag = "LayerNorm_over_channels"
        for c in range(8):
            P.act.activation(out=sqb[:, c, 0:TB], in_=zc[:, c, :], func=AF.Copy)
        for c in range(8):
            P.pe.matmul(out=pf[0][:, :], lhsT=onesb[:], rhs=sqb[:, c, 0:TB], start=(c == 0), stop=(c == 7))
        P.dve.tensor_scalar(out=mu[:], in0=pf[0][:, :], scalar1=1.0 / 1024.0, scalar2=None, op0=ALU.mult)
        for c in range(8):
            P.dve.tensor_tensor(out=zc[:, c, :], in0=zc[:, c, :], in1=mu[:], op=ALU.subtract)
        stats(lambda c: zc[:, c, :], TB, lrs)
        for c in range(8):
            a = t1[c % 2]; b_ = t2[c % 2]
            P.dve.tensor_tensor(out=a[:], in0=zc[:, c, :], in1=lrs[:], op=ALU.mult)
            P.act.activation(out=sT[:, c, :], in_=a[:], func=AF.Silu, scale=V(LNG, c), bias=V(LNB, c))
        if scrA is not None:
            for q in range(4):
                cd = cand[0]
                tg = q * T + t0
                P.sp.dma_start(out=cd[:], in_=scrG[tg // 1024][:, tg % 1024:tg % 1024 + TB].rearrange("(c p) n -> p c n", p=128))
                if q == 0:
                    P.dve.tensor_scalar(out=aT[:], in0=cd[:], scalar1=qm[:, 0:1], scalar2=None, op0=ALU.mult)
                else:
                    P.dve.scalar_tensor_tensor(out=aT[:], in0=cd[:], scalar=qm[:, q:q + 1], in1=aT[:], op0=ALU.mult, op1=ALU.add)
        P.tag = "conv_out_gates_merge"
        for oc in range(8):
            wga_ = load_w(wb, 2048 + oc * 128, 8)
            wgc_ = load_w(wb, 3072 + oc * 128, 8)
            wco_ = load_w(wco, oc * 128, 8)
            pga, pgc, pco = bank(), bank(), bank()
            for k in range(8):
                P.pe.matmul(out=pga[:, :], lhsT=wga_[:, k, :], rhs=hT[:, k, HALO:TT], start=(k == 0), stop=(k == 7))
            for k in range(8):
                P.pe.matmul(out=pgc[:, :], lhsT=wgc_[:, k, :], rhs=hT[:, k, HALO:TT], start=(k == 0), stop=(k == 7))
            for k in range(8):
                P.pe.matmul(out=pco[:, :], lhsT=wco_[:, k, :], rhs=sT[:, k, :], start=(k == 0), stop=(k == 7))
            a = t1[oc % 2]; b_ = t2[oc % 2]; c_ = t3[oc % 2]
            P.act.activation(out=a[:], in_=pga[:, :], func=AF.Sigmoid)
            P.act.activation(out=b_[:], in_=pgc[:, :], func=AF.Sigmoid)
            P.dve.tensor_tensor(out=a[:], in0=a[:], in1=aT[:, oc, :], op=ALU.mult)
            P.dve.scalar_tensor_tensor(out=c_[:], in0=pco[:, :], scalar=V(BCO, oc), in1=b_[:], op0=ALU.add, op1=ALU.mult)
            P.pool.tensor_tensor(out=mg[:, oc, :], in0=a[:], in1=c_[:], op=ALU.add)
        P.tag = "x1__x__merged__w_out_in"
        for oc in range(8):
            wo_ = load_w(wo, oc * 128, 8)
            po = bank()
            for k in range(8):
                P.pe.matmul(out=po[:, :], lhsT=wo_[:, k, :], rhs=mg[:, k, :], start=(k == 0), stop=(k == 7))
            P.dve.tensor_tensor(out=xT[:, oc, HALO:TT], in0=xT[:, oc, HALO:TT], in1=po[:, :], op=ALU.add)
        P.tag = "FFN"
        stats(lambda c: xT[:, c, HALO:TT], TB, rrep)
        for c in range(8):
            P.dve.scalar_tensor_tensor(out=hT[:, c, 0:TB], in0=xT[:, c, HALO:TT], scalar=V(GFFN, c), in1=rrep[:, 0:TB], op0=ALU.mult, op1=ALU.mult)
        for f in range(22):
            wg_ = load_w(wg, f * 128, 8)
            wu_ = load_w(wu, f * 128, 8)
            pg, pu = bank(), bank()
            for k in range(8):
                P.pe.matmul(out=pg[:, :], lhsT=wg_[:, k, :], rhs=hT[:, k, 0:TB], start=(k == 0), stop=(k == 7))
            for k in range(8):
                P.pe.matmul(out=pu[:, :], lhsT=wu_[:, k, :], rhs=hT[:, k, 0:TB], start=(k == 0), stop=(k == 7))
            a = t1[f % 2]; b_ = t2[f % 2]
            P.act.activation(out=b_[:], in_=pg[:, :], func=AF.Silu)
            P.dve.tensor_tensor(out=actT[:, f, :], in0=b_[:], in1=pu[:, :], op=ALU.mult)
        for oc in range(8):
            wd_ = load_w(wd, oc * 128, 22)
            pd_ = bank()
            for k in range(22):
                P.pe.matmul(out=pd_[:, :], lhsT=wd_[:, k, :], rhs=actT[:, k, :], start=(k == 0), stop=(k == 21))
            P.dve.tensor_tensor(out=xT[:, oc, HALO:TT], in0=xT[:, oc, HALO:TT], in1=pd_[:, :], op=ALU.add)
        P.tag = "final_norm__store"
        stats(lambda c: xT[:, c, HALO:TT], TB, rrep)
        for c in range(8):
            y = yb[c % 2]
            P.dve.scalar_tensor_tensor(out=y[:], in0=xT[:, c, HALO:TT], scalar=V(GFIN, c), in1=rrep[:, 0:TB], op0=ALU.mult, op1=ALU.mult)
            P.pool.dma_start(out=yTd[c * 128:(c + 1) * 128, t0:t0 + TB], in_=y[:])


def inputs_B(inp, attn_b, b, t0, T):
    x = inp["x"][b]
    xh = np.zeros((HALO + T, 1024), np.float32)
    lo = t0 - HALO
    if lo >= 0:
        xh[:] = x[lo:t0 + T]
    else:
        xh[-lo:] = x[0:t0 + T]
    w_in = inp["w_in"]
    lay = lambda v: np.ascontiguousarray(v.reshape(8, 128).T)
    vecs = np.zeros((128, 8, 8), np.float32)
    for k, name in enumerate(["norm_mix_g", "norm_ffn_g", "norm_final_g", "conv_b", "conv_norm_g", "conv_norm_b", "b_conv_out"]):
        vecs[:, k, :] = lay(inp[name])
    cw = np.ascontiguousarray(inp["conv_w"][:, 0, :].reshape(31, 8, 128).transpose(2, 1, 0))
    return dict(
        xT=np.ascontiguousarray(xh.T), **({} if attn_b is None else {"aT": np.ascontiguousarray(attn_b[t0:t0 + T].T)}),
        wb=np.ascontiguousarray(w_in[:, 2608:6704]), wco=inp["w_conv_out"], wo=inp["w_out"],
        wg=inp["w_ffn_gate"], wu=inp["w_ffn_up"], wd=inp["w_ffn_down"], vecs=vecs, cw=cw, identB=np.eye(128, dtype=np.float32))
from concourse.bass_utils import run_bass_kernel_spmd

S_FULL = 8192
T_CORE = 2048


def build_fused(S, T):
    nc = bass.Bass("TRN2", target_bir_lowering=False)
    scrA = [nc.dram_tensor(f"scrA{j}", [256, 1024], F32) for j in range(S // 1024)]
    P1 = Prog(nc, "A_")
    record_A(P1, S, scrA)
    P1.emit()
    P2 = Prog(nc, "B_")
    record_B(P2, T, scrA, S)
    P2.emit()
    return nc


def fused_inputs(inp, c, S, T):
    b, r = c // 4, c % 4
    d = inputs_A(inp, b, r, S)
    db = inputs_B(inp, None, b, r * T, T)
    d.update(db)
    qm = np.zeros((128, 4), np.float32)
    qm[:, r] = 1.0
    d["qmask"] = qm
    return d


def kernel(**inputs):
    inp = {k: np.asarray(v) for k, v in inputs.items()}
    B = inp["x"].shape[0]
    nc = build_fused(S_FULL, T_CORE)
    maps = [fused_inputs(inp, c, S_FULL, T_CORE) for c in range(8)]
    res = run_bass_kernel_spmd(nc, maps, core_ids=list(range(8)))
    out = np.zeros((B, S_FULL, 1024), np.float32)
    for c in range(8):
        out[c // 4, (c % 4) * T_CORE:(c % 4 + 1) * T_CORE, :] = res.results[c]["yT"].T
    return out
```
